# Optimizing a Trainium2 kernel written in Bass

```python
import jax, jax.numpy as jnp
from jax import lax
import numpy as np

D_MODEL = 1024
BATCH = 4
SEQ = 4096
DEPTH = 4
DEC_BATCH = 8
DEC_SEQ = 64
PAST_LEN = 1024

CHUNK = 64
N_META = 16
N_MIXERS = 4
NORM_EPS = 1e-6
LN_EPS = 1e-5
CONV_W = 31
D_CONV = D_MODEL
D_RNN = 1280
LRU_BLOCKS = 10
LRU_BLOCK = D_RNN // LRU_BLOCKS
LRU_CONV_W = 4
LRU_C = 8.0
RWKV_HEAD = 64
RWKV_HEADS = D_MODEL // RWKV_HEAD
D_DECAY_LORA = 64
D_A_LORA = 64
RWKV_GN_EPS = 64e-5
RET_HEADS = 4
RET_DK = D_MODEL // RET_HEADS
RET_DV = 2 * RET_DK
D_RET_V = RET_HEADS * RET_DV
ROPE_BASE = 10000.0

kernel_name = 'hybrid_streaming_encoder_step'


def rmsnorm(x, g):
    x32 = x.astype(jnp.float32)
    y = x32 * lax.rsqrt(jnp.mean(x32 * x32, axis=-1, keepdims=True) + NORM_EPS)
    return (y * g.astype(jnp.float32)).astype(x.dtype)


def layernorm(x, g, b):
    x32 = x.astype(jnp.float32)
    mu = jnp.mean(x32, axis=-1, keepdims=True)
    var = jnp.mean(jnp.square(x32 - mu), axis=-1, keepdims=True)
    y = (x32 - mu) * lax.rsqrt(var + LN_EPS)
    return (y * g.astype(jnp.float32) + b.astype(jnp.float32)).astype(x.dtype)


def causal_dwconv(x, buf, w, b):
    width = w.shape[0]
    xp = jnp.concatenate([buf.astype(x.dtype), x], axis=1)
    y = lax.conv_general_dilated(xp, w[:, None, :].astype(x.dtype), window_strides=(1,), padding='VALID',
                                 dimension_numbers=('NWC', 'WIO', 'NWC'), feature_group_count=x.shape[-1])
    return y + b.astype(x.dtype), xp[:, xp.shape[1] - (width - 1):]


def conformer_conv_mixer(h, conv_buf, p):
    a, b, gate = jnp.split(h @ p['w_in'], 3, axis=-1)
    glu = a * jax.nn.sigmoid(b)
    c, new_buf = causal_dwconv(glu, conv_buf, p['w_dw'], p['b_dw'])
    c = layernorm(c, p['ln_g'], p['ln_b'])
    y = jax.nn.silu(c) * jax.nn.silu(gate)
    return y @ p['w_out'], new_buf


def _affine_combine(e, l):
    return (e[0] * l[0], l[0] * e[1] + l[1])


def rglru_mixer(h, conv_buf, h0, p):
    B, T, _ = h.shape
    xb, gate = jnp.split(h @ p['w_in'], 2, axis=-1)
    xc, new_buf = causal_dwconv(xb, conv_buf, p['w_conv'], p['b_conv'])
    xblk = xc.reshape(B, T, LRU_BLOCKS, LRU_BLOCK)
    r = jax.nn.sigmoid((jnp.einsum('btnk,nkj->btnj', xblk, p['w_rg']).reshape(B, T, D_RNN) + p['b_rg']).astype(jnp.float32))
    i = jax.nn.sigmoid((jnp.einsum('btnk,nkj->btnj', xblk, p['w_ig']).reshape(B, T, D_RNN) + p['b_ig']).astype(jnp.float32))
    log_a = -LRU_C * r * jax.nn.softplus(-p['lam'].astype(jnp.float32))
    a = jnp.exp(log_a)
    bx = jnp.sqrt(-jnp.expm1(2.0 * log_a)) * (i * xc.astype(jnp.float32))
    bx = bx.at[:, 0].add(a[:, 0] * h0.astype(jnp.float32))
    _, hs = lax.associative_scan(_affine_combine, (a, bx), axis=1)
    y = hs.astype(h.dtype) * jax.nn.silu(gate)
    return y @ p['w_out'], new_buf, hs[:, -1].astype(h.dtype)


def _wkv_step(S, inp):
    r_t, w_t, k_t, v_t, a_t, b_t = inp
    sa = jnp.einsum('bhij,bhj->bhi', S, a_t)
    S = S * w_t[:, :, None, :] + sa[..., None] * b_t[:, :, None, :] + v_t[..., None] * k_t[:, :, None, :]
    return S, jnp.einsum('bhij,bhj->bhi', S, r_t)


def rwkv7_mixer(h, shift_prev, S0, p):
    B, T, D = h.shape
    H, N = RWKV_HEADS, RWKV_HEAD
    h_prev = jnp.concatenate([shift_prev[:, None].astype(h.dtype), h[:, :-1]], axis=1)
    xx = h_prev - h
    xm = h[None] + xx[None] * p['mu'][:, None, None, :].astype(h.dtype)
    rkvg = jnp.einsum('sbtd,dse->sbte', xm[:4], p['w_in'].reshape(D, 4, D))
    r, k, v, g = rkvg[0], rkvg[1], rkvg[2], rkvg[3]
    w_log = -jax.nn.softplus(-(p['w0'] + jnp.tanh(xm[4] @ p['w1']) @ p['w2']).astype(jnp.float32)) - 0.5
    a = jax.nn.sigmoid((p['a0'] + (xm[5] @ p['a1']) @ p['a2']).astype(jnp.float32))

    def heads(z):
        return z.astype(jnp.float32).reshape(B, T, H, N)

    rh, vh, ah = heads(r), heads(v), heads(a)
    decay = jnp.exp(-jnp.exp(heads(w_log)))
    kk = heads(k * p['k_k'])
    kk = kk / jnp.maximum(jnp.sqrt(jnp.sum(kk * kk, axis=-1, keepdims=True)), 1e-12)
    kh = heads(k) * (1.0 + (ah - 1.0) * p['k_a'].astype(jnp.float32).reshape(H, N))
    aa = -kk
    bb = kk * ah
    xs = tuple(jnp.moveaxis(z, 1, 0) for z in (rh, decay, kh, vh, aa, bb))
    S_T, ys = lax.scan(_wkv_step, S0.astype(jnp.float32), xs)
    o = jnp.moveaxis(ys, 0, 1)
    mu = jnp.mean(o, axis=-1, keepdims=True)
    var = jnp.mean(jnp.square(o - mu), axis=-1, keepdims=True)
    o = ((o - mu) * lax.rsqrt(var + RWKV_GN_EPS)).reshape(B, T, D) * p['gn_g'].astype(jnp.float32) + p['gn_b'].astype(jnp.float32)
    bonus = jnp.sum(rh * kh * p['r_k'].astype(jnp.float32), axis=-1, keepdims=True) * vh
    o = (o + bonus.reshape(B, T, D)).astype(h.dtype) * jax.nn.silu(g)
    return o @ p['w_out'], h[:, -1], S_T.astype(h.dtype)


def rotary(x, pos):
    half = x.shape[-1] // 2
    inv = ROPE_BASE ** (-jnp.arange(half, dtype=jnp.float32) / half)
    ang = pos.astype(jnp.float32)[:, None] * inv[None]
    cos = jnp.cos(ang)[None, :, None, :]
    sin = jnp.sin(ang)[None, :, None, :]
    x1, x2 = x[..., :half], x[..., half:]
    return jnp.concatenate([x1 * cos - x2 * sin, x1 * sin + x2 * cos], axis=-1)


def retention_chunk(q, k, v, S, lg):
    L = q.shape[1]
    idx = jnp.arange(L, dtype=jnp.float32)
    diff = idx[:, None] - idx[None, :]
    causal = diff >= 0
    dmask = jnp.where(causal[None], jnp.exp(jnp.where(causal, diff, 0.0)[None] * lg[:, None, None]), 0.0)
    scores = jnp.einsum('bnhd,bmhd->bhnm', q, k) * dmask[None]
    o = jnp.einsum('bhnm,bmhe->bnhe', scores, v)
    o = o + jnp.einsum('bnhd,bhde->bnhe', q, S) * jnp.exp((idx[:, None] + 1.0) * lg[None, :])[None, :, :, None]
    k_dec = k * jnp.exp((L - 1.0 - idx)[:, None] * lg[None, :])[None, :, :, None]
    S = S * jnp.exp(L * lg)[None, :, None, None] + jnp.einsum('bmhd,bmhe->bhde', k_dec, v)
    return o, S


def retention_mixer(h, S0, pos, n_lead, p):
    B, T, _ = h.shape
    q, k, v, gate = jnp.split(h @ p['w_in'], [D_MODEL, 2 * D_MODEL, 2 * D_MODEL + D_RET_V], axis=-1)
    q = rotary(q.astype(jnp.float32).reshape(B, T, RET_HEADS, RET_DK), pos) * (RET_DK ** -0.5)
    k = rotary(k.astype(jnp.float32).reshape(B, T, RET_HEADS, RET_DK), pos)
    v = v.astype(jnp.float32).reshape(B, T, RET_HEADS, RET_DV)
    lg = jnp.log1p(-jnp.exp2(-5.0 - jnp.arange(RET_HEADS, dtype=jnp.float32)))
    S = S0.astype(jnp.float32)
    outs = []
    if n_lead > 0:
        o0, S = retention_chunk(q[:, :n_lead], k[:, :n_lead], v[:, :n_lead], S, lg)
        outs.append(o0)
    rest = T - n_lead
    if rest <= CHUNK:
        o1, S = retention_chunk(q[:, n_lead:], k[:, n_lead:], v[:, n_lead:], S, lg)
    else:
        nc = rest // CHUNK

        def blk(z):
            return jnp.moveaxis(z[:, n_lead:].reshape(B, nc, CHUNK, z.shape[2], z.shape[3]), 1, 0)

        def step(Sc, qkv):
            oc, Sc = retention_chunk(qkv[0], qkv[1], qkv[2], Sc, lg)
            return Sc, oc

        S, oc = lax.scan(step, S, (blk(q), blk(k), blk(v)))
        o1 = jnp.moveaxis(oc, 0, 1).reshape(B, rest, RET_HEADS, RET_DV)
    outs.append(o1)
    o = jnp.concatenate(outs, axis=1)
    o = o * lax.rsqrt(jnp.mean(o * o, axis=-1, keepdims=True) + NORM_EPS)
    o = o.reshape(B, T, D_RET_V) * p['gn_g'].astype(jnp.float32)
    y = o.astype(h.dtype) * jax.nn.silu(gate)
    return y @ p['w_out'], S.astype(h.dtype)


def run_trunk(x, st, pos, n_lead, norm_g, final_norm_g, pa, pb, pc, pd):
    new = {}
    for i in range(DEPTH):
        kind = i % N_MIXERS
        h = rmsnorm(x, norm_g[i])
        if kind == 0:
            y, new['conv_a'] = conformer_conv_mixer(h, st['conv_a'], pa)
        elif kind == 1:
            y, new['conv_b'], new['lru_b'] = rglru_mixer(h, st['conv_b'], st['lru_b'], pb)
        elif kind == 2:
            y, new['shift_c'], new['wkv_c'] = rwkv7_mixer(h, st['shift_c'], st['wkv_c'], pc)
        else:
            y, new['ret_d'] = retention_mixer(h, st['ret_d'], pos, n_lead, pd)
        x = x + y
    return rmsnorm(x, final_norm_g), new


def setup_inputs(seed: int = 0) -> dict:
    key = jax.random.key(seed)
    keys = iter(jax.random.split(key, 64))
    f32 = jnp.float32

    def nrm(shape, scale):
        return jax.random.normal(next(keys), shape, f32) * scale

    def unif(shape, lo, hi):
        return jax.random.uniform(next(keys), shape, f32, lo, hi)

    D = D_MODEL
    u = unif((D_RNN,), 0.9, 0.999)
    s = u ** (1.0 / LRU_C)
    lam = jnp.log(s) - jnp.log1p(-s)
    return {
        'x_prompt': nrm((BATCH, SEQ, D), 1.0),
        'x_sample': nrm((DEC_BATCH, DEC_SEQ, D), 1.0),
        'cache_conv_a': nrm((DEC_BATCH, CONV_W - 1, D_CONV), 0.5),
        'cache_conv_b': nrm((DEC_BATCH, LRU_CONV_W - 1, D_RNN), 1.0),
        'state_lru_b': nrm((DEC_BATCH, D_RNN), 0.5),
        'state_shift_c': nrm((DEC_BATCH, D), 1.0),
        'state_wkv_c': nrm((DEC_BATCH, RWKV_HEADS, RWKV_HEAD, RWKV_HEAD), 0.5),
        'state_ret_d': nrm((DEC_BATCH, RET_HEADS, RET_DK, RET_DV), 0.5),
        'meta_tokens': nrm((N_META, D), 1.0),
        'norm_g': 1.0 + nrm((DEPTH, D), 0.05),
        'final_norm_g': 1.0 + nrm((D,), 0.05),
        'a_w_in': nrm((D, 3 * D_CONV), D ** -0.5),
        'a_w_dw': nrm((CONV_W, D_CONV), CONV_W ** -0.5),
        'a_b_dw': nrm((D_CONV,), 0.02),
        'a_ln_g': 1.0 + nrm((D_CONV,), 0.05),
        'a_ln_b': nrm((D_CONV,), 0.02),
        'a_w_out': nrm((D_CONV, D), D_CONV ** -0.5),
        'b_w_in': nrm((D, 2 * D_RNN), D ** -0.5),
        'b_w_conv': nrm((LRU_CONV_W, D_RNN), LRU_CONV_W ** -0.5),
        'b_b_conv': nrm((D_RNN,), 0.02),
        'b_w_rg': nrm((LRU_BLOCKS, LRU_BLOCK, LRU_BLOCK), LRU_BLOCK ** -0.5),
        'b_b_rg': nrm((D_RNN,), 0.02),
        'b_w_ig': nrm((LRU_BLOCKS, LRU_BLOCK, LRU_BLOCK), LRU_BLOCK ** -0.5),
        'b_b_ig': nrm((D_RNN,), 0.02),
        'b_lam': lam,
        'b_w_out': nrm((D_RNN, D), D_RNN ** -0.5),
        'c_mu': unif((6, D), 0.0, 1.0),
        'c_w_in': nrm((D, 4 * D), D ** -0.5),
        'c_w0': unif((D,), -6.0, -1.0),
        'c_w1': nrm((D, D_DECAY_LORA), D ** -0.5),
        'c_w2': nrm((D_DECAY_LORA, D), 0.1),
        'c_a0': nrm((D,), 0.1),
        'c_a1': nrm((D, D_A_LORA), D ** -0.5),
        'c_a2': nrm((D_A_LORA, D), 0.5 * D_A_LORA ** -0.5),
        'c_k_k': 0.85 + nrm((D,), 0.05),
        'c_k_a': 1.0 + nrm((D,), 0.05),
        'c_r_k': nrm((RWKV_HEADS, RWKV_HEAD), 0.1),
        'c_gn_g': 1.0 + nrm((D,), 0.05),
        'c_gn_b': nrm((D,), 0.02),
        'c_w_out': nrm((D, D), D ** -0.5),
        'd_w_in': nrm((D, 2 * D + 2 * D_RET_V), D ** -0.5),
        'd_gn_g': 1.0 + nrm((D_RET_V,), 0.05),
        'd_w_out': nrm((D_RET_V, D), D_RET_V ** -0.5),
    }


def reference(x_prompt, x_sample, cache_conv_a, cache_conv_b, state_lru_b, state_shift_c, state_wkv_c, state_ret_d,
              meta_tokens, norm_g, final_norm_g,
              a_w_in, a_w_dw, a_b_dw, a_ln_g, a_ln_b, a_w_out,
              b_w_in, b_w_conv, b_b_conv, b_w_rg, b_b_rg, b_w_ig, b_b_ig, b_lam, b_w_out,
              c_mu, c_w_in, c_w0, c_w1, c_w2, c_a0, c_a1, c_a2, c_k_k, c_k_a, c_r_k, c_gn_g, c_gn_b, c_w_out,
              d_w_in, d_gn_g, d_w_out):
    pa = {'w_in': a_w_in, 'w_dw': a_w_dw, 'b_dw': a_b_dw, 'ln_g': a_ln_g, 'ln_b': a_ln_b, 'w_out': a_w_out}
    pb = {'w_in': b_w_in, 'w_conv': b_w_conv, 'b_conv': b_b_conv, 'w_rg': b_w_rg, 'b_rg': b_b_rg,
          'w_ig': b_w_ig, 'b_ig': b_b_ig, 'lam': b_lam, 'w_out': b_w_out}
    pc = {'mu': c_mu, 'w_in': c_w_in, 'w0': c_w0, 'w1': c_w1, 'w2': c_w2, 'a0': c_a0, 'a1': c_a1, 'a2': c_a2,
          'k_k': c_k_k, 'k_a': c_k_a, 'r_k': c_r_k, 'gn_g': c_gn_g, 'gn_b': c_gn_b, 'w_out': c_w_out}
    pd = {'w_in': d_w_in, 'gn_g': d_gn_g, 'w_out': d_w_out}

    B = x_prompt.shape[0]
    dt = x_prompt.dtype
    meta = jnp.broadcast_to(meta_tokens.astype(dt)[None], (B, N_META, D_MODEL))
    xp = jnp.concatenate([meta, x_prompt], axis=1)
    st0 = {
        'conv_a': jnp.zeros((B, CONV_W - 1, D_CONV), dt),
        'conv_b': jnp.zeros((B, LRU_CONV_W - 1, D_RNN), dt),
        'lru_b': jnp.zeros((B, D_RNN), dt),
        'shift_c': jnp.zeros((B, D_MODEL), dt),
        'wkv_c': jnp.zeros((B, RWKV_HEADS, RWKV_HEAD, RWKV_HEAD), dt),
        'ret_d': jnp.zeros((B, RET_HEADS, RET_DK, RET_DV), dt),
    }
    pos_p = jnp.arange(N_META + x_prompt.shape[1])
    yp, stp = run_trunk(xp, st0, pos_p, N_META, norm_g, final_norm_g, pa, pb, pc, pd)
    y_prompt = yp[:, N_META:]

    st_s = {'conv_a': cache_conv_a, 'conv_b': cache_conv_b, 'lru_b': state_lru_b,
            'shift_c': state_shift_c, 'wkv_c': state_wkv_c, 'ret_d': state_ret_d}
    pos_s = N_META + PAST_LEN + jnp.arange(x_sample.shape[1])
    y_sample, sts = run_trunk(x_sample, st_s, pos_s, 0, norm_g, final_norm_g, pa, pb, pc, pd)

    return (y_prompt, y_sample,
            stp['conv_a'], sts['conv_a'],
            stp['conv_b'], sts['conv_b'],
            stp['lru_b'], sts['lru_b'],
            stp['shift_c'], sts['shift_c'],
            stp['wkv_c'], sts['wkv_c'],
            stp['ret_d'], sts['ret_d'])
```

```python
import numpy as np
from contextlib import ExitStack
import concourse.bass as bass
import concourse.mybir as mybir
from concourse.bass_utils import run_bass_kernel_spmd

F32 = mybir.dt.float32
BF16 = mybir.dt.bfloat16
AF = mybir.ActivationFunctionType
ALU = mybir.AluOpType
AX = mybir.AxisListType

D = 1024
NMETA = 16
PAST = 1024
DRNN = 1280
NORM_EPS = 1e-6
LN_EPS = 1e-5
GN_EPS = 64e-5
SLOT = 4096
NSLOT = 4
ARENA_BYTES = 120 * 1024
MAXB = 256
KTW = 1280


def _esz(dt):
    return 2 if dt == BF16 else 4


class Prog:
    ENGS = ["pe", "act", "dve", "pool", "sp"]

    def __init__(self, nc, es):
        self.nc = nc
        self.es = es
        self.semobj = {}
        self.cnt = {}
        self.ops = {e: [] for e in self.ENGS}
        self.waited = {e: {} for e in self.ENGS}
        self.recs = {}
        self.tracked_dram = set()
        for e in self.ENGS:
            self.semobj["e_" + e] = es.enter_context(nc.semaphore("s_" + e))
            self.cnt["e_" + e] = 0
        self.nops = 0
        self.pe_filler = None

    def rng(self, ap):
        t = ap.tensor
        name = t.name
        tn = type(t).__name__
        if "DRam" in tn:
            if name in self.tracked_dram:
                return (name, 0, 1 << 60)
            return None
        pairs = ap.ap
        row = pairs[0][0]
        esz = _esz(ap.dtype)
        off = ap.offset % row if row > 0 else ap.offset
        ext = sum((c - 1) * abs(s) for s, c in pairs[1:]) + 1
        lo, hi = off * esz, (off + ext) * esz
        if "PSum" in tn:
            lo = (lo // 2048) * 2048
            hi = ((hi + 2047) // 2048) * 2048
            return ("@" + name, lo, hi)
        return (name, lo, hi)

    def op(self, eng, fn, reads, writes, dma_key=None):
        need = {}
        accs = [(self.rng(a), "r") for a in reads] + [(self.rng(a), "w") for a in writes]
        accs = [(r, "w" if (r is not None and r[0].startswith("@")) else k) for r, k in accs]
        for r, kind in accs:
            if r is None:
                continue
            name, lo, hi = r
            for (lo2, hi2, k2, s2), v2 in self.recs.get(name, {}).items():
                if lo2 < hi and lo < hi2 and (kind == "w" or k2 == "w"):
                    if need.get(s2, 0) < v2:
                        need[s2] = v2
        if dma_key is None:
            sk = "e_" + eng
            self.cnt[sk] += 1
            inc = 1
        else:
            sk = "d_" + dma_key + "_" + eng
            if sk not in self.semobj:
                self.semobj[sk] = self.es.enter_context(self.nc.semaphore("s_" + dma_key + "_" + eng))
                self.cnt[sk] = 0
            if self.cnt[sk] > 0:
                need[sk] = max(need.get(sk, 0), self.cnt[sk])
            self.cnt[sk] += 16
            inc = 16
        tokv = self.cnt[sk]
        waits = []
        for s, v in need.items():
            if s == "e_pe" and eng == "pe":
                continue
            if s.startswith("d_"):
                v = self.cnt[s] if s != sk else v
            if self.waited[eng].get(s, 0) >= v:
                continue
            self.waited[eng][s] = v
            waits.append((self.semobj[s], v))
        self.ops[eng].append((waits, fn, (self.semobj[sk], inc), self.pe_filler if eng == "pe" else None))
        self.nops += 1
        for r, kind in accs:
            if r is None:
                continue
            name, lo, hi = r
            d = self.recs.setdefault(name, {})
            if kind == "w":
                for key in [k for k in d if lo <= k[0] and k[1] <= hi]:
                    del d[key]
            d[(lo, hi, kind, sk)] = tokv

    def call(self, eng, method, out, **kw):
        reads = [v for v in kw.values() if isinstance(v, bass.AP)]
        writes = [out]

        def fn(e, method=method, out=out, kw=kw):
            return getattr(e, method)(out=out, **kw)

        self.op(eng, fn, reads, writes)

    def mm(self, groups):
        reads = []
        writes = []
        for out, pairs in groups:
            writes.append(out)
            for l, r in pairs:
                reads.append(l)
                reads.append(r)

        def fn(e, groups=groups):
            ins = None
            for out, pairs in groups:
                n = len(pairs)
                for i, (l, r) in enumerate(pairs):
                    ins = e.matmul(out, l, r, start=(i == 0), stop=(i == n - 1))
            return ins

        self.op("pe", fn, reads, writes)

    def tr(self, items, ident):
        reads = [ident] + [i for _, i in items]
        writes = [o for o, _ in items]

        def fn(e, items=items, ident=ident):
            ins = None
            for o, i in items:
                k = i.shape[0]
                ins = e.transpose(o, i, ident[:k, :k])
            return ins

        self.op("pe", fn, reads, writes)

    def tr3(self, items):
        reads = [i for _, i, _ in items] + [items[0][2]]
        writes = [o for o, _, _ in items]

        def fn(e, items=items):
            ins = None
            for o, i, idn in items:
                ins = e.transpose(o, i, idn)
            return ins

        self.op("pe", fn, reads, writes)

    def dma(self, eng, out, in_, key, slow=False):
        def fn(e, out=out, in_=in_):
            if slow:
                return e.dma_start(out=out, in_=in_, allow_slow_non_contiguous=True)
            return e.dma_start(out=out, in_=in_)

        self.op(eng, fn, [in_], [out], dma_key=key)

    def memset(self, eng, ap, val):
        def fn(e, ap=ap, val=val):
            return e.memset(ap, val)

        self.op(eng, fn, [], [ap])

    def act(self, out, in_, func, **kw):
        self.call("act", "activation", out, in_=in_, func=func, **kw)

    def tt(self, out, in0, in1, op, eng="dve"):
        self.call(eng, "tensor_tensor", out, in0=in0, in1=in1, op=op)

    def ts(self, out, in0, s1, s2, op0, op1=None, eng="dve"):
        if op1 is None:
            self.call(eng, "tensor_scalar", out, in0=in0, scalar1=s1, scalar2=None, op0=op0)
        else:
            self.call(eng, "tensor_scalar", out, in0=in0, scalar1=s1, scalar2=s2, op0=op0, op1=op1)

    def stt(self, out, in0, scalar, in1, op0, op1):
        self.call("dve", "scalar_tensor_tensor", out, in0=in0, scalar=scalar, in1=in1, op0=op0, op1=op1)

    def copy(self, out, in_, eng="dve"):
        if eng == "act":
            self.call("act", "copy", out, in_=in_)
        else:
            self.call(eng, "tensor_copy", out, in_=in_)

    def emit(self):
        nc = self.nc
        final = [(self.semobj[s], self.cnt[s]) for s in self.semobj if s.startswith("d_") and self.cnt[s] > 0]
        ops = self.ops

        def run(name, e):
            for waits, fn, inc, post in ops[name]:
                for s, v in waits:
                    e.wait_ge(s, v)
                ins = fn(e)
                ins.then_inc(inc[0], inc[1])
                if post is not None:
                    post(e)

        with nc.Block() as block:

            @block.tensor
            def _(e):
                run("pe", e)

            @block.scalar
            def _(e):
                run("act", e)

            @block.vector
            def _(e):
                run("dve", e)

            @block.gpsimd
            def _(e):
                run("pool", e)

            @block.sync
            def _(e):
                run("sp", e)
                for s, v in final:
                    e.wait_ge(s, v)


def bc(ap, pos, n):
    pairs = [list(p) for p in ap.ap]
    pairs.insert(pos, [0, n])
    return bass.AP(ap.tensor, ap.offset, pairs)


def make_plan(Tp):
    seqs = [[NMETA] + [64] * ((Tp - NMETA) // 64), [64], [64]]
    blocks = []
    cur = []
    cur_n = 0
    tok = 0
    for s, chs in enumerate(seqs):
        for ci, L in enumerate(chs):
            if cur_n + L > MAXB:
                blocks.append(cur)
                cur = []
                cur_n = 0
            cur.append(dict(seq=s, L=L, off=cur_n, first=(ci == 0), last=(ci == len(chs) - 1), gtok=tok, ci=ci))
            cur_n += L
            tok += L
    blocks.append(cur)
    out = []
    for b in blocks:
        segs = []
        for c in b:
            if segs and segs[-1]["seq"] == c["seq"]:
                segs[-1]["chunks"].append(c)
                segs[-1]["len"] += c["L"]
                segs[-1]["last"] = c["last"]
            else:
                segs.append(dict(seq=c["seq"], off=c["off"], len=c["L"], chunks=[c], first=c["first"], last=c["last"]))
        out.append(dict(Tb=sum(c["L"] for c in b), tok0=b[0]["gtok"], segs=segs, chunks=b))
    return out, tok


C128 = {}
_o = 0
for _n, _w in [("norm_g", 32), ("fin_g", 8), ("a_wdw", 8 * 31), ("a_bdw", 8), ("a_lng", 8), ("a_lnb", 8),
               ("b_wc", 40), ("b_bc", 10), ("b_brg", 10), ("b_big", 10), ("b_lam", 10), ("c_mu", 48), ("d_gng", 16),
               ("c_w0", 8), ("c_a0", 8), ("c_kk", 8), ("c_ka", 8), ("c_rk", 8), ("c_gng", 8), ("c_gnb", 8)]:
    C128[_n] = (_o, _w)
    _o += _w
NC128 = _o
C64 = {}
_o = 0
for _n, _w in [("w0", 16), ("a0", 16), ("kk", 16), ("ka", 16), ("rk", 16), ("gng", 16), ("gnb", 16)]:
    C64[_n] = (_o, _w)
    _o += _w
NC64 = _o


def weight_units():
    U = []

    def colslab(src, c0, nc_, nk=8):
        return (src, 128, nk, nc_, lambda w, c0=c0, nc_=nc_: w.rearrange("(k p) n -> p k n", p=128)[:, :, c0:c0 + nc_])

    for j in range(4):
        U.append(("A_in%d" % j, [colslab("a_w_in", 1024 + 256 * j, 256), colslab("a_w_in", 256 * j, 256)]))
    for j in range(2):
        U.append(("A_g%d" % j, [colslab("a_w_in", 2048 + 512 * j, 512)]))
    for j in range(2):
        U.append(("A_o%d" % j, [colslab("a_w_out", 512 * j, 512)]))
    for j in range(5):
        U.append(("B_in%d" % j, [colslab("b_w_in", 512 * j, 512)]))
    gv = lambda w: w.rearrange("n k j -> k n j")
    U.append(("B_gt", [("b_w_rg", 128, 10, 128, gv), ("b_w_ig", 128, 10, 128, gv)]))
    for j, (c0, n_) in enumerate([(0, 384), (384, 384), (768, 256)]):
        U.append(("B_o%d" % j, [colslab("b_w_out", c0, n_, nk=10)]))
    U.append(("C_lora", [colslab("c_w1", 0, 64), colslab("c_a1", 0, 64),
                         ("c_w2", 64, 1, 1024, lambda w: w.rearrange("(o p) n -> p o n", o=1)),
                         ("c_a2", 64, 1, 1024, lambda w: w.rearrange("(o p) n -> p o n", o=1))]))
    for s, nm in enumerate(["r", "k", "v", "g"]):
        for j in range(2):
            U.append(("C_%s%d" % (nm, j), [colslab("c_w_in", 1024 * s + 512 * j, 512)]))
    for j in range(2):
        U.append(("C_o%d" % j, [colslab("c_w_out", 512 * j, 512)]))
    for j in range(12):
        U.append(("D_in%d" % j, [colslab("d_w_in", 512 * j, 512)]))
    for j in range(4):
        U.append(("D_o%d" % j, [colslab("d_w_out", 256 * j, 256, nk=16)]))
    return U


WSHAPES = {"a_w_in": (1024, 3072), "a_w_out": (1024, 1024), "b_w_in": (1024, 2560), "b_w_rg": (10, 128, 128),
           "b_w_ig": (10, 128, 128), "b_w_out": (1280, 1024), "c_w_in": (1024, 4096), "c_w1": (1024, 64),
           "c_w2": (64, 1024), "c_a1": (1024, 64), "c_a2": (64, 1024), "c_w_out": (1024, 1024),
           "d_w_in": (1024, 6144), "d_w_out": (2048, 1024)}


def build(Tp, NL=4):
    plan, NT = make_plan(Tp)
    nc = bass.Bass("TRN2", target_bir_lowering=False)
    es = ExitStack()
    es.enter_context(nc.allow_low_precision("bf16 matmul operands, fp32 accumulation"))
    P = Prog(nc, es)

    def din(name, shape, dt=F32):
        return nc.dram_tensor(name, list(shape), dt, kind="ExternalInput").ap()

    def dout(name, shape):
        return nc.dram_tensor(name, list(shape), F32, kind="ExternalOutput").ap()

    xin = din("xin", (D, NT))
    cst128_d = din("cst128", (128, NC128))
    ktab_d = din("ktab", (128, KTW))
    rope_d = din("rope", (128, 6, NT))
    st_ca = din("st_ca", (3, D, 30))
    st_cb = din("st_cb", (3, DRNN, 3))
    st_lru = din("st_lru", (3, DRNN))
    st_sh = din("st_sh", (3, D))
    st_wkv = din("st_wkv", (3, 128, 8, 64))
    st_ret = din("st_ret", (3, 1024, 512))
    wd = {n: din(n, s) for n, s in WSHAPES.items()}
    yT = dout("yT", (D, NT))
    o_ca = dout("o_ca", (3, D, 30))
    o_cb = dout("o_cb", (3, DRNN, 3))
    o_lru = dout("o_lru", (3, DRNN))
    o_sh = dout("o_sh", (3, D))
    o_wkv = dout("o_wkv", (3, 128, 8, 64))
    o_ret = dout("o_ret", (3, 1024, 512))
    units = weight_units()
    NU = len(units)
    wscr = nc.dram_tensor("wscr", [NU, 128, SLOT], BF16, kind="Internal").ap()
    P.tracked_dram.add("wscr")

    def sb(name, shape, dt=F32):
        return es.enter_context(nc.sbuf_tensor("sb_" + name, list(shape), dt))

    psall = es.enter_context(nc.psum_tensor("psall", [128, 4096], F32))
    psall_bf = psall.bitcast(BF16)
    arena = sb("arena", [128, ARENA_BYTES // 4])
    arena_bf = arena.bitcast(BF16)
    xT = sb("xT", [128, 8, MAXB])
    cst = sb("cst", [128, NC128])
    ktab = sb("ktab_sb", [128, KTW])
    ktb = sb("ktb", [128, 640], BF16)
    cder = sb("cder", [128, 64])
    haloA = sb("haloA", [128, 8, 30])
    haloB = sb("haloB", [128, 10, 3])
    hstB = sb("hstB", [128, 10])
    shC = sb("shC", [128, 8])
    Mst = sb("Mst", [128, 8, 64])
    Mbf = sb("Mbf", [128, 8, 64], BF16)
    Sst = sb("Sst", [128, 8, 512])
    Sbf = sb("Sbf", [128, 8, 512], BF16)
    ring = [sb("ring%d" % i, [128, SLOT], BF16) for i in range(NSLOT)]

    NBANK = 7 if FILLER > 0 else 8
    bankctr = [0]

    def bank():
        i = bankctr[0] % NBANK
        bankctr[0] += 1
        return psall[:, 512 * i:512 * i + 512], psall_bf[:, 1024 * i:1024 * i + 1024]

    pairctr = [0]

    def bank2():
        i = (pairctr[0] % (NBANK // 2)) * 2
        pairctr[0] += 1
        return psall[:, 512 * i:512 * i + 1024], psall_bf[:, 1024 * i:1024 * i + 2048]

    aoff = [0]

    def areset():
        aoff[0] = 0

    def aal(shape, dt=F32):
        n = 1
        for s in shape[1:]:
            n *= s
        nb = n * _esz(dt)
        nb = (nb + 31) // 32 * 32
        o = aoff[0]
        aoff[0] += nb
        assert aoff[0] <= ARENA_BYTES, ("arena overflow", aoff[0])
        base = arena if dt == F32 else arena_bf
        e0 = o // _esz(dt)
        ap = base[:shape[0], e0:e0 + n]
        if len(shape) == 3:
            ap = ap.rearrange("p (a b) -> p a b", a=shape[1])
        return ap

    def c128(name, i0=0, n=None):
        o, w = C128[name]
        n = w - i0 if n is None else n
        return cst[:, o + i0:o + i0 + n]

    ident_bf = ktb[:, 0:128]
    onesD = ktb[:, 128:256]
    ones512 = ktb[:, 256:384]
    bd64 = ktb[:, 384:512]
    bd1 = ktb[:, 512:640]
    triS = ktab[:, 640:704]
    triI = ktab[:, 704:768]
    triL = ktab[:, 768:832]
    identf = ktab[:, 832:896]
    dmaskT = ktab[:64, 896:1152]
    kdec64 = ktab[:64, 1152:1156]
    kdec16 = ktab[:64, 1156:1160]
    onesf = ktab[:, 1160:1224]

    if FILLER > 0:
        dummy_ps = psall[:, 512 * 7:512 * 8]

        def _filler(e):
            for _ in range(FILLER):
                e.matmul(dummy_ps, ident_bf, ktb[:, 0:512], start=True, stop=True)

        P.pe_filler = _filler

    P.dma("pool", cst[:, :], cst128_d, "cst")
    P.dma("pool", ktab[:, :], ktab_d, "cst")
    P.copy(ktb[:, :], ktab[:, 0:640])
    P.act(cder[:, 0:10], c128("b_lam"), AF.Exp, scale=-1.0)
    xz = cder[:, 0:10]
    zz = cder[:, 32:42]
    z2 = cder[:, 42:52]
    P.ts(zz, xz, 2.0, None, ALU.add)
    P.call("dve", "reciprocal", zz, in_=zz)
    P.tt(zz, zz, xz, ALU.mult)
    P.tt(z2, zz, zz, ALU.mult)
    P.ts(xz, z2, 1.0 / 9.0, 1.0 / 7.0, ALU.mult, ALU.add)
    P.tt(xz, xz, z2, ALU.mult)
    P.ts(xz, xz, 1.0 / 5.0, None, ALU.add)
    P.tt(xz, xz, z2, ALU.mult)
    P.ts(xz, xz, 1.0 / 3.0, None, ALU.add)
    P.tt(xz, xz, z2, ALU.mult)
    P.ts(xz, xz, 1.0, None, ALU.add)
    P.tt(xz, xz, zz, ALU.mult)
    P.ts(cder[:, 0:10], xz, -16.0, None, ALU.mult)
    P.ts(cder[:, 16:24], c128("c_ka"), -1.0, 1.0, ALU.mult, ALU.add)

    areset()
    NSTG = 4
    stg32 = [aal([128, SLOT]) for _ in range(NSTG)]
    stgbf = [aal([128, SLOT], BF16) for _ in range(NSTG)]
    cast_engs = ["dve", "act", "dve", "act", "pool"]
    for u, (uname, pieces) in enumerate(units):
        par = u % NSTG
        off = 0
        for pi, (src, Pp, a, b, view) in enumerate(pieces):
            n = a * b
            dst32 = stg32[par][:Pp, off:off + n].rearrange("p (a b) -> p a b", a=a)
            P.dma("sp" if pi % 2 == 0 else "pool", dst32, view(wd[src]), "stg%d" % par)
            P.copy(stgbf[par][:Pp, off:off + n], stg32[par][:Pp, off:off + n], eng=cast_engs[(u + pi) % 5])
            P.dma("sp" if pi % 2 == 0 else "pool", wscr[u, :Pp, off:off + n], stgbf[par][:Pp, off:off + n], "stgo%d" % par)
            off += n
        assert off <= SLOT

    uidx = {n: i for i, (n, _) in enumerate(units)}
    layer_units = {L: [n for n, _ in units if n.startswith(L + "_")] for L in "ABCD"}
    stream = []
    for bi in range(len(plan)):
        for L in "ABCD"[:NL]:
            for n in layer_units[L]:
                stream.append(n)
    issued = [0]
    sidx = [0]

    def issue_upto(k):
        while issued[0] < min(k, len(stream)):
            i = issued[0]
            u = uidx[stream[i]]
            slot = ring[i % NSLOT]
            off = 0
            for pi, (src, Pp, a, b, view) in enumerate(units[u][1]):
                n = a * b
                P.dma("sp", slot[:Pp, off:off + n], wscr[u, :Pp, off:off + n], "ring%d" % (i % NSLOT))
                off += n
            issued[0] += 1

    def wget(name):
        i = sidx[0]
        assert stream[i] == name, (stream[i], name)
        issue_upto(i + NSLOT)
        sidx[0] += 1
        slot = ring[i % NSLOT]
        views = []
        off = 0
        for (src, Pp, a, b, view) in units[uidx[name]][1]:
            n = a * b
            views.append(slot[:Pp, off:off + n].rearrange("p (a b) -> p a b", a=a))
            off += n
        return views

    def rsqrt(out, in_, eps):
        P.act(out, in_, AF.Ln, bias=float(eps))
        P.act(out, out, AF.Exp, scale=-0.5)

    def rmsnorm(Tb, gname, gi0, dst, dstf=None):
        sq = aal([128, 8, MAXB], BF16)
        rstd = aal([128, MAXB])
        P.act(sq[:, :, :Tb], xT[:, :, :Tb], AF.Square)
        ps, _ = bank()
        P.mm([(ps[:, :Tb], [(onesD, sq[:, k, :Tb]) for k in range(8)])])
        rsqrt(rstd[:, :Tb], ps[:, :Tb], NORM_EPS)
        for k in range(8):
            g = c128(gname, gi0 + k, 1)
            if dst is not None:
                P.stt(dst[:, k, :Tb], xT[:, k, :Tb], g, rstd[:, :Tb], ALU.mult, ALU.mult)
            if dstf is not None:
                P.stt(dstf[:, k, :Tb], xT[:, k, :Tb], g, rstd[:, :Tb], ALU.mult, ALU.mult)

    def outproj(Tb, unit_names, ncols_per_unit, yK, nk):
        m = 0
        for un, ncu in zip(unit_names, ncols_per_unit):
            (w,) = wget(un)
            for j in range(ncu // 128):
                ps, _ = bank()
                P.mm([(ps[:, :Tb], [(w[:, k, 128 * j:128 * j + 128], yK(k)) for k in range(nk)])])
                P.tt(xT[:, m, :Tb], xT[:, m, :Tb], ps[:, :Tb], ALU.add)
                m += 1
        assert m == 8

    def layerA(blk):
        Tb = blk["Tb"]
        segs = blk["segs"]
        areset()
        hT = aal([128, 8, MAXB], BF16)
        WA = 30 * len(segs) + Tb
        glu = aal([128, 8, WA])
        gate = aal([128, 8, MAXB], BF16)
        sig = [aal([128, MAXB]) for _ in range(2)]
        acc = aal([128, 8, MAXB])
        rmsnorm(Tb, "norm_g", 0, hT)
        scol = [30 * (i + 1) + s["off"] for i, s in enumerate(segs)]
        for i, s in enumerate(segs):
            h = glu[:, :, scol[i] - 30:scol[i]]
            if s["first"]:
                P.dma("pool", h, st_ca[s["seq"]].rearrange("(k p) t -> p k t", p=128), "stin")
            else:
                P.copy(h, haloA[:, :, :], eng="pool")
        for j in range(4):
            wb, wa = wget("A_in%d" % j)
            for jj in range(2):
                m = 2 * j + jj
                ps, _ = bank()
                P.mm([(ps[:, :Tb], [(wb[:, k, 128 * jj:128 * jj + 128], hT[:, k, :Tb]) for k in range(8)])])
                sg = sig[m % 2]
                P.act(sg[:, :Tb], ps[:, :Tb], AF.Sigmoid)
                ps2, _ = bank()
                P.mm([(ps2[:, :Tb], [(wa[:, k, 128 * jj:128 * jj + 128], hT[:, k, :Tb]) for k in range(8)])])
                for i, s in enumerate(segs):
                    P.tt(glu[:, m, scol[i]:scol[i] + s["len"]], ps2[:, s["off"]:s["off"] + s["len"]],
                         sg[:, s["off"]:s["off"] + s["len"]], ALU.mult)
        for j in range(2):
            (wg,) = wget("A_g%d" % j)
            for jj in range(4):
                m = 4 * j + jj
                ps, _ = bank()
                P.mm([(ps[:, :Tb], [(wg[:, k, 128 * jj:128 * jj + 128], hT[:, k, :Tb]) for k in range(8)])])
                P.act(gate[:, m, :Tb], ps[:, :Tb], AF.Silu)
        o_w, _ = C128["a_wdw"]
        ptmp = aal([128, 2, MAXB])
        for j in range(31):
            for m in range(8):
                for i, s in enumerate(segs):
                    c0 = scol[i] - 30
                    L = s["len"]
                    dst = acc[:, m, s["off"]:s["off"] + L]
                    src = glu[:, m, c0 + j:c0 + j + L]
                    wj = cst[:, o_w + m * 31 + j:o_w + m * 31 + j + 1]
                    if m < 8:
                        if j == 0:
                            P.ts(dst, src, wj, c128("a_bdw", m, 1), ALU.mult, ALU.add)
                        else:
                            P.stt(dst, src, wj, dst, ALU.mult, ALU.add)
                    else:
                        if j == 0:
                            P.ts(dst, src, wj, c128("a_bdw", m, 1), ALU.mult, ALU.add, eng="pool")
                        else:
                            tp_ = ptmp[:, m - 6, s["off"]:s["off"] + L]
                            P.ts(tp_, src, wj, None, ALU.mult, eng="pool")
                            P.tt(dst, dst, tp_, ALU.add, eng="pool")
        for i, s in enumerate(segs):
            last30 = glu[:, :, scol[i] + s["len"] - 30:scol[i] + s["len"]]
            if s["last"]:
                P.dma("pool", o_ca[s["seq"]].rearrange("(k p) t -> p k t", p=128), last30, "o_ca")
            else:
                P.copy(haloA[:, :, :], last30, eng="pool")
        cb = aal([128, 8, MAXB], BF16)
        csq = aal([128, 8, MAXB], BF16)
        P.copy(cb[:, :, :Tb], acc[:, :, :Tb], eng="act")
        P.act(csq[:, :, :Tb], acc[:, :, :Tb], AF.Square)
        psm, _ = bank()
        pss, _ = bank()
        P.mm([(psm[:, :Tb], [(onesD, cb[:, k, :Tb]) for k in range(8)])])
        P.mm([(pss[:, :Tb], [(onesD, csq[:, k, :Tb]) for k in range(8)])])
        mean = aal([128, MAXB])
        var = aal([128, MAXB])
        P.copy(mean[:, :Tb], psm[:, :Tb], eng="act")
        P.tt(var[:, :Tb], mean[:, :Tb], mean[:, :Tb], ALU.mult)
        P.tt(var[:, :Tb], pss[:, :Tb], var[:, :Tb], ALU.subtract)
        rsqrt(var[:, :Tb], var[:, :Tb], LN_EPS)
        P.stt(mean[:, :Tb], mean[:, :Tb], -1.0, var[:, :Tb], ALU.mult, ALU.mult)
        P.tt(acc[:, :, :Tb], acc[:, :, :Tb], bc(var[:, :Tb], 1, 8), ALU.mult)
        P.tt(acc[:, :, :Tb], acc[:, :, :Tb], bc(mean[:, :Tb], 1, 8), ALU.add)
        for m in range(8):
            P.act(acc[:, m, :Tb], acc[:, m, :Tb], AF.Silu, bias=c128("a_lnb", m, 1), scale=c128("a_lng", m, 1))
        yb = hT
        P.tt(yb[:, :, :Tb], acc[:, :, :Tb], gate[:, :, :Tb], ALU.mult)
        outproj(Tb, ["A_o0", "A_o1"], [512, 512], lambda k: yb[:, k, :Tb], 8)

    def layerB(blk):
        Tb = blk["Tb"]
        segs = blk["segs"]
        areset()
        hT = aal([128, 8, MAXB], BF16)
        WB = 3 * len(segs) + Tb
        xb = aal([128, 10, WB])
        gs = aal([128, 10, MAXB], BF16)
        xc = aal([128, 10, MAXB])
        xcb = aal([128, 10, MAXB], BF16)
        rmsnorm(Tb, "norm_g", 8, hT)
        scol = [3 * (i + 1) + s["off"] for i, s in enumerate(segs)]
        for i, s in enumerate(segs):
            h = xb[:, :, scol[i] - 3:scol[i]]
            if s["first"]:
                P.dma("pool", h, st_cb[s["seq"]].rearrange("(k p) t -> p k t", p=128), "stin")
            else:
                P.copy(h, haloB[:, :, :], eng="pool")
        for j in range(5):
            (w,) = wget("B_in%d" % j)
            for jj in range(4):
                m = 4 * j + jj
                ps, _ = bank()
                P.mm([(ps[:, :Tb], [(w[:, k, 128 * jj:128 * jj + 128], hT[:, k, :Tb]) for k in range(8)])])
                if m < 10:
                    for i, s in enumerate(segs):
                        P.copy(xb[:, m, scol[i]:scol[i] + s["len"]], ps[:, s["off"]:s["off"] + s["len"]], eng="act")
                else:
                    P.act(gs[:, m - 10, :Tb], ps[:, :Tb], AF.Silu)
        o_w, _ = C128["b_wc"]
        for j in range(4):
            for m in range(10):
                for i, s in enumerate(segs):
                    c0 = scol[i] - 3
                    L = s["len"]
                    dst = xc[:, m, s["off"]:s["off"] + L]
                    wj = cst[:, o_w + m * 4 + j:o_w + m * 4 + j + 1]
                    if j == 0:
                        P.ts(dst, xb[:, m, c0:c0 + L], wj, c128("b_bc", m, 1), ALU.mult, ALU.add)
                    else:
                        P.stt(dst, xb[:, m, c0 + j:c0 + j + L], wj, dst, ALU.mult, ALU.add)
        P.copy(xcb[:, :, :Tb], xc[:, :, :Tb], eng="act")
        for i, s in enumerate(segs):
            last3 = xb[:, :, scol[i] + s["len"] - 3:scol[i] + s["len"]]
            if s["last"]:
                P.dma("pool", o_cb[s["seq"]].rearrange("(k p) t -> p k t", p=128), last3, "o_cb", slow=True)
            else:
                P.copy(haloB[:, :, :], last3, eng="pool")
        R = aal([128, 10, MAXB])
        I = aal([128, 10, MAXB])
        A1 = aal([128, 10, MAXB])
        wrg, wig = wget("B_gt")
        for n in range(10):
            ps, _ = bank()
            P.mm([(ps[:, :Tb], [(wrg[:, n, :], xcb[:, n, :Tb])])])
            P.act(R[:, n, :Tb], ps[:, :Tb], AF.Sigmoid, bias=c128("b_brg", n, 1))
            ps2, _ = bank()
            P.mm([(ps2[:, :Tb], [(wig[:, n, :], xcb[:, n, :Tb])])])
            P.act(I[:, n, :Tb], ps2[:, :Tb], AF.Sigmoid, bias=c128("b_big", n, 1))
        for n in range(10):
            P.ts(R[:, n, :Tb], R[:, n, :Tb], cder[:, n:n + 1], None, ALU.mult)
        P.act(A1[:, :, :Tb], R[:, :, :Tb], AF.Exp)
        P.act(R[:, :, :Tb], R[:, :, :Tb], AF.Exp, scale=2.0)
        P.act(R[:, :, :Tb], R[:, :, :Tb], AF.Sqrt, scale=-1.0, bias=1.0)
        P.tt(I[:, :, :Tb], I[:, :, :Tb], R[:, :, :Tb], ALU.mult)
        P.tt(I[:, :, :Tb], I[:, :, :Tb], xc[:, :, :Tb], ALU.mult)
        for i, s in enumerate(segs):
            if s["first"]:
                P.dma("pool", hstB[:, :], st_lru[s["seq"]].rearrange("(k p) -> p k", p=128), "stin", slow=True)
            o, L = s["off"], s["len"]
            for n in range(10):
                P.call("dve", "tensor_tensor_scan", R[:, n, o:o + L], data0=A1[:, n, o:o + L], data1=I[:, n, o:o + L],
                       initial=hstB[:, n:n + 1], op0=ALU.mult, op1=ALU.add)
            P.copy(hstB[:, :], R[:, :, o + L - 1])
            if s["last"]:
                P.dma("pool", o_lru[s["seq"]].rearrange("(k p) -> p k", p=128), hstB[:, :], "o_lru", slow=True)
        yb = xcb
        P.tt(yb[:, :, :Tb], R[:, :, :Tb], gs[:, :, :Tb], ALU.mult)
        outproj(Tb, ["B_o0", "B_o1", "B_o2"], [384, 384, 256], lambda k: yb[:, k, :Tb], 10)

    def layerD(blk):
        Tb = blk["Tb"]
        segs = blk["segs"]
        tok0 = blk["tok0"]
        areset()
        hT = aal([128, 8, MAXB], BF16)
        qk = aal([128, 16, MAXB])
        vT = aal([128, 16, MAXB], BF16)
        gs = aal([128, 16, MAXB], BF16)
        rope = aal([128, 6, MAXB])
        rmsnorm(Tb, "norm_g", 24, hT)
        P.dma("pool", rope[:, :, :Tb], rope_d[:, :, tok0:tok0 + Tb], "rope")
        for j in range(12):
            (w,) = wget("D_in%d" % j)
            for jj in range(4):
                m = 4 * j + jj
                ps, _ = bank()
                P.mm([(ps[:, :Tb], [(w[:, k, 128 * jj:128 * jj + 128], hT[:, k, :Tb]) for k in range(8)])])
                if m < 16:
                    P.copy(qk[:, m, :Tb], ps[:, :Tb], eng="act")
                elif m < 32:
                    P.copy(vT[:, m - 16, :Tb], ps[:, :Tb], eng="act")
                else:
                    P.act(gs[:, m - 32, :Tb], ps[:, :Tb], AF.Silu)
        qr = aal([128, 16, MAXB], BF16)
        qd = aal([128, 8, MAXB], BF16)
        t1 = aal([128, 8, MAXB])
        t2 = aal([128, 8, MAXB])
        x1 = bass.AP(qk.tensor, qk.offset, [list(qk.ap[0]), [2 * MAXB, 8], [1, Tb]])
        x2 = bass.AP(qk.tensor, qk.offset + MAXB, [list(qk.ap[0]), [2 * MAXB, 8], [1, Tb]])
        o1 = bass.AP(qr.tensor, qr.offset, [list(qr.ap[0]), [2 * MAXB, 8], [1, Tb]])
        o2 = bass.AP(qr.tensor, qr.offset + MAXB, [list(qr.ap[0]), [2 * MAXB, 8], [1, Tb]])
        cosb = bc(rope[:, 0, :Tb], 1, 8)
        sinb = bc(rope[:, 1, :Tb], 1, 8)
        P.tt(t1[:, :, :Tb], x1, cosb, ALU.mult)
        P.tt(t2[:, :, :Tb], x2, sinb, ALU.mult)
        P.tt(o1, t1[:, :, :Tb], t2[:, :, :Tb], ALU.subtract)
        P.tt(t1[:, :, :Tb], x1, sinb, ALU.mult)
        P.tt(t2[:, :, :Tb], x2, cosb, ALU.mult)
        P.tt(o2, t1[:, :, :Tb], t2[:, :, :Tb], ALU.add)
        for h in range(4):
            P.tt(qd[:, 2 * h:2 * h + 2, :Tb], qr[:, 2 * h:2 * h + 2, :Tb], bc(rope[:, 2 + h, :Tb], 1, 2), ALU.mult)
        oT = aal([128, 16, MAXB])
        vtok = [aal([64, 2048], BF16) for _ in range(2)]
        kdt = [aal([64, 1024], BF16) for _ in range(2)]
        scm = [aal([64, 256], BF16) for _ in range(2)]
        dchunks = blk["chunks"]

        def d_pre(ci):
            c = dchunks[ci]
            o, L = c["off"], c["L"]
            vt = vtok[ci % 2]
            kt = kdt[ci % 2]
            sm = scm[ci % 2]
            _, pb = bank2()
            P.tr([(pb[:L, 128 * e:128 * e + 128], vT[:, e, o:o + L]) for e in range(16)], ident_bf)
            P.copy(vt[:L, :], pb[:L, 0:2048], eng="act")
            _, pk = bank()
            P.tr([(pk[:L, 128 * e:128 * e + 128], qr[:, 8 + e, o:o + L]) for e in range(8)], ident_bf)
            kdc = kdec64 if L == 64 else kdec16
            for h in range(4):
                P.ts(kt[:L, 256 * h:256 * h + 256], pk[:L, 256 * h:256 * h + 256], kdc[:L, h:h + 1], None, ALU.mult)
            ps, _ = bank()
            P.mm([(ps[:L, 64 * h:64 * h + L], [(qr[:, 8 + 2 * h + dc, o:o + L], qr[:, 2 * h + dc, o:o + L]) for dc in range(2)])
                  for h in range(4)])
            P.tt(sm[:L, :].rearrange("p (h n) -> p h n", h=4)[:, :, :L], ps[:L, 0:256].rearrange("p (h n) -> p h n", h=4)[:, :, :L],
                 dmaskT[:L, :].rearrange("p (h n) -> p h n", h=4)[:, :, :L], ALU.mult)

        def d_main(ci):
            c = dchunks[ci]
            o, L, sq_ = c["off"], c["L"], c["seq"]
            vt = vtok[ci % 2]
            kt = kdt[ci % 2]
            sm = scm[ci % 2]
            if c["first"]:
                P.dma("pool", Sst[:, :, :], st_ret[sq_].rearrange("(c p) e -> p c e", p=128), "stin")
                P.copy(Sbf[:, :, :], Sst[:, :, :], eng="act")
            po, _ = bank2()
            grp = []
            for h in range(4):
                for ec in range(4):
                    col = (4 * h + ec) * 64
                    pairs = [(vt[:L, 512 * h + 128 * ec:512 * h + 128 * ec + 128], sm[:L, 64 * h:64 * h + L])]
                    for dc in range(2):
                        pairs.append((Sbf[:, 2 * h + dc, 128 * ec:128 * ec + 128], qd[:, 2 * h + dc, o:o + L]))
                    grp.append((po[:, col:col + L], pairs))
            P.mm(grp)
            P.copy(oT[:, :, o:o + L], po[:, 0:1024].rearrange("p (a n) -> p a n", a=16)[:, :, :L], eng="act")
            for h in range(4):
                gL = float((1.0 - 2.0 ** (-5 - h)) ** L)
                for dc in range(2):
                    pS, _ = bank()
                    P.mm([(pS[:, :], [(kt[:L, 256 * h + 128 * dc:256 * h + 128 * dc + 128], vt[:L, 512 * h:512 * h + 512])])])
                    P.stt(Sst[:, 2 * h + dc, :], Sst[:, 2 * h + dc, :], gL, pS[:, :], ALU.mult, ALU.add)
            P.copy(Sbf[:, :, :], Sst[:, :, :], eng="act")
            if c["last"]:
                P.dma("pool", o_ret[sq_].rearrange("(c p) e -> p c e", p=128), Sst[:, :, :], "o_ret")

        d_pre(0)
        for ci in range(len(dchunks)):
            if ci + 1 < len(dchunks):
                d_pre(ci + 1)
            d_main(ci)
        osq = qr
        P.act(osq[:, :, :Tb], oT[:, :, :Tb], AF.Square)
        rs = t1
        for h in range(4):
            ps, _ = bank()
            P.mm([(ps[:, :Tb], [(ones512, osq[:, 4 * h + ec, :Tb]) for ec in range(4)])])
            rsqrt(rs[:, h, :Tb], ps[:, :Tb], NORM_EPS)
        for c_ in range(16):
            P.stt(oT[:, c_, :Tb], oT[:, c_, :Tb], c128("d_gng", c_, 1), rs[:, c_ // 4, :Tb], ALU.mult, ALU.mult)
        yb = vT
        P.tt(yb[:, :, :Tb], oT[:, :, :Tb], gs[:, :, :Tb], ALU.mult)
        outproj(Tb, ["D_o0", "D_o1", "D_o2", "D_o3"], [256] * 4, lambda k: yb[:, k, :Tb], 16)

    def layerC(blk):
        Tb = blk["Tb"]
        segs = blk["segs"]
        areset()
        hf = aal([128, 8, MAXB + 1])
        xx = aal([128, 8, MAXB])
        xm = [aal([128, 8, MAXB], BF16) for _ in range(2)]
        lw = aal([64, MAXB], BF16)
        la = aal([64, MAXB], BF16)
        rmsnorm(Tb, "norm_g", 16, None, dstf=hf[:, :, 1:])
        prefix_end = aoff[0]
        s0 = segs[0]
        if s0["first"]:
            P.dma("pool", hf[:, :, 0], st_sh[s0["seq"]].rearrange("(k p) -> p k", p=128), "stin", slow=True)
        else:
            P.copy(hf[:, :, 0], shC[:, :], eng="pool")
        P.tt(xx[:, :, :Tb], hf[:, :, 0:Tb], hf[:, :, 1:Tb + 1], ALU.subtract)
        for i, s in enumerate(segs):
            if i > 0:
                P.dma("pool", shC[:, :], st_sh[s["seq"]].rearrange("(k p) -> p k", p=128), "stin", slow=True)
                P.tt(xx[:, :, s["off"]], shC[:, :], hf[:, :, 1 + s["off"]], ALU.subtract)
            if s["last"]:
                P.dma("pool", o_sh[s["seq"]].rearrange("(k p) -> p k", p=128), hf[:, :, s["off"] + s["len"]], "o_sh", slow=True)
        if not segs[-1]["last"]:
            P.copy(shC[:, :], hf[:, :, Tb], eng="pool")

        def drain(names):
            for n_ in names:
                wget(n_)

        C_ALL = ["C_lora"] + ["C_%s%d" % (nm, j) for nm in "rkvg" for j in range(2)] + ["C_o0", "C_o1"]
        if CSTOP == 0:
            drain(C_ALL)
            return
        o_mu, _ = C128["c_mu"]

        def mkxm(si):
            t = xm[si % 2]
            for k in range(8):
                P.stt(t[:, k, :Tb], xx[:, k, :Tb], cst[:, o_mu + si * 8 + k:o_mu + si * 8 + k + 1], hf[:, k, 1:Tb + 1], ALU.mult, ALU.add)
            return t

        rr = aal([128, 8, MAXB])
        vb = aal([128, 8, MAXB], BF16)
        gsl = aal([128, 8, MAXB], BF16)
        bon = aal([128, 8, MAXB], BF16)
        bh = aal([128, 8, MAXB], BF16)
        kh_ = aal([128, 8, MAXB], BF16)
        at = aal([128, 8, MAXB], BF16)
        rt = aal([128, 8, MAXB], BF16)
        pl = aal([128, 8, 8])
        dead_start = aoff[0]
        kk_ = aal([128, 8, MAXB])
        lwd = aal([128, 8, MAXB])
        av = aal([128, 8, MAXB])
        tA = aal([128, 8, MAXB])
        tB = aal([128, 8, MAXB])
        ex = aal([128, 8, MAXB])
        tbf = aal([128, 8, MAXB], BF16)
        dead_end = aoff[0]
        w1, a1, w2, a2 = wget("C_lora")
        x4 = mkxm(4)
        ps, _ = bank()
        P.mm([(ps[:64, :Tb], [(w1[:, k, :], x4[:, k, :Tb]) for k in range(8)])])
        P.act(lw[:, :Tb], ps[:64, :Tb], AF.Tanh)
        x5 = mkxm(5)
        ps, _ = bank()
        P.mm([(ps[:64, :Tb], [(a1[:, k, :], x5[:, k, :Tb]) for k in range(8)])])
        P.copy(la[:, :Tb], ps[:64, :Tb], eng="act")
        for p in range(8):
            ps, _ = bank()
            P.mm([(ps[:, :Tb], [(w2[:, 0, 128 * p:128 * p + 128], lw[:, :Tb])])])
            P.act(lwd[:, p, :Tb], ps[:, :Tb], AF.Sigmoid, bias=c128("c_w0", p, 1))
            ps2, _ = bank()
            P.mm([(ps2[:, :Tb], [(a2[:, 0, 128 * p:128 * p + 128], la[:, :Tb])])])
            P.act(av[:, p, :Tb], ps2[:, :Tb], AF.Sigmoid, bias=c128("c_a0", p, 1))
        P.ts(lwd[:, :, :Tb], lwd[:, :, :Tb], -float(np.exp(-0.5)), None, ALU.mult)
        if CSTOP == 1:
            drain(C_ALL[1:])
            return

        def proj(si, nm, evac):
            x = mkxm(si)
            for j in range(2):
                (w,) = wget("C_%s%d" % (nm, j))
                for jj in range(4):
                    p = 4 * j + jj
                    ps, _ = bank()
                    P.mm([(ps[:, :Tb], [(w[:, k, 128 * jj:128 * jj + 128], x[:, k, :Tb]) for k in range(8)])])
                    evac(p, ps)

        proj(0, "r", lambda p, ps: P.copy(rr[:, p, :Tb], ps[:, :Tb], eng="act"))
        proj(1, "k", lambda p, ps: P.copy(kk_[:, p, :Tb], ps[:, :Tb], eng="act"))
        proj(2, "v", lambda p, ps: P.copy(vb[:, p, :Tb], ps[:, :Tb], eng="act"))
        proj(3, "g", lambda p, ps: P.act(gsl[:, p, :Tb], ps[:, :Tb], AF.Silu))

        if CSTOP == 2:
            drain(C_ALL[9:])
            return
        for p in range(8):
            P.ts(tA[:, p, :Tb], kk_[:, p, :Tb], c128("c_kk", p, 1), None, ALU.mult)
        P.act(tbf[:, :, :Tb], tA[:, :, :Tb], AF.Square)
        for p in range(8):
            ps, _ = bank()
            P.mm([(ps[:, :Tb], [(bd1, tbf[:, p, :Tb])])])
            P.ts(tB[:, p, :Tb], ps[:, :Tb], 1e-24, None, ALU.max)
            P.act(tB[:, p, :Tb], tB[:, p, :Tb], AF.Ln)
            P.act(tB[:, p, :Tb], tB[:, p, :Tb], AF.Exp, scale=-0.5)
        P.tt(tA[:, :, :Tb], tA[:, :, :Tb], tB[:, :, :Tb], ALU.mult)
        for p in range(8):
            P.ts(tB[:, p, :Tb], av[:, p, :Tb], c128("c_ka", p, 1), cder[:, 16 + p:17 + p], ALU.mult, ALU.add)
        P.tt(kk_[:, :, :Tb], kk_[:, :, :Tb], tB[:, :, :Tb], ALU.mult)
        P.tt(tB[:, :, :Tb], rr[:, :, :Tb], kk_[:, :, :Tb], ALU.mult)
        for p in range(8):
            P.ts(tbf[:, p, :Tb], tB[:, p, :Tb], c128("c_rk", p, 1), None, ALU.mult)
        for p in range(8):
            ps, _ = bank()
            P.mm([(ps[:, :Tb], [(bd1, tbf[:, p, :Tb])])])
            P.tt(bon[:, p, :Tb], ps[:, :Tb], vb[:, p, :Tb], ALU.mult)
        for c in blk["chunks"]:
            o, L = c["off"], c["L"]
            for p in range(8):
                P.call("dve", "tensor_tensor_scan", tB[:, p, o:o + L], data0=onesf[:, :L], data1=lwd[:, p, o:o + L],
                       initial=0.0, op0=ALU.mult, op1=ALU.add)
        P.act(ex[:, :, :Tb], tB[:, :, :Tb], AF.Exp, scale=-1.0)
        P.tt(av[:, :, :Tb], av[:, :, :Tb], tA[:, :, :Tb], ALU.mult)
        P.tt(bh[:, :, :Tb], av[:, :, :Tb], ex[:, :, :Tb], ALU.mult)
        P.tt(kh_[:, :, :Tb], kk_[:, :, :Tb], ex[:, :, :Tb], ALU.mult)
        P.act(ex[:, :, :Tb], tB[:, :, :Tb], AF.Exp)
        P.tt(rt[:, :, :Tb], rr[:, :, :Tb], ex[:, :, :Tb], ALU.mult)
        for ci, c in enumerate(blk["chunks"]):
            P.copy(pl[:, :, ci], ex[:, :, c["off"] + c["L"] - 1], eng="pool")
        P.tt(tB[:, :, :Tb], tB[:, :, :Tb], lwd[:, :, :Tb], ALU.subtract)
        P.act(ex[:, :, :Tb], tB[:, :, :Tb], AF.Exp)
        P.stt(at[:, :, :Tb], tA[:, :, :Tb], -1.0, ex[:, :, :Tb], ALU.mult, ALU.mult)

        if CSTOP == 3:
            drain(C_ALL[9:])
            return
        end_all = aoff[0]
        oTc = rr
        chunks = blk["chunks"]
        aoff[0] = dead_start
        FtL = [[aal([128, 8, 64], BF16) for _ in range(6)] for _ in range(2)]
        assert aoff[0] <= dead_end, (aoff[0], dead_end)
        aoff[0] = 0
        tkB = [aal([128, 8, 64], BF16) for _ in range(2)]
        tkK = [aal([128, 8, 64], BF16) for _ in range(2)]
        tkV = [aal([128, 8, 64], BF16) for _ in range(2)]
        sArb = [aal([128, 8, 64], BF16) for _ in range(2)]
        sArk = [aal([128, 8, 64], BF16) for _ in range(2)]
        sAak = [aal([128, 8, 64], BF16) for _ in range(2)]
        sAab = aal([128, 8, 64], BF16)
        Nn = [aal([128, 8, 64], BF16) for _ in range(2)]
        Nt = [aal([128, 8, 64], BF16) for _ in range(2)]
        Xb = [aal([128, 8, 64], BF16) for _ in range(2)]
        assert aoff[0] <= prefix_end, (aoff[0], prefix_end)
        aoff[0] = end_all

        def HS(t3, p, h2, c0, c1):
            return t3[64 * h2:64 * h2 + 64, p, c0:c1]

        def headmm(rows, ncols, pairs_fn):
            banks = [bank()[0], bank()[0]]
            groups = []
            for h2 in range(2):
                for p in range(8):
                    groups.append((banks[h2][64 * h2:64 * h2 + rows, 64 * p:64 * p + ncols], pairs_fn(p, h2)))
            P.mm(groups)
            return banks

        def pview(banks, h2, rows, ncols):
            return banks[h2][64 * h2:64 * h2 + rows, 0:512].rearrange("p (a n) -> p a n", a=8)[:, :, :ncols]

        def RLf(t3, p, h2, L, c1=None):
            return t3[64 * h2:64 * h2 + L, p, :c1] if c1 is not None else t3[64 * h2:64 * h2 + L, p, :]

        def pre_steps(ci):
            c = chunks[ci]
            o, L = c["off"], c["L"]
            q = ci % 2
            nlev = 6 if L == 64 else 4
            steps = []

            def tstep(src, dstv):
                def f():
                    bk = [bank()[1], bank()[1]]
                    items = []
                    for h2 in range(2):
                        for p in range(8):
                            items.append((bk[h2][64 * h2:64 * h2 + L, 64 * p:64 * p + 64], src[64 * h2:64 * h2 + 64, p, o:o + L],
                                          ident_bf[64 * h2:64 * h2 + 64, 64 * h2:64 * h2 + 64]))
                    P.tr3(items)
                    for h2 in range(2):
                        P.copy(dstv[64 * h2:64 * h2 + L, :, :], bk[h2][64 * h2:64 * h2 + L, 0:512].rearrange("p (a n) -> p a n", a=8), eng="act")
                return f

            for src, dstv in ((bh, tkB[q]), (kh_, tkK[q]), (vb, tkV[q])):
                steps.append(tstep(src, dstv))

            def sstep(lhs, rhs, dst, msk):
                def f():
                    bks = headmm(L, L, lambda p, h2: [(HS(lhs, p, h2, o, o + L), HS(rhs, p, h2, o, o + L))])
                    for h2 in range(2):
                        P.tt(dst[64 * h2:64 * h2 + L, :, :L], pview(bks, h2, L, L), bc(msk[64 * h2:64 * h2 + L, :L], 1, 8), ALU.mult)
                return f

            for lhs, rhs, dst, msk in ((bh, at, sAab, triS), (at, bh, Nn[0], triL), (kh_, at, sAak[q], triS),
                                       (bh, rt, sArb[q], triI), (kh_, rt, sArk[q], triI)):
                steps.append(sstep(lhs, rhs, dst, msk))

            def f0():
                for h2 in range(2):
                    R_ = slice(64 * h2, 64 * h2 + L)
                    P.copy(Nt[0][R_, :, :L], sAab[R_, :, :L], eng="pool")
                    P.tt(FtL[q][0][R_, :, :L], sAab[R_, :, :L], bc(identf[R_, :L], 1, 8), ALU.add, eng="pool")
            steps.insert(5, f0)

            def qstep(lv):
                def f():
                    a_, b_ = lv % 2, (lv + 1) % 2
                    b1 = headmm(L, L, lambda p, h2: [(RLf(Nt[a_], p, h2, L, L), RLf(Nn[a_], p, h2, L, L))])
                    b2 = headmm(L, L, lambda p, h2: [(RLf(Nn[a_], p, h2, L, L), RLf(Nt[a_], p, h2, L, L))])
                    for h2 in range(2):
                        R_ = slice(64 * h2, 64 * h2 + L)
                        if lv < nlev - 2:
                            P.copy(Nn[b_][R_, :, :L], pview(b1, h2, L, L))
                            P.copy(Nt[b_][R_, :, :L], pview(b2, h2, L, L), eng="act")
                        P.tt(FtL[q][lv + 1][R_, :, :L], pview(b2, h2, L, L), bc(identf[R_, :L], 1, 8), ALU.add)
                return f

            for lv in range(nlev - 1):
                steps.append(qstep(lv))
            return steps

        def main_steps(ci):
            c = chunks[ci]
            o, L, sq_ = c["off"], c["L"], c["seq"]
            q = ci % 2
            nlev = 6 if L == 64 else 4
            steps = []

            def x0():
                if c["first"]:
                    P.dma("pool", Mst[:, :, :], st_wkv[sq_], "stin")
                    P.copy(Mbf[:, :, :], Mst[:, :, :], eng="act")
                bks = headmm(L, 64, lambda p, h2: [(HS(at, p, h2, o, o + L), Mbf[64 * h2:64 * h2 + 64, p, :]),
                                                   (RLf(sAak[q], p, h2, L, L), RLf(tkV[q], p, h2, L))])
                for h2 in range(2):
                    P.copy(Xb[0][64 * h2:64 * h2 + L, :, :], pview(bks, h2, L, 64), eng="act")
            steps.append(x0)

            def xstep(lv):
                def f():
                    a_, b_ = lv % 2, (lv + 1) % 2
                    bks = headmm(L, 64, lambda p, h2: [(RLf(FtL[q][lv], p, h2, L, L), RLf(Xb[a_], p, h2, L))])
                    for h2 in range(2):
                        P.copy(Xb[b_][64 * h2:64 * h2 + L, :, :], pview(bks, h2, L, 64), eng="act")
                return f

            for lv in range(nlev):
                steps.append(xstep(lv))
            Ub = Xb[nlev % 2]

            def ostep():
                bks = headmm(64, L, lambda p, h2: [(Mbf[64 * h2:64 * h2 + 64, p, :], HS(rt, p, h2, o, o + L)),
                                                   (RLf(Ub, p, h2, L), RLf(sArb[q], p, h2, L, L)),
                                                   (RLf(tkV[q], p, h2, L), RLf(sArk[q], p, h2, L, L))])
                for h2 in range(2):
                    P.copy(oTc[64 * h2:64 * h2 + 64, :, o:o + L], pview(bks, h2, 64, L), eng="act")
            steps.append(ostep)

            def ststep():
                bks = headmm(64, 64, lambda p, h2: [(RLf(tkB[q], p, h2, L), RLf(Ub, p, h2, L)), (RLf(tkK[q], p, h2, L), RLf(tkV[q], p, h2, L))])
                for h2 in range(2):
                    R_ = slice(64 * h2, 64 * h2 + 64)
                    P.tt(Mst[R_, :, :], Mst[R_, :, :], pview(bks, h2, 64, 64), ALU.add)
                for p in range(8):
                    P.ts(Mst[:, p, :], Mst[:, p, :], pl[:, p, ci:ci + 1], None, ALU.mult)
                P.copy(Mbf[:, :, :], Mst[:, :, :], eng="act")
                if c["last"]:
                    P.dma("pool", o_wkv[sq_], Mst[:, :, :], "o_wkv")
            steps.append(ststep)
            return steps

        for f in pre_steps(0):
            f()
        for ci in range(len(chunks)):
            ms = main_steps(ci)
            pr = pre_steps(ci + 1) if ci + 1 < len(chunks) else []
            for i in range(max(len(ms), len(pr))):
                if i < len(pr):
                    pr[i]()
                if i < len(ms):
                    ms[i]()

        if CSTOP == 4 or 30 < CSTOP < 40:
            drain(C_ALL[9:])
            return
        ob = tbf
        osq = bh
        P.copy(ob[:, :, :Tb], oTc[:, :, :Tb], eng="act")
        P.act(osq[:, :, :Tb], oTc[:, :, :Tb], AF.Square)
        yC = at
        for p in range(8):
            pm, _ = bank()
            pq, _ = bank()
            P.mm([(pm[:, :Tb], [(bd64, ob[:, p, :Tb])])])
            P.mm([(pq[:, :Tb], [(bd64, osq[:, p, :Tb])])])
            mu = tA[:, p, :Tb]
            vr = tB[:, p, :Tb]
            P.copy(mu, pm[:, :Tb], eng="act")
            P.tt(vr, mu, mu, ALU.mult)
            P.tt(vr, pq[:, :Tb], vr, ALU.subtract)
            rsqrt(vr, vr, GN_EPS)
            P.tt(oTc[:, p, :Tb], oTc[:, p, :Tb], mu, ALU.subtract)
            P.tt(oTc[:, p, :Tb], oTc[:, p, :Tb], vr, ALU.mult)
            P.ts(oTc[:, p, :Tb], oTc[:, p, :Tb], c128("c_gng", p, 1), c128("c_gnb", p, 1), ALU.mult, ALU.add)
            P.tt(oTc[:, p, :Tb], oTc[:, p, :Tb], bon[:, p, :Tb], ALU.add)
            P.tt(yC[:, p, :Tb], oTc[:, p, :Tb], gsl[:, p, :Tb], ALU.mult)
        outproj(Tb, ["C_o0", "C_o1"], [512, 512], lambda k: yC[:, k, :Tb], 8)

    layers = [layerA, layerB, layerC, layerD][:NL]
    for blk in plan:
        Tb = blk["Tb"]
        P.dma("pool", xT[:, :, :Tb], xin.rearrange("(k p) t -> p k t", p=128)[:, :, blk["tok0"]:blk["tok0"] + Tb], "xin")
        for lf in layers:
            lf(blk)
        aoff[0] = ARENA_BYTES - (8 * MAXB * 4 + 8 * MAXB * 2 + MAXB * 4 + 256)
        yo = aal([128, 8, MAXB])
        rmsnorm(Tb, "fin_g", 0, None, dstf=yo)
        P.dma("pool", yT.rearrange("(k p) t -> p k t", p=128)[:, :, blk["tok0"]:blk["tok0"] + Tb], yo[:, :, :Tb], "yout")
    P.emit()
    return nc, P


def host_tables(plan, NT, Tp):
    kt = np.zeros((128, KTW), np.float32)
    kt[:, 0:128] = np.eye(128, dtype=np.float32)
    kt[:, 128:256] = 1.0 / 1024
    kt[:, 256:384] = 1.0 / 512
    bd = np.zeros((128, 128), np.float32)
    bd[:64, :64] = 1.0
    bd[64:, 64:] = 1.0
    kt[:, 384:512] = bd / 64.0
    kt[:, 512:640] = bd
    s = np.arange(64)[:, None]
    t = np.arange(64)[None, :]
    kt[:64, 640:704] = (t > s)
    kt[:64, 704:768] = (t >= s)
    kt[:64, 768:832] = (s > t)
    kt[:64, 832:896] = np.eye(64)
    gam = (1.0 - 2.0 ** (-5.0 - np.arange(4))).astype(np.float64)
    for h in range(4):
        dm = np.where(t >= s, gam[h] ** np.maximum(t - s, 0), 0.0) / 16.0
        kt[:64, 896 + 64 * h:896 + 64 * h + 64] = dm
        m = np.arange(64)
        kt[:64, 1152 + h] = gam[h] ** (63.0 - m)
        kt[:64, 1156 + h] = gam[h] ** np.maximum(15.0 - m, 0.0)
    kt[64:128, 640:896] = kt[0:64, 640:896]
    kt[:, 1160:1224] = 1.0
    return kt


def _prep(inputs):
    x_prompt = np.asarray(inputs["x_prompt"], np.float32)
    x_sample = np.asarray(inputs["x_sample"], np.float32)
    B, S, _ = x_prompt.shape
    Tp = S + NMETA
    plan, NT = make_plan(Tp)
    meta = np.asarray(inputs["meta_tokens"], np.float32)

    def pk(v, p):
        v = np.asarray(v, np.float32).reshape(-1, p)
        return np.ascontiguousarray(v.T)

    c128 = np.zeros((128, NC128), np.float32)

    def put(name, arr):
        o, w = C128[name]
        assert arr.shape == (128, w), (name, arr.shape)
        c128[:, o:o + w] = arr

    put("norm_g", np.concatenate([pk(inputs["norm_g"][i], 128) for i in range(4)], axis=1))
    put("fin_g", pk(inputs["final_norm_g"], 128))
    wdw = np.asarray(inputs["a_w_dw"], np.float32)
    put("a_wdw", np.ascontiguousarray(wdw.reshape(31, 8, 128).transpose(2, 1, 0)).reshape(128, 248))
    put("a_bdw", pk(inputs["a_b_dw"], 128))
    put("a_lng", pk(inputs["a_ln_g"], 128))
    put("a_lnb", pk(inputs["a_ln_b"], 128))
    wc = np.asarray(inputs["b_w_conv"], np.float32)
    put("b_wc", np.ascontiguousarray(wc.reshape(4, 10, 128).transpose(2, 1, 0)).reshape(128, 40))
    put("b_bc", pk(inputs["b_b_conv"], 128))
    put("b_brg", pk(inputs["b_b_rg"], 128))
    put("b_big", pk(inputs["b_b_ig"], 128))
    put("b_lam", pk(inputs["b_lam"], 128))
    mu = np.asarray(inputs["c_mu"], np.float32)
    put("c_mu", np.ascontiguousarray(mu.reshape(6, 8, 128).transpose(2, 0, 1)).reshape(128, 48))
    put("d_gng", pk(inputs["d_gn_g"], 128))
    for name, key in [("c_w0", "c_w0"), ("c_a0", "c_a0"), ("c_kk", "c_k_k"), ("c_ka", "c_k_a"), ("c_rk", "c_r_k"),
                      ("c_gng", "c_gn_g"), ("c_gnb", "c_gn_b")]:
        put(name, pk(np.asarray(inputs[key]).reshape(-1), 128))
    c64 = None

    kt = host_tables(plan, NT, Tp)
    pos = np.zeros(NT, np.float32)
    nin = np.zeros(NT, np.float64)
    for blk in plan:
        for c in blk["chunks"]:
            g0 = c["gtok"]
            L = c["L"]
            if c["seq"] == 0:
                p0 = g0
            else:
                p0 = NMETA + PAST
            pos[g0:g0 + L] = p0 + np.arange(L)
            nin[g0:g0 + L] = np.arange(L)
    half = 128
    inv = (10000.0 ** (-np.arange(half, dtype=np.float32) / half)).astype(np.float32)
    ang = pos[None, :].astype(np.float32) * inv[:, None]
    rope = np.zeros((128, 6, NT), np.float32)
    rope[:, 0, :] = np.cos(ang)
    rope[:, 1, :] = np.sin(ang)
    gam = (1.0 - 2.0 ** (-5.0 - np.arange(4))).astype(np.float64)
    for h in range(4):
        rope[:, 2 + h, :] = (gam[h] ** (nin + 1.0) / 16.0)[None, :]
    return plan, NT, Tp, c128, c64, kt, rope, meta, x_prompt, x_sample


_CACHE = {}
_SIM = None
CSTOP = 99
FILLER = 0


def kernel(**inputs):
    plan, NT, Tp, c128, c64, kt, rope, meta, x_prompt, x_sample = _prep(inputs)
    NL = int(inputs.pop("_NL", 4)) if "_NL" in inputs else 4
    key = (Tp, NL)
    if key not in _CACHE:
        _CACHE[key] = build(Tp, NL)
    nc, _ = _CACHE[key]
    B = x_prompt.shape[0]
    f32 = lambda a: np.ascontiguousarray(np.asarray(a, np.float32))
    ktf = kt.copy()
    in_maps = []
    for core in range(8):
        b = core % B
        sidx = [2 * (core % 4), 2 * (core % 4) + 1]
        xs = np.concatenate([meta, x_prompt[b], x_sample[sidx[0]], x_sample[sidx[1]]], axis=0)
        m = {"xin": f32(xs.T), "cst128": c128, "ktab": ktf, "rope": rope}

        def st3(samp, zshape, tr):
            z = np.zeros(zshape, np.float32)
            return f32(np.stack([z, tr(samp[sidx[0]]), tr(samp[sidx[1]])], axis=0))

        m["st_ca"] = st3(np.asarray(inputs["cache_conv_a"], np.float32), (D, 30), lambda a: a.T)
        m["st_cb"] = st3(np.asarray(inputs["cache_conv_b"], np.float32), (DRNN, 3), lambda a: a.T)
        m["st_lru"] = st3(np.asarray(inputs["state_lru_b"], np.float32), (DRNN,), lambda a: a)
        m["st_sh"] = st3(np.asarray(inputs["state_shift_c"], np.float32), (D,), lambda a: a)
        m["st_wkv"] = st3(np.asarray(inputs["state_wkv_c"], np.float32), (128, 8, 64),
                           lambda a: a.reshape(8, 2, 64, 64).transpose(1, 3, 0, 2).reshape(128, 8, 64))
        m["st_ret"] = st3(np.asarray(inputs["state_ret_d"], np.float32), (1024, 512), lambda a: a.reshape(1024, 512))
        for n in WSHAPES:
            m[n] = f32(inputs[n])
        in_maps.append(m)
    if "_SIM" in globals() and _SIM is not None:
        res = _SIM(nc, in_maps)
    else:
        res = run_bass_kernel_spmd(nc, in_maps, core_ids=list(range(8)))
    R = res.results
    S = x_prompt.shape[1]
    y_prompt = np.stack([R[b]["yT"][:, NMETA:Tp].T for b in range(B)], axis=0)
    y_sample = np.stack([R[i // 2]["yT"][:, Tp + 64 * (i % 2):Tp + 64 * (i % 2) + 64].T for i in range(8)], axis=0)

    def outs(name, tr):
        p = np.stack([tr(R[b][name][0]) for b in range(B)], axis=0)
        s = np.stack([tr(R[i // 2][name][1 + i % 2]) for i in range(8)], axis=0)
        return np.ascontiguousarray(p, np.float32), np.ascontiguousarray(s, np.float32)

    ca_p, ca_s = outs("o_ca", lambda a: a.T)
    cb_p, cb_s = outs("o_cb", lambda a: a.T)
    lr_p, lr_s = outs("o_lru", lambda a: a)
    sh_p, sh_s = outs("o_sh", lambda a: a)
    wk_p, wk_s = outs("o_wkv", lambda a: a.reshape(2, 64, 8, 64).transpose(2, 0, 3, 1).reshape(16, 64, 64))
    rt_p, rt_s = outs("o_ret", lambda a: a.reshape(4, 256, 512))
    return (np.ascontiguousarray(y_prompt, np.float32), np.ascontiguousarray(y_sample, np.float32),
            ca_p, ca_s, cb_p, cb_s, lr_p, lr_s, sh_p, sh_s, wk_p, wk_s, rt_p, rt_s)
```

```python
import numpy as np
from contextlib import ExitStack
import concourse.bass as bass
import concourse.mybir as mybir
from concourse.bass_utils import run_bass_kernel_spmd

F32 = mybir.dt.float32
BF16 = mybir.dt.bfloat16
AF = mybir.ActivationFunctionType
ALU = mybir.AluOpType
AX = mybir.AxisListType

D = 1024
NMETA = 16
PAST = 1024
DRNN = 1280
NORM_EPS = 1e-6
LN_EPS = 1e-5
GN_EPS = 64e-5
SLOT = 4096
NSLOT = 5
ARENA_BYTES = 120 * 1024
MAXB = 256
KTW = 1280


def _esz(dt):
    return 2 if dt == BF16 else 4


class Prog:
    ENGS = ["pe", "act", "dve", "pool", "sp"]

    def __init__(self, nc, es):
        self.nc = nc
        self.es = es
        self.semobj = {}
        self.cnt = {}
        self.ops = {e: [] for e in self.ENGS}
        self.waited = {e: {} for e in self.ENGS}
        self.recs = {}
        self.tracked_dram = set()
        for e in self.ENGS:
            self.semobj["e_" + e] = es.enter_context(nc.semaphore("s_" + e))
            self.cnt["e_" + e] = 0
        self.nops = 0
        self.pe_filler = None

    def rng(self, ap):
        t = ap.tensor
        name = t.name
        tn = type(t).__name__
        if "DRam" in tn:
            if name in self.tracked_dram:
                return (name, 0, 1 << 60)
            return None
        pairs = ap.ap
        row = pairs[0][0]
        esz = _esz(ap.dtype)
        off = ap.offset % row if row > 0 else ap.offset
        ext = sum((c - 1) * abs(s) for s, c in pairs[1:]) + 1
        lo, hi = off * esz, (off + ext) * esz
        if "PSum" in tn:
            lo = (lo // 2048) * 2048
            hi = ((hi + 2047) // 2048) * 2048
            return ("@" + name, lo, hi)
        return (name, lo, hi)

    def op(self, eng, fn, reads, writes, dma_key=None):
        need = {}
        accs = [(self.rng(a), "r") for a in reads] + [(self.rng(a), "w") for a in writes]
        accs = [(r, "w" if (r is not None and r[0].startswith("@")) else k) for r, k in accs]
        for r, kind in accs:
            if r is None:
                continue
            name, lo, hi = r
            for (lo2, hi2, k2, s2), v2 in self.recs.get(name, {}).items():
                if lo2 < hi and lo < hi2 and (kind == "w" or k2 == "w"):
                    if need.get(s2, 0) < v2:
                        need[s2] = v2
        if dma_key is None:
            sk = "e_" + eng
            self.cnt[sk] += 1
            inc = 1
        else:
            sk = "d_" + dma_key + "_" + eng
            if sk not in self.semobj:
                self.semobj[sk] = self.es.enter_context(self.nc.semaphore("s_" + dma_key + "_" + eng))
                self.cnt[sk] = 0
            if self.cnt[sk] > 0:
                need[sk] = max(need.get(sk, 0), self.cnt[sk])
            self.cnt[sk] += 16
            inc = 16
        tokv = self.cnt[sk]
        waits = []
        for s, v in need.items():
            if s == "e_pe" and eng == "pe":
                continue
            if s.startswith("d_"):
                v = self.cnt[s] if s != sk else v
            if self.waited[eng].get(s, 0) >= v:
                continue
            self.waited[eng][s] = v
            waits.append((self.semobj[s], v))
        self.ops[eng].append((waits, fn, (self.semobj[sk], inc), self.pe_filler if eng == "pe" else None))
        self.nops += 1
        for r, kind in accs:
            if r is None:
                continue
            name, lo, hi = r
            d = self.recs.setdefault(name, {})
            if kind == "w":
                for key in [k for k in d if lo <= k[0] and k[1] <= hi]:
                    del d[key]
            d[(lo, hi, kind, sk)] = tokv

    def call(self, eng, method, out, **kw):
        reads = [v for v in kw.values() if isinstance(v, bass.AP)]
        writes = [out]

        def fn(e, method=method, out=out, kw=kw):
            return getattr(e, method)(out=out, **kw)

        self.op(eng, fn, reads, writes)

    def mm(self, groups):
        reads = []
        writes = []
        for out, pairs in groups:
            writes.append(out)
            for l, r in pairs:
                reads.append(l)
                reads.append(r)

        def fn(e, groups=groups):
            ins = None
            for out, pairs in groups:
                n = len(pairs)
                for i, (l, r) in enumerate(pairs):
                    ins = e.matmul(out, l, r, start=(i == 0), stop=(i == n - 1))
            return ins

        self.op("pe", fn, reads, writes)

    def tr(self, items, ident):
        reads = [ident] + [i for _, i in items]
        writes = [o for o, _ in items]

        def fn(e, items=items, ident=ident):
            ins = None
            for o, i in items:
                k = i.shape[0]
                ins = e.transpose(o, i, ident[:k, :k])
            return ins

        self.op("pe", fn, reads, writes)

    def tr3(self, items):
        reads = [i for _, i, _ in items] + [items[0][2]]
        writes = [o for o, _, _ in items]

        def fn(e, items=items):
            ins = None
            for o, i, idn in items:
                ins = e.transpose(o, i, idn)
            return ins

        self.op("pe", fn, reads, writes)

    def dma(self, eng, out, in_, key, slow=False):
        def fn(e, out=out, in_=in_):
            if slow:
                return e.dma_start(out=out, in_=in_, allow_slow_non_contiguous=True)
            return e.dma_start(out=out, in_=in_)

        self.op(eng, fn, [in_], [out], dma_key=key)

    def memset(self, eng, ap, val):
        def fn(e, ap=ap, val=val):
            return e.memset(ap, val)

        self.op(eng, fn, [], [ap])

    def act(self, out, in_, func, **kw):
        self.call("act", "activation", out, in_=in_, func=func, **kw)

    def tt(self, out, in0, in1, op, eng="dve"):
        self.call(eng, "tensor_tensor", out, in0=in0, in1=in1, op=op)

    def ts(self, out, in0, s1, s2, op0, op1=None, eng="dve"):
        if op1 is None:
            self.call(eng, "tensor_scalar", out, in0=in0, scalar1=s1, scalar2=None, op0=op0)
        else:
            self.call(eng, "tensor_scalar", out, in0=in0, scalar1=s1, scalar2=s2, op0=op0, op1=op1)

    def stt(self, out, in0, scalar, in1, op0, op1):
        self.call("dve", "scalar_tensor_tensor", out, in0=in0, scalar=scalar, in1=in1, op0=op0, op1=op1)

    def copy(self, out, in_, eng="dve"):
        if eng == "act":
            self.call("act", "copy", out, in_=in_)
        else:
            self.call(eng, "tensor_copy", out, in_=in_)

    def emit(self):
        nc = self.nc
        final = [(self.semobj[s], self.cnt[s]) for s in self.semobj if s.startswith("d_") and self.cnt[s] > 0]
        ops = self.ops

        def run(name, e):
            for waits, fn, inc, post in ops[name]:
                for s, v in waits:
                    e.wait_ge(s, v)
                ins = fn(e)
                ins.then_inc(inc[0], inc[1])
                if post is not None:
                    post(e)

        with nc.Block() as block:

            @block.tensor
            def _(e):
                run("pe", e)

            @block.scalar
            def _(e):
                run("act", e)

            @block.vector
            def _(e):
                run("dve", e)

            @block.gpsimd
            def _(e):
                run("pool", e)

            @block.sync
            def _(e):
                run("sp", e)
                for s, v in final:
                    e.wait_ge(s, v)


def bc(ap, pos, n):
    pairs = [list(p) for p in ap.ap]
    pairs.insert(pos, [0, n])
    return bass.AP(ap.tensor, ap.offset, pairs)


def make_plan(Tp):
    seqs = [[NMETA] + [64] * ((Tp - NMETA) // 64), [64], [64]]
    blocks = []
    cur = []
    cur_n = 0
    tok = 0
    for s, chs in enumerate(seqs):
        for ci, L in enumerate(chs):
            if cur_n + L > MAXB:
                blocks.append(cur)
                cur = []
                cur_n = 0
            cur.append(dict(seq=s, L=L, off=cur_n, first=(ci == 0), last=(ci == len(chs) - 1), gtok=tok, ci=ci))
            cur_n += L
            tok += L
    blocks.append(cur)
    out = []
    for b in blocks:
        segs = []
        for c in b:
            if segs and segs[-1]["seq"] == c["seq"]:
                segs[-1]["chunks"].append(c)
                segs[-1]["len"] += c["L"]
                segs[-1]["last"] = c["last"]
            else:
                segs.append(dict(seq=c["seq"], off=c["off"], len=c["L"], chunks=[c], first=c["first"], last=c["last"]))
        out.append(dict(Tb=sum(c["L"] for c in b), tok0=b[0]["gtok"], segs=segs, chunks=b))
    return out, tok


C128 = {}
_o = 0
for _n, _w in [("norm_g", 32), ("fin_g", 8), ("a_wdw", 8 * 31), ("a_bdw", 8), ("a_lng", 8), ("a_lnb", 8),
               ("b_wc", 40), ("b_bc", 10), ("b_brg", 10), ("b_big", 10), ("b_lam", 10), ("c_mu", 48), ("d_gng", 16),
               ("c_w0", 8), ("c_a0", 8), ("c_kk", 8), ("c_ka", 8), ("c_rk", 8), ("c_gng", 8), ("c_gnb", 8)]:
    C128[_n] = (_o, _w)
    _o += _w
NC128 = _o
C64 = {}
_o = 0
for _n, _w in [("w0", 16), ("a0", 16), ("kk", 16), ("ka", 16), ("rk", 16), ("gng", 16), ("gnb", 16)]:
    C64[_n] = (_o, _w)
    _o += _w
NC64 = _o


def weight_units():
    U = []

    def colslab(src, c0, nc_, nk=8):
        return (src, 128, nk, nc_, lambda w, c0=c0, nc_=nc_: w.rearrange("(k p) n -> p k n", p=128)[:, :, c0:c0 + nc_])

    for j in range(4):
        U.append(("A_in%d" % j, [colslab("a_w_in", 1024 + 256 * j, 256), colslab("a_w_in", 256 * j, 256)]))
    for j in range(2):
        U.append(("A_g%d" % j, [colslab("a_w_in", 2048 + 512 * j, 512)]))
    for j in range(2):
        U.append(("A_o%d" % j, [colslab("a_w_out", 512 * j, 512)]))
    for j in range(5):
        U.append(("B_in%d" % j, [colslab("b_w_in", 512 * j, 512)]))
    gv = lambda w: w.rearrange("n k j -> k n j")
    U.append(("B_gt", [("b_w_rg", 128, 10, 128, gv), ("b_w_ig", 128, 10, 128, gv)]))
    for j, (c0, n_) in enumerate([(0, 384), (384, 384), (768, 256)]):
        U.append(("B_o%d" % j, [colslab("b_w_out", c0, n_, nk=10)]))
    U.append(("C_lora", [colslab("c_w1", 0, 64), colslab("c_a1", 0, 64),
                         ("c_w2", 64, 1, 1024, lambda w: w.rearrange("(o p) n -> p o n", o=1)),
                         ("c_a2", 64, 1, 1024, lambda w: w.rearrange("(o p) n -> p o n", o=1))]))
    for s, nm in enumerate(["r", "k", "v", "g"]):
        for j in range(2):
            U.append(("C_%s%d" % (nm, j), [colslab("c_w_in", 1024 * s + 512 * j, 512)]))
    for j in range(2):
        U.append(("C_o%d" % j, [colslab("c_w_out", 512 * j, 512)]))
    for j in range(12):
        U.append(("D_in%d" % j, [colslab("d_w_in", 512 * j, 512)]))
    for j in range(4):
        U.append(("D_o%d" % j, [colslab("d_w_out", 256 * j, 256, nk=16)]))
    return U


WSHAPES = {"a_w_in": (1024, 3072), "a_w_out": (1024, 1024), "b_w_in": (1024, 2560), "b_w_rg": (10, 128, 128),
           "b_w_ig": (10, 128, 128), "b_w_out": (1280, 1024), "c_w_in": (1024, 4096), "c_w1": (1024, 64),
           "c_w2": (64, 1024), "c_a1": (1024, 64), "c_a2": (64, 1024), "c_w_out": (1024, 1024),
           "d_w_in": (1024, 6144), "d_w_out": (2048, 1024)}


def build(Tp, NL=4):
    plan, NT = make_plan(Tp)
    nc = bass.Bass("TRN2", target_bir_lowering=False)
    es = ExitStack()
    es.enter_context(nc.allow_low_precision("bf16 matmul operands, fp32 accumulation"))
    P = Prog(nc, es)

    def din(name, shape, dt=F32):
        return nc.dram_tensor(name, list(shape), dt, kind="ExternalInput").ap()

    def dout(name, shape):
        return nc.dram_tensor(name, list(shape), F32, kind="ExternalOutput").ap()

    xin = din("xin", (D, NT))
    cst128_d = din("cst128", (128, NC128))
    ktab_d = din("ktab", (128, KTW))
    rope_d = din("rope", (128, 6, NT))
    st_ca = din("st_ca", (3, D, 30))
    st_cb = din("st_cb", (3, DRNN, 3))
    st_lru = din("st_lru", (3, DRNN))
    st_sh = din("st_sh", (3, D))
    st_wkv = din("st_wkv", (3, 128, 8, 64))
    st_ret = din("st_ret", (3, 1024, 512))
    wd = {n: din(n, s) for n, s in WSHAPES.items()}
    yT = dout("yT", (D, NT))
    o_ca = dout("o_ca", (3, D, 30))
    o_cb = dout("o_cb", (3, DRNN, 3))
    o_lru = dout("o_lru", (3, DRNN))
    o_sh = dout("o_sh", (3, D))
    o_wkv = dout("o_wkv", (3, 128, 8, 64))
    o_ret = dout("o_ret", (3, 1024, 512))
    units = weight_units()
    NU = len(units)
    wscr = nc.dram_tensor("wscr", [NU, 128, SLOT], BF16, kind="Internal").ap()
    P.tracked_dram.add("wscr")

    def sb(name, shape, dt=F32):
        return es.enter_context(nc.sbuf_tensor("sb_" + name, list(shape), dt))

    psall = es.enter_context(nc.psum_tensor("psall", [128, 4096], F32))
    psall_bf = psall.bitcast(BF16)
    arena = sb("arena", [128, ARENA_BYTES // 4])
    arena_bf = arena.bitcast(BF16)
    xT = sb("xT", [128, 8, MAXB])
    cst = sb("cst", [128, NC128])
    ktab = sb("ktab_sb", [128, KTW])
    ktb = sb("ktb", [128, 640], BF16)
    cder = sb("cder", [128, 64])
    haloA = sb("haloA", [128, 8, 30])
    haloB = sb("haloB", [128, 10, 3])
    hstB = sb("hstB", [128, 10])
    shC = sb("shC", [128, 8])
    Mst = sb("Mst", [128, 8, 64])
    Mbf = sb("Mbf", [128, 8, 64], BF16)
    Sst = sb("Sst", [128, 8, 512])
    Sbf = sb("Sbf", [128, 8, 512], BF16)
    ring = [sb("ring%d" % i, [128, SLOT], BF16) for i in range(NSLOT)]

    NBANK = 7 if FILLER > 0 else 8
    bankctr = [0]

    def bank():
        i = bankctr[0] % NBANK
        bankctr[0] += 1
        return psall[:, 512 * i:512 * i + 512], psall_bf[:, 1024 * i:1024 * i + 1024]

    pairctr = [0]

    def bank2():
        i = (pairctr[0] % (NBANK // 2)) * 2
        pairctr[0] += 1
        return psall[:, 512 * i:512 * i + 1024], psall_bf[:, 1024 * i:1024 * i + 2048]

    aoff = [0]

    def areset():
        aoff[0] = 0

    def aal(shape, dt=F32):
        n = 1
        for s in shape[1:]:
            n *= s
        nb = n * _esz(dt)
        nb = (nb + 31) // 32 * 32
        o = aoff[0]
        aoff[0] += nb
        assert aoff[0] <= ARENA_BYTES, ("arena overflow", aoff[0])
        base = arena if dt == F32 else arena_bf
        e0 = o // _esz(dt)
        ap = base[:shape[0], e0:e0 + n]
        if len(shape) == 3:
            ap = ap.rearrange("p (a b) -> p a b", a=shape[1])
        return ap

    def c128(name, i0=0, n=None):
        o, w = C128[name]
        n = w - i0 if n is None else n
        return cst[:, o + i0:o + i0 + n]

    ident_bf = ktb[:, 0:128]
    onesD = ktb[:, 128:256]
    ones512 = ktb[:, 256:384]
    bd64 = ktb[:, 384:512]
    bd1 = ktb[:, 512:640]
    triS = ktab[:, 640:704]
    triI = ktab[:, 704:768]
    triL = ktab[:, 768:832]
    identf = ktab[:, 832:896]
    dmaskT = ktab[:64, 896:1152]
    kdec64 = ktab[:64, 1152:1156]
    kdec16 = ktab[:64, 1156:1160]
    onesf = ktab[:, 1160:1224]

    if FILLER > 0:
        dummy_ps = psall[:, 512 * 7:512 * 8]

        def _filler(e):
            for _ in range(FILLER):
                e.matmul(dummy_ps, ident_bf, ktb[:, 0:512], start=True, stop=True)

        P.pe_filler = _filler

    P.dma("pool", cst[:, :], cst128_d, "cst")
    P.dma("pool", ktab[:, :], ktab_d, "cst")
    P.copy(ktb[:, :], ktab[:, 0:640])
    P.act(cder[:, 0:10], c128("b_lam"), AF.Exp, scale=-1.0)
    xz = cder[:, 0:10]
    zz = cder[:, 32:42]
    z2 = cder[:, 42:52]
    P.ts(zz, xz, 2.0, None, ALU.add)
    P.call("dve", "reciprocal", zz, in_=zz)
    P.tt(zz, zz, xz, ALU.mult)
    P.tt(z2, zz, zz, ALU.mult)
    P.ts(xz, z2, 1.0 / 9.0, 1.0 / 7.0, ALU.mult, ALU.add)
    P.tt(xz, xz, z2, ALU.mult)
    P.ts(xz, xz, 1.0 / 5.0, None, ALU.add)
    P.tt(xz, xz, z2, ALU.mult)
    P.ts(xz, xz, 1.0 / 3.0, None, ALU.add)
    P.tt(xz, xz, z2, ALU.mult)
    P.ts(xz, xz, 1.0, None, ALU.add)
    P.tt(xz, xz, zz, ALU.mult)
    P.ts(cder[:, 0:10], xz, -16.0, None, ALU.mult)
    P.ts(cder[:, 16:24], c128("c_ka"), -1.0, 1.0, ALU.mult, ALU.add)

    areset()
    NSTG = 4
    stg32 = [aal([128, SLOT]) for _ in range(NSTG)]
    stgbf = [aal([128, SLOT], BF16) for _ in range(NSTG)]
    cast_engs = ["dve", "act", "dve", "act", "pool"]
    for u, (uname, pieces) in enumerate(units):
        par = u % NSTG
        off = 0
        for pi, (src, Pp, a, b, view) in enumerate(pieces):
            n = a * b
            dst32 = stg32[par][:Pp, off:off + n].rearrange("p (a b) -> p a b", a=a)
            P.dma("sp" if pi % 2 == 0 else "pool", dst32, view(wd[src]), "stg%d" % par)
            P.copy(stgbf[par][:Pp, off:off + n], stg32[par][:Pp, off:off + n], eng=cast_engs[(u + pi) % 5])
            P.dma("sp" if pi % 2 == 0 else "pool", wscr[u, :Pp, off:off + n], stgbf[par][:Pp, off:off + n], "stgo%d" % par)
            off += n
        assert off <= SLOT

    uidx = {n: i for i, (n, _) in enumerate(units)}
    layer_units = {L: [n for n, _ in units if n.startswith(L + "_")] for L in "ABCD"}
    stream = []
    for bi in range(len(plan)):
        for L in "ABCD"[:NL]:
            for n in layer_units[L]:
                stream.append(n)
    issued = [0]
    sidx = [0]

    def issue_upto(k):
        while issued[0] < min(k, len(stream)):
            i = issued[0]
            u = uidx[stream[i]]
            slot = ring[i % NSLOT]
            off = 0
            for pi, (src, Pp, a, b, view) in enumerate(units[u][1]):
                n = a * b
                P.dma("sp", slot[:Pp, off:off + n], wscr[u, :Pp, off:off + n], "ring%d" % (i % NSLOT))
                off += n
            issued[0] += 1

    def wget(name):
        i = sidx[0]
        assert stream[i] == name, (stream[i], name)
        issue_upto(i + NSLOT)
        sidx[0] += 1
        slot = ring[i % NSLOT]
        views = []
        off = 0
        for (src, Pp, a, b, view) in units[uidx[name]][1]:
            n = a * b
            views.append(slot[:Pp, off:off + n].rearrange("p (a b) -> p a b", a=a))
            off += n
        return views

    def rsqrt(out, in_, eps):
        P.act(out, in_, AF.Ln, bias=float(eps))
        P.act(out, out, AF.Exp, scale=-0.5)

    def rmsnorm(Tb, gname, gi0, dst, dstf=None):
        sq = aal([128, 8, MAXB], BF16)
        rstd = aal([128, MAXB])
        P.act(sq[:, :, :Tb], xT[:, :, :Tb], AF.Square)
        ps, _ = bank()
        P.mm([(ps[:, :Tb], [(onesD, sq[:, k, :Tb]) for k in range(8)])])
        rsqrt(rstd[:, :Tb], ps[:, :Tb], NORM_EPS)
        for k in range(8):
            g = c128(gname, gi0 + k, 1)
            if dst is not None:
                P.stt(dst[:, k, :Tb], xT[:, k, :Tb], g, rstd[:, :Tb], ALU.mult, ALU.mult)
            if dstf is not None:
                P.stt(dstf[:, k, :Tb], xT[:, k, :Tb], g, rstd[:, :Tb], ALU.mult, ALU.mult)

    def outproj(Tb, unit_names, ncols_per_unit, yK, nk):
        m = 0
        for un, ncu in zip(unit_names, ncols_per_unit):
            (w,) = wget(un)
            for j in range(ncu // 128):
                ps, _ = bank()
                P.mm([(ps[:, :Tb], [(w[:, k, 128 * j:128 * j + 128], yK(k)) for k in range(nk)])])
                P.tt(xT[:, m, :Tb], xT[:, m, :Tb], ps[:, :Tb], ALU.add)
                m += 1
        assert m == 8

    def layerA(blk):
        Tb = blk["Tb"]
        segs = blk["segs"]
        areset()
        hT = aal([128, 8, MAXB], BF16)
        WA = 30 * len(segs) + Tb
        glu = aal([128, 8, WA])
        gate = aal([128, 8, MAXB], BF16)
        sig = [aal([128, MAXB]) for _ in range(2)]
        acc = aal([128, 8, MAXB])
        rmsnorm(Tb, "norm_g", 0, hT)
        scol = [30 * (i + 1) + s["off"] for i, s in enumerate(segs)]
        for i, s in enumerate(segs):
            h = glu[:, :, scol[i] - 30:scol[i]]
            if s["first"]:
                P.dma("pool", h, st_ca[s["seq"]].rearrange("(k p) t -> p k t", p=128), "stin")
            else:
                P.copy(h, haloA[:, :, :], eng="pool")
        for j in range(4):
            wb, wa = wget("A_in%d" % j)
            for jj in range(2):
                m = 2 * j + jj
                ps, _ = bank()
                P.mm([(ps[:, :Tb], [(wb[:, k, 128 * jj:128 * jj + 128], hT[:, k, :Tb]) for k in range(8)])])
                sg = sig[m % 2]
                P.act(sg[:, :Tb], ps[:, :Tb], AF.Sigmoid)
                ps2, _ = bank()
                P.mm([(ps2[:, :Tb], [(wa[:, k, 128 * jj:128 * jj + 128], hT[:, k, :Tb]) for k in range(8)])])
                for i, s in enumerate(segs):
                    P.tt(glu[:, m, scol[i]:scol[i] + s["len"]], ps2[:, s["off"]:s["off"] + s["len"]],
                         sg[:, s["off"]:s["off"] + s["len"]], ALU.mult)
        for j in range(2):
            (wg,) = wget("A_g%d" % j)
            for jj in range(4):
                m = 4 * j + jj
                ps, _ = bank()
                P.mm([(ps[:, :Tb], [(wg[:, k, 128 * jj:128 * jj + 128], hT[:, k, :Tb]) for k in range(8)])])
                P.act(gate[:, m, :Tb], ps[:, :Tb], AF.Silu)
        o_w, _ = C128["a_wdw"]
        ptmp = aal([128, 2, MAXB])
        for j in range(31):
            for m in range(8):
                for i, s in enumerate(segs):
                    c0 = scol[i] - 30
                    L = s["len"]
                    dst = acc[:, m, s["off"]:s["off"] + L]
                    src = glu[:, m, c0 + j:c0 + j + L]
                    wj = cst[:, o_w + m * 31 + j:o_w + m * 31 + j + 1]
                    if m < 8:
                        if j == 0:
                            P.ts(dst, src, wj, c128("a_bdw", m, 1), ALU.mult, ALU.add)
                        else:
                            P.stt(dst, src, wj, dst, ALU.mult, ALU.add)
                    else:
                        if j == 0:
                            P.ts(dst, src, wj, c128("a_bdw", m, 1), ALU.mult, ALU.add, eng="pool")
                        else:
                            tp_ = ptmp[:, m - 6, s["off"]:s["off"] + L]
                            P.ts(tp_, src, wj, None, ALU.mult, eng="pool")
                            P.tt(dst, dst, tp_, ALU.add, eng="pool")
        for i, s in enumerate(segs):
            last30 = glu[:, :, scol[i] + s["len"] - 30:scol[i] + s["len"]]
            if s["last"]:
                P.dma("pool", o_ca[s["seq"]].rearrange("(k p) t -> p k t", p=128), last30, "o_ca")
            else:
                P.copy(haloA[:, :, :], last30, eng="pool")
        cb = aal([128, 8, MAXB], BF16)
        csq = aal([128, 8, MAXB], BF16)
        P.copy(cb[:, :, :Tb], acc[:, :, :Tb], eng="act")
        P.act(csq[:, :, :Tb], acc[:, :, :Tb], AF.Square)
        psm, _ = bank()
        pss, _ = bank()
        P.mm([(psm[:, :Tb], [(onesD, cb[:, k, :Tb]) for k in range(8)])])
        P.mm([(pss[:, :Tb], [(onesD, csq[:, k, :Tb]) for k in range(8)])])
        mean = aal([128, MAXB])
        var = aal([128, MAXB])
        P.copy(mean[:, :Tb], psm[:, :Tb], eng="act")
        P.tt(var[:, :Tb], mean[:, :Tb], mean[:, :Tb], ALU.mult)
        P.tt(var[:, :Tb], pss[:, :Tb], var[:, :Tb], ALU.subtract)
        rsqrt(var[:, :Tb], var[:, :Tb], LN_EPS)
        P.stt(mean[:, :Tb], mean[:, :Tb], -1.0, var[:, :Tb], ALU.mult, ALU.mult)
        P.tt(acc[:, :, :Tb], acc[:, :, :Tb], bc(var[:, :Tb], 1, 8), ALU.mult)
        P.tt(acc[:, :, :Tb], acc[:, :, :Tb], bc(mean[:, :Tb], 1, 8), ALU.add)
        for m in range(8):
            P.act(acc[:, m, :Tb], acc[:, m, :Tb], AF.Silu, bias=c128("a_lnb", m, 1), scale=c128("a_lng", m, 1))
        yb = hT
        P.tt(yb[:, :, :Tb], acc[:, :, :Tb], gate[:, :, :Tb], ALU.mult)
        outproj(Tb, ["A_o0", "A_o1"], [512, 512], lambda k: yb[:, k, :Tb], 8)

    def layerB(blk):
        Tb = blk["Tb"]
        segs = blk["segs"]
        areset()
        hT = aal([128, 8, MAXB], BF16)
        WB = 3 * len(segs) + Tb
        xb = aal([128, 10, WB])
        gs = aal([128, 10, MAXB], BF16)
        xc = aal([128, 10, MAXB])
        xcb = aal([128, 10, MAXB], BF16)
        rmsnorm(Tb, "norm_g", 8, hT)
        scol = [3 * (i + 1) + s["off"] for i, s in enumerate(segs)]
        for i, s in enumerate(segs):
            h = xb[:, :, scol[i] - 3:scol[i]]
            if s["first"]:
                P.dma("pool", h, st_cb[s["seq"]].rearrange("(k p) t -> p k t", p=128), "stin")
            else:
                P.copy(h, haloB[:, :, :], eng="pool")
        for j in range(5):
            (w,) = wget("B_in%d" % j)
            for jj in range(4):
                m = 4 * j + jj
                ps, _ = bank()
                P.mm([(ps[:, :Tb], [(w[:, k, 128 * jj:128 * jj + 128], hT[:, k, :Tb]) for k in range(8)])])
                if m < 10:
                    for i, s in enumerate(segs):
                        P.copy(xb[:, m, scol[i]:scol[i] + s["len"]], ps[:, s["off"]:s["off"] + s["len"]], eng="act")
                else:
                    P.act(gs[:, m - 10, :Tb], ps[:, :Tb], AF.Silu)
        o_w, _ = C128["b_wc"]
        for j in range(4):
            for m in range(10):
                for i, s in enumerate(segs):
                    c0 = scol[i] - 3
                    L = s["len"]
                    dst = xc[:, m, s["off"]:s["off"] + L]
                    wj = cst[:, o_w + m * 4 + j:o_w + m * 4 + j + 1]
                    if j == 0:
                        P.ts(dst, xb[:, m, c0:c0 + L], wj, c128("b_bc", m, 1), ALU.mult, ALU.add)
                    else:
                        P.stt(dst, xb[:, m, c0 + j:c0 + j + L], wj, dst, ALU.mult, ALU.add)
        P.copy(xcb[:, :, :Tb], xc[:, :, :Tb], eng="act")
        for i, s in enumerate(segs):
            last3 = xb[:, :, scol[i] + s["len"] - 3:scol[i] + s["len"]]
            if s["last"]:
                P.dma("pool", o_cb[s["seq"]].rearrange("(k p) t -> p k t", p=128), last3, "o_cb", slow=True)
            else:
                P.copy(haloB[:, :, :], last3, eng="pool")
        R = aal([128, 10, MAXB])
        I = aal([128, 10, MAXB])
        A1 = aal([128, 10, MAXB])
        wrg, wig = wget("B_gt")
        for n in range(10):
            ps, _ = bank()
            P.mm([(ps[:, :Tb], [(wrg[:, n, :], xcb[:, n, :Tb])])])
            P.act(R[:, n, :Tb], ps[:, :Tb], AF.Sigmoid, bias=c128("b_brg", n, 1))
            ps2, _ = bank()
            P.mm([(ps2[:, :Tb], [(wig[:, n, :], xcb[:, n, :Tb])])])
            P.act(I[:, n, :Tb], ps2[:, :Tb], AF.Sigmoid, bias=c128("b_big", n, 1))
        for n in range(10):
            P.ts(R[:, n, :Tb], R[:, n, :Tb], cder[:, n:n + 1], None, ALU.mult)
        P.act(A1[:, :, :Tb], R[:, :, :Tb], AF.Exp)
        P.act(R[:, :, :Tb], R[:, :, :Tb], AF.Exp, scale=2.0)
        P.act(R[:, :, :Tb], R[:, :, :Tb], AF.Sqrt, scale=-1.0, bias=1.0)
        P.tt(I[:, :, :Tb], I[:, :, :Tb], R[:, :, :Tb], ALU.mult)
        P.tt(I[:, :, :Tb], I[:, :, :Tb], xc[:, :, :Tb], ALU.mult)
        for i, s in enumerate(segs):
            if s["first"]:
                P.dma("pool", hstB[:, :], st_lru[s["seq"]].rearrange("(k p) -> p k", p=128), "stin", slow=True)
            o, L = s["off"], s["len"]
            for n in range(10):
                P.call("dve", "tensor_tensor_scan", R[:, n, o:o + L], data0=A1[:, n, o:o + L], data1=I[:, n, o:o + L],
                       initial=hstB[:, n:n + 1], op0=ALU.mult, op1=ALU.add)
            P.copy(hstB[:, :], R[:, :, o + L - 1])
            if s["last"]:
                P.dma("pool", o_lru[s["seq"]].rearrange("(k p) -> p k", p=128), hstB[:, :], "o_lru", slow=True)
        yb = xcb
        P.tt(yb[:, :, :Tb], R[:, :, :Tb], gs[:, :, :Tb], ALU.mult)
        outproj(Tb, ["B_o0", "B_o1", "B_o2"], [384, 384, 256], lambda k: yb[:, k, :Tb], 10)

    def layerD(blk):
        Tb = blk["Tb"]
        segs = blk["segs"]
        tok0 = blk["tok0"]
        areset()
        hT = aal([128, 8, MAXB], BF16)
        qk = aal([128, 16, MAXB])
        vT = aal([128, 16, MAXB], BF16)
        gs = aal([128, 16, MAXB], BF16)
        rope = aal([128, 6, MAXB])
        rmsnorm(Tb, "norm_g", 24, hT)
        P.dma("pool", rope[:, :, :Tb], rope_d[:, :, tok0:tok0 + Tb], "rope")
        for j in range(12):
            (w,) = wget("D_in%d" % j)
            for jj in range(4):
                m = 4 * j + jj
                ps, _ = bank()
                P.mm([(ps[:, :Tb], [(w[:, k, 128 * jj:128 * jj + 128], hT[:, k, :Tb]) for k in range(8)])])
                if m < 16:
                    P.copy(qk[:, m, :Tb], ps[:, :Tb], eng="act")
                elif m < 32:
                    P.copy(vT[:, m - 16, :Tb], ps[:, :Tb], eng="act")
                else:
                    P.act(gs[:, m - 32, :Tb], ps[:, :Tb], AF.Silu)
        qr = aal([128, 16, MAXB], BF16)
        qd = aal([128, 8, MAXB], BF16)
        t1 = aal([128, 8, MAXB])
        t2 = aal([128, 8, MAXB])
        x1 = bass.AP(qk.tensor, qk.offset, [list(qk.ap[0]), [2 * MAXB, 8], [1, Tb]])
        x2 = bass.AP(qk.tensor, qk.offset + MAXB, [list(qk.ap[0]), [2 * MAXB, 8], [1, Tb]])
        o1 = bass.AP(qr.tensor, qr.offset, [list(qr.ap[0]), [2 * MAXB, 8], [1, Tb]])
        o2 = bass.AP(qr.tensor, qr.offset + MAXB, [list(qr.ap[0]), [2 * MAXB, 8], [1, Tb]])
        cosb = bc(rope[:, 0, :Tb], 1, 8)
        sinb = bc(rope[:, 1, :Tb], 1, 8)
        P.tt(t1[:, :, :Tb], x1, cosb, ALU.mult)
        P.tt(t2[:, :, :Tb], x2, sinb, ALU.mult)
        P.tt(o1, t1[:, :, :Tb], t2[:, :, :Tb], ALU.subtract)
        P.tt(t1[:, :, :Tb], x1, sinb, ALU.mult)
        P.tt(t2[:, :, :Tb], x2, cosb, ALU.mult)
        P.tt(o2, t1[:, :, :Tb], t2[:, :, :Tb], ALU.add)
        for h in range(4):
            P.tt(qd[:, 2 * h:2 * h + 2, :Tb], qr[:, 2 * h:2 * h + 2, :Tb], bc(rope[:, 2 + h, :Tb], 1, 2), ALU.mult)
        oT = aal([128, 16, MAXB])
        vtok = [aal([64, 2048], BF16) for _ in range(2)]
        kdt = [aal([64, 1024], BF16) for _ in range(2)]
        scm = [aal([64, 256], BF16) for _ in range(2)]
        dchunks = blk["chunks"]

        def d_pre(ci):
            c = dchunks[ci]
            o, L = c["off"], c["L"]
            vt = vtok[ci % 2]
            kt = kdt[ci % 2]
            sm = scm[ci % 2]
            _, pb = bank2()
            P.tr([(pb[:L, 128 * e:128 * e + 128], vT[:, e, o:o + L]) for e in range(16)], ident_bf)
            P.copy(vt[:L, :], pb[:L, 0:2048], eng="act")
            _, pk = bank()
            P.tr([(pk[:L, 128 * e:128 * e + 128], qr[:, 8 + e, o:o + L]) for e in range(8)], ident_bf)
            kdc = kdec64 if L == 64 else kdec16
            for h in range(4):
                P.ts(kt[:L, 256 * h:256 * h + 256], pk[:L, 256 * h:256 * h + 256], kdc[:L, h:h + 1], None, ALU.mult)
            ps, _ = bank()
            P.mm([(ps[:L, 64 * h:64 * h + L], [(qr[:, 8 + 2 * h + dc, o:o + L], qr[:, 2 * h + dc, o:o + L]) for dc in range(2)])
                  for h in range(4)])
            P.tt(sm[:L, :].rearrange("p (h n) -> p h n", h=4)[:, :, :L], ps[:L, 0:256].rearrange("p (h n) -> p h n", h=4)[:, :, :L],
                 dmaskT[:L, :].rearrange("p (h n) -> p h n", h=4)[:, :, :L], ALU.mult)

        def d_main(ci):
            c = dchunks[ci]
            o, L, sq_ = c["off"], c["L"], c["seq"]
            vt = vtok[ci % 2]
            kt = kdt[ci % 2]
            sm = scm[ci % 2]
            if c["first"]:
                P.dma("pool", Sst[:, :, :], st_ret[sq_].rearrange("(c p) e -> p c e", p=128), "stin")
                P.copy(Sbf[:, :, :], Sst[:, :, :], eng="act")
            po, _ = bank2()
            grp = []
            for h in range(4):
                for ec in range(4):
                    col = (4 * h + ec) * 64
                    pairs = [(vt[:L, 512 * h + 128 * ec:512 * h + 128 * ec + 128], sm[:L, 64 * h:64 * h + L])]
                    for dc in range(2):
                        pairs.append((Sbf[:, 2 * h + dc, 128 * ec:128 * ec + 128], qd[:, 2 * h + dc, o:o + L]))
                    grp.append((po[:, col:col + L], pairs))
            P.mm(grp)
            P.copy(oT[:, :, o:o + L], po[:, 0:1024].rearrange("p (a n) -> p a n", a=16)[:, :, :L], eng="act")
            for h in range(4):
                gL = float((1.0 - 2.0 ** (-5 - h)) ** L)
                for dc in range(2):
                    pS, _ = bank()
                    P.mm([(pS[:, :], [(kt[:L, 256 * h + 128 * dc:256 * h + 128 * dc + 128], vt[:L, 512 * h:512 * h + 512])])])
                    P.stt(Sst[:, 2 * h + dc, :], Sst[:, 2 * h + dc, :], gL, pS[:, :], ALU.mult, ALU.add)
            P.copy(Sbf[:, :, :], Sst[:, :, :], eng="act")
            if c["last"]:
                P.dma("pool", o_ret[sq_].rearrange("(c p) e -> p c e", p=128), Sst[:, :, :], "o_ret")

        d_pre(0)
        for ci in range(len(dchunks)):
            if ci + 1 < len(dchunks):
                d_pre(ci + 1)
            d_main(ci)
        osq = qr
        P.act(osq[:, :, :Tb], oT[:, :, :Tb], AF.Square)
        rs = t1
        for h in range(4):
            ps, _ = bank()
            P.mm([(ps[:, :Tb], [(ones512, osq[:, 4 * h + ec, :Tb]) for ec in range(4)])])
            rsqrt(rs[:, h, :Tb], ps[:, :Tb], NORM_EPS)
        for c_ in range(16):
            P.stt(oT[:, c_, :Tb], oT[:, c_, :Tb], c128("d_gng", c_, 1), rs[:, c_ // 4, :Tb], ALU.mult, ALU.mult)
        yb = vT
        P.tt(yb[:, :, :Tb], oT[:, :, :Tb], gs[:, :, :Tb], ALU.mult)
        outproj(Tb, ["D_o0", "D_o1", "D_o2", "D_o3"], [256] * 4, lambda k: yb[:, k, :Tb], 16)

    def layerC(blk):
        Tb = blk["Tb"]
        segs = blk["segs"]
        areset()
        hf = aal([128, 8, MAXB + 1])
        xx = aal([128, 8, MAXB])
        xm = [aal([128, 8, MAXB], BF16) for _ in range(2)]
        lw = aal([64, MAXB], BF16)
        la = aal([64, MAXB], BF16)
        rmsnorm(Tb, "norm_g", 16, None, dstf=hf[:, :, 1:])
        prefix_end = aoff[0]
        s0 = segs[0]
        if s0["first"]:
            P.dma("pool", hf[:, :, 0], st_sh[s0["seq"]].rearrange("(k p) -> p k", p=128), "stin", slow=True)
        else:
            P.copy(hf[:, :, 0], shC[:, :], eng="pool")
        P.tt(xx[:, :, :Tb], hf[:, :, 0:Tb], hf[:, :, 1:Tb + 1], ALU.subtract)
        for i, s in enumerate(segs):
            if i > 0:
                P.dma("pool", shC[:, :], st_sh[s["seq"]].rearrange("(k p) -> p k", p=128), "stin", slow=True)
                P.tt(xx[:, :, s["off"]], shC[:, :], hf[:, :, 1 + s["off"]], ALU.subtract)
            if s["last"]:
                P.dma("pool", o_sh[s["seq"]].rearrange("(k p) -> p k", p=128), hf[:, :, s["off"] + s["len"]], "o_sh", slow=True)
        if not segs[-1]["last"]:
            P.copy(shC[:, :], hf[:, :, Tb], eng="pool")

        def drain(names):
            for n_ in names:
                wget(n_)

        C_ALL = ["C_lora"] + ["C_%s%d" % (nm, j) for nm in "rkvg" for j in range(2)] + ["C_o0", "C_o1"]
        if CSTOP == 0:
            drain(C_ALL)
            return
        o_mu, _ = C128["c_mu"]

        def mkxm(si):
            t = xm[si % 2]
            for k in range(8):
                P.stt(t[:, k, :Tb], xx[:, k, :Tb], cst[:, o_mu + si * 8 + k:o_mu + si * 8 + k + 1], hf[:, k, 1:Tb + 1], ALU.mult, ALU.add)
            return t

        rr = aal([128, 8, MAXB])
        vb = aal([128, 8, MAXB], BF16)
        gsl = aal([128, 8, MAXB], BF16)
        bon = aal([128, 8, MAXB], BF16)
        bh = aal([128, 8, MAXB], BF16)
        kh_ = aal([128, 8, MAXB], BF16)
        at = aal([128, 8, MAXB], BF16)
        rt = aal([128, 8, MAXB], BF16)
        pl = aal([128, 8, 8])
        dead_start = aoff[0]
        kk_ = aal([128, 8, MAXB])
        lwd = aal([128, 8, MAXB])
        av = aal([128, 8, MAXB])
        tA = aal([128, 8, MAXB])
        tB = aal([128, 8, MAXB])
        ex = aal([128, 8, MAXB])
        tbf = aal([128, 8, MAXB], BF16)
        dead_end = aoff[0]
        w1, a1, w2, a2 = wget("C_lora")
        x4 = mkxm(4)
        ps, _ = bank()
        P.mm([(ps[:64, :Tb], [(w1[:, k, :], x4[:, k, :Tb]) for k in range(8)])])
        P.act(lw[:, :Tb], ps[:64, :Tb], AF.Tanh)
        x5 = mkxm(5)
        ps, _ = bank()
        P.mm([(ps[:64, :Tb], [(a1[:, k, :], x5[:, k, :Tb]) for k in range(8)])])
        P.copy(la[:, :Tb], ps[:64, :Tb], eng="act")
        for p in range(8):
            ps, _ = bank()
            P.mm([(ps[:, :Tb], [(w2[:, 0, 128 * p:128 * p + 128], lw[:, :Tb])])])
            P.act(lwd[:, p, :Tb], ps[:, :Tb], AF.Sigmoid, bias=c128("c_w0", p, 1))
            ps2, _ = bank()
            P.mm([(ps2[:, :Tb], [(a2[:, 0, 128 * p:128 * p + 128], la[:, :Tb])])])
            P.act(av[:, p, :Tb], ps2[:, :Tb], AF.Sigmoid, bias=c128("c_a0", p, 1))
        P.ts(lwd[:, :, :Tb], lwd[:, :, :Tb], -float(np.exp(-0.5)), None, ALU.mult)
        if CSTOP == 1:
            drain(C_ALL[1:])
            return

        def proj(si, nm, evac):
            x = mkxm(si)
            for j in range(2):
                (w,) = wget("C_%s%d" % (nm, j))
                for jj in range(4):
                    p = 4 * j + jj
                    ps, _ = bank()
                    P.mm([(ps[:, :Tb], [(w[:, k, 128 * jj:128 * jj + 128], x[:, k, :Tb]) for k in range(8)])])
                    evac(p, ps)

        proj(0, "r", lambda p, ps: P.copy(rr[:, p, :Tb], ps[:, :Tb], eng="act"))
        proj(1, "k", lambda p, ps: P.copy(kk_[:, p, :Tb], ps[:, :Tb], eng="act"))
        proj(2, "v", lambda p, ps: P.copy(vb[:, p, :Tb], ps[:, :Tb], eng="act"))
        proj(3, "g", lambda p, ps: P.act(gsl[:, p, :Tb], ps[:, :Tb], AF.Silu))

        if CSTOP == 2:
            drain(C_ALL[9:])
            return
        for p in range(8):
            P.ts(tA[:, p, :Tb], kk_[:, p, :Tb], c128("c_kk", p, 1), None, ALU.mult)
        P.act(tbf[:, :, :Tb], tA[:, :, :Tb], AF.Square)
        for p in range(8):
            ps, _ = bank()
            P.mm([(ps[:, :Tb], [(bd1, tbf[:, p, :Tb])])])
            P.ts(tB[:, p, :Tb], ps[:, :Tb], 1e-24, None, ALU.max)
            P.act(tB[:, p, :Tb], tB[:, p, :Tb], AF.Ln)
            P.act(tB[:, p, :Tb], tB[:, p, :Tb], AF.Exp, scale=-0.5)
        P.tt(tA[:, :, :Tb], tA[:, :, :Tb], tB[:, :, :Tb], ALU.mult)
        for p in range(8):
            P.ts(tB[:, p, :Tb], av[:, p, :Tb], c128("c_ka", p, 1), cder[:, 16 + p:17 + p], ALU.mult, ALU.add)
        P.tt(kk_[:, :, :Tb], kk_[:, :, :Tb], tB[:, :, :Tb], ALU.mult)
        P.tt(tB[:, :, :Tb], rr[:, :, :Tb], kk_[:, :, :Tb], ALU.mult)
        for p in range(8):
            P.ts(tbf[:, p, :Tb], tB[:, p, :Tb], c128("c_rk", p, 1), None, ALU.mult)
        for p in range(8):
            ps, _ = bank()
            P.mm([(ps[:, :Tb], [(bd1, tbf[:, p, :Tb])])])
            P.tt(bon[:, p, :Tb], ps[:, :Tb], vb[:, p, :Tb], ALU.mult)
        for c in blk["chunks"]:
            o, L = c["off"], c["L"]
            for p in range(8):
                P.call("dve", "tensor_tensor_scan", tB[:, p, o:o + L], data0=onesf[:, :L], data1=lwd[:, p, o:o + L],
                       initial=0.0, op0=ALU.mult, op1=ALU.add)
        P.act(ex[:, :, :Tb], tB[:, :, :Tb], AF.Exp, scale=-1.0)
        P.tt(av[:, :, :Tb], av[:, :, :Tb], tA[:, :, :Tb], ALU.mult)
        P.tt(bh[:, :, :Tb], av[:, :, :Tb], ex[:, :, :Tb], ALU.mult)
        P.tt(kh_[:, :, :Tb], kk_[:, :, :Tb], ex[:, :, :Tb], ALU.mult)
        P.act(ex[:, :, :Tb], tB[:, :, :Tb], AF.Exp)
        P.tt(rt[:, :, :Tb], rr[:, :, :Tb], ex[:, :, :Tb], ALU.mult)
        for ci, c in enumerate(blk["chunks"]):
            P.copy(pl[:, :, ci], ex[:, :, c["off"] + c["L"] - 1], eng="pool")
        P.tt(tB[:, :, :Tb], tB[:, :, :Tb], lwd[:, :, :Tb], ALU.subtract)
        P.act(ex[:, :, :Tb], tB[:, :, :Tb], AF.Exp)
        P.stt(at[:, :, :Tb], tA[:, :, :Tb], -1.0, ex[:, :, :Tb], ALU.mult, ALU.mult)

        if CSTOP == 3:
            drain(C_ALL[9:])
            return
        end_all = aoff[0]
        oTc = rr
        chunks = blk["chunks"]
        aoff[0] = dead_start
        FtL = [[aal([128, 8, 64], BF16) for _ in range(6)] for _ in range(2)]
        assert aoff[0] <= dead_end, (aoff[0], dead_end)
        aoff[0] = 0
        tkB = [aal([128, 8, 64], BF16) for _ in range(2)]
        tkK = [aal([128, 8, 64], BF16) for _ in range(2)]
        tkV = [aal([128, 8, 64], BF16) for _ in range(2)]
        sArb = [aal([128, 8, 64], BF16) for _ in range(2)]
        sArk = [aal([128, 8, 64], BF16) for _ in range(2)]
        sAak = [aal([128, 8, 64], BF16) for _ in range(2)]
        sAab = aal([128, 8, 64], BF16)
        Nn = [aal([128, 8, 64], BF16) for _ in range(2)]
        Nt = [aal([128, 8, 64], BF16) for _ in range(2)]
        Xb = [aal([128, 8, 64], BF16) for _ in range(2)]
        assert aoff[0] <= prefix_end, (aoff[0], prefix_end)
        aoff[0] = end_all

        def HS(t3, p, h2, c0, c1):
            return t3[64 * h2:64 * h2 + 64, p, c0:c1]

        def headmm(rows, ncols, pairs_fn):
            banks = [bank()[0], bank()[0]]
            groups = []
            for h2 in range(2):
                for p in range(8):
                    groups.append((banks[h2][64 * h2:64 * h2 + rows, 64 * p:64 * p + ncols], pairs_fn(p, h2)))
            P.mm(groups)
            return banks

        def pview(banks, h2, rows, ncols):
            return banks[h2][64 * h2:64 * h2 + rows, 0:512].rearrange("p (a n) -> p a n", a=8)[:, :, :ncols]

        def RLf(t3, p, h2, L, c1=None):
            return t3[64 * h2:64 * h2 + L, p, :c1] if c1 is not None else t3[64 * h2:64 * h2 + L, p, :]

        def pre_steps(ci):
            c = chunks[ci]
            o, L = c["off"], c["L"]
            q = ci % 2
            nlev = 6 if L == 64 else 4
            steps = []

            def tstep(src, dstv):
                def f():
                    bk = [bank()[1], bank()[1]]
                    items = []
                    for h2 in range(2):
                        for p in range(8):
                            items.append((bk[h2][64 * h2:64 * h2 + L, 64 * p:64 * p + 64], src[64 * h2:64 * h2 + 64, p, o:o + L],
                                          ident_bf[64 * h2:64 * h2 + 64, 64 * h2:64 * h2 + 64]))
                    P.tr3(items)
                    for h2 in range(2):
                        P.copy(dstv[64 * h2:64 * h2 + L, :, :], bk[h2][64 * h2:64 * h2 + L, 0:512].rearrange("p (a n) -> p a n", a=8), eng="act")
                return f

            for src, dstv in ((bh, tkB[q]), (kh_, tkK[q]), (vb, tkV[q])):
                steps.append(tstep(src, dstv))

            def sstep(lhs, rhs, dst, msk):
                def f():
                    bks = headmm(L, L, lambda p, h2: [(HS(lhs, p, h2, o, o + L), HS(rhs, p, h2, o, o + L))])
                    for h2 in range(2):
                        P.tt(dst[64 * h2:64 * h2 + L, :, :L], pview(bks, h2, L, L), bc(msk[64 * h2:64 * h2 + L, :L], 1, 8), ALU.mult)
                return f

            for lhs, rhs, dst, msk in ((bh, at, sAab, triS), (at, bh, Nn[0], triL), (kh_, at, sAak[q], triS),
                                       (bh, rt, sArb[q], triI), (kh_, rt, sArk[q], triI)):
                steps.append(sstep(lhs, rhs, dst, msk))

            def f0():
                for h2 in range(2):
                    R_ = slice(64 * h2, 64 * h2 + L)
                    P.copy(Nt[0][R_, :, :L], sAab[R_, :, :L], eng="pool")
                    P.tt(FtL[q][0][R_, :, :L], sAab[R_, :, :L], bc(identf[R_, :L], 1, 8), ALU.add, eng="pool")
            steps.insert(5, f0)

            def qstep(lv):
                def f():
                    a_, b_ = lv % 2, (lv + 1) % 2
                    b1 = headmm(L, L, lambda p, h2: [(RLf(Nt[a_], p, h2, L, L), RLf(Nn[a_], p, h2, L, L))])
                    b2 = headmm(L, L, lambda p, h2: [(RLf(Nn[a_], p, h2, L, L), RLf(Nt[a_], p, h2, L, L))])
                    for h2 in range(2):
                        R_ = slice(64 * h2, 64 * h2 + L)
                        if lv < nlev - 2:
                            P.copy(Nn[b_][R_, :, :L], pview(b1, h2, L, L))
                            P.copy(Nt[b_][R_, :, :L], pview(b2, h2, L, L), eng="act")
                        P.tt(FtL[q][lv + 1][R_, :, :L], pview(b2, h2, L, L), bc(identf[R_, :L], 1, 8), ALU.add)
                return f

            for lv in range(nlev - 1):
                steps.append(qstep(lv))
            return steps

        def main_steps(ci):
            c = chunks[ci]
            o, L, sq_ = c["off"], c["L"], c["seq"]
            q = ci % 2
            nlev = 6 if L == 64 else 4
            steps = []

            def x0():
                if c["first"]:
                    P.dma("pool", Mst[:, :, :], st_wkv[sq_], "stin")
                    P.copy(Mbf[:, :, :], Mst[:, :, :], eng="act")
                bks = headmm(L, 64, lambda p, h2: [(HS(at, p, h2, o, o + L), Mbf[64 * h2:64 * h2 + 64, p, :]),
                                                   (RLf(sAak[q], p, h2, L, L), RLf(tkV[q], p, h2, L))])
                for h2 in range(2):
                    P.copy(Xb[0][64 * h2:64 * h2 + L, :, :], pview(bks, h2, L, 64), eng="act")
            steps.append(x0)

            def xstep(lv):
                def f():
                    a_, b_ = lv % 2, (lv + 1) % 2
                    bks = headmm(L, 64, lambda p, h2: [(RLf(FtL[q][lv], p, h2, L, L), RLf(Xb[a_], p, h2, L))])
                    for h2 in range(2):
                        P.copy(Xb[b_][64 * h2:64 * h2 + L, :, :], pview(bks, h2, L, 64), eng="act")
                return f

            for lv in range(nlev):
                steps.append(xstep(lv))
            Ub = Xb[nlev % 2]

            def ostep():
                bks = headmm(64, L, lambda p, h2: [(Mbf[64 * h2:64 * h2 + 64, p, :], HS(rt, p, h2, o, o + L)),
                                                   (RLf(Ub, p, h2, L), RLf(sArb[q], p, h2, L, L)),
                                                   (RLf(tkV[q], p, h2, L), RLf(sArk[q], p, h2, L, L))])
                for h2 in range(2):
                    P.copy(oTc[64 * h2:64 * h2 + 64, :, o:o + L], pview(bks, h2, 64, L), eng="act")
            steps.append(ostep)

            def ststep():
                bks = headmm(64, 64, lambda p, h2: [(RLf(tkB[q], p, h2, L), RLf(Ub, p, h2, L)), (RLf(tkK[q], p, h2, L), RLf(tkV[q], p, h2, L))])
                for h2 in range(2):
                    R_ = slice(64 * h2, 64 * h2 + 64)
                    P.tt(Mst[R_, :, :], Mst[R_, :, :], pview(bks, h2, 64, 64), ALU.add)
                for p in range(8):
                    P.ts(Mst[:, p, :], Mst[:, p, :], pl[:, p, ci:ci + 1], None, ALU.mult)
                P.copy(Mbf[:, :, :], Mst[:, :, :], eng="act")
                if c["last"]:
                    P.dma("pool", o_wkv[sq_], Mst[:, :, :], "o_wkv")
            steps.append(ststep)
            return steps

        for f in pre_steps(0):
            f()
        for ci in range(len(chunks)):
            ms = main_steps(ci)
            pr = pre_steps(ci + 1) if ci + 1 < len(chunks) else []
            for i in range(max(len(ms), len(pr))):
                if i < len(pr):
                    pr[i]()
                if i < len(ms):
                    ms[i]()

        if CSTOP == 4 or 30 < CSTOP < 40:
            drain(C_ALL[9:])
            return
        ob = tbf
        osq = bh
        P.copy(ob[:, :, :Tb], oTc[:, :, :Tb], eng="act")
        P.act(osq[:, :, :Tb], oTc[:, :, :Tb], AF.Square)
        yC = at
        for p in range(8):
            pm, _ = bank()
            pq, _ = bank()
            P.mm([(pm[:, :Tb], [(bd64, ob[:, p, :Tb])])])
            P.mm([(pq[:, :Tb], [(bd64, osq[:, p, :Tb])])])
            mu = tA[:, p, :Tb]
            vr = tB[:, p, :Tb]
            P.copy(mu, pm[:, :Tb], eng="act")
            P.tt(vr, mu, mu, ALU.mult)
            P.tt(vr, pq[:, :Tb], vr, ALU.subtract)
            rsqrt(vr, vr, GN_EPS)
            P.tt(oTc[:, p, :Tb], oTc[:, p, :Tb], mu, ALU.subtract)
            P.tt(oTc[:, p, :Tb], oTc[:, p, :Tb], vr, ALU.mult)
            P.ts(oTc[:, p, :Tb], oTc[:, p, :Tb], c128("c_gng", p, 1), c128("c_gnb", p, 1), ALU.mult, ALU.add)
            P.tt(oTc[:, p, :Tb], oTc[:, p, :Tb], bon[:, p, :Tb], ALU.add)
            P.tt(yC[:, p, :Tb], oTc[:, p, :Tb], gsl[:, p, :Tb], ALU.mult)
        outproj(Tb, ["C_o0", "C_o1"], [512, 512], lambda k: yC[:, k, :Tb], 8)

    layers = [layerA, layerB, layerC, layerD][:NL]
    for blk in plan:
        Tb = blk["Tb"]
        P.dma("pool", xT[:, :, :Tb], xin.rearrange("(k p) t -> p k t", p=128)[:, :, blk["tok0"]:blk["tok0"] + Tb], "xin")
        for lf in layers:
            lf(blk)
        aoff[0] = ARENA_BYTES - (8 * MAXB * 4 + 8 * MAXB * 2 + MAXB * 4 + 256)
        yo = aal([128, 8, MAXB])
        rmsnorm(Tb, "fin_g", 0, None, dstf=yo)
        P.dma("pool", yT.rearrange("(k p) t -> p k t", p=128)[:, :, blk["tok0"]:blk["tok0"] + Tb], yo[:, :, :Tb], "yout")
    P.emit()
    return nc, P


def host_tables(plan, NT, Tp):
    kt = np.zeros((128, KTW), np.float32)
    kt[:, 0:128] = np.eye(128, dtype=np.float32)
    kt[:, 128:256] = 1.0 / 1024
    kt[:, 256:384] = 1.0 / 512
    bd = np.zeros((128, 128), np.float32)
    bd[:64, :64] = 1.0
    bd[64:, 64:] = 1.0
    kt[:, 384:512] = bd / 64.0
    kt[:, 512:640] = bd
    s = np.arange(64)[:, None]
    t = np.arange(64)[None, :]
    kt[:64, 640:704] = (t > s)
    kt[:64, 704:768] = (t >= s)
    kt[:64, 768:832] = (s > t)
    kt[:64, 832:896] = np.eye(64)
    gam = (1.0 - 2.0 ** (-5.0 - np.arange(4))).astype(np.float64)
    for h in range(4):
        dm = np.where(t >= s, gam[h] ** np.maximum(t - s, 0), 0.0) / 16.0
        kt[:64, 896 + 64 * h:896 + 64 * h + 64] = dm
        m = np.arange(64)
        kt[:64, 1152 + h] = gam[h] ** (63.0 - m)
        kt[:64, 1156 + h] = gam[h] ** np.maximum(15.0 - m, 0.0)
    kt[64:128, 640:896] = kt[0:64, 640:896]
    kt[:, 1160:1224] = 1.0
    return kt


def _prep(inputs):
    x_prompt = np.asarray(inputs["x_prompt"], np.float32)
    x_sample = np.asarray(inputs["x_sample"], np.float32)
    B, S, _ = x_prompt.shape
    Tp = S + NMETA
    plan, NT = make_plan(Tp)
    meta = np.asarray(inputs["meta_tokens"], np.float32)

    def pk(v, p):
        v = np.asarray(v, np.float32).reshape(-1, p)
        return np.ascontiguousarray(v.T)

    c128 = np.zeros((128, NC128), np.float32)

    def put(name, arr):
        o, w = C128[name]
        assert arr.shape == (128, w), (name, arr.shape)
        c128[:, o:o + w] = arr

    put("norm_g", np.concatenate([pk(inputs["norm_g"][i], 128) for i in range(4)], axis=1))
    put("fin_g", pk(inputs["final_norm_g"], 128))
    wdw = np.asarray(inputs["a_w_dw"], np.float32)
    put("a_wdw", np.ascontiguousarray(wdw.reshape(31, 8, 128).transpose(2, 1, 0)).reshape(128, 248))
    put("a_bdw", pk(inputs["a_b_dw"], 128))
    put("a_lng", pk(inputs["a_ln_g"], 128))
    put("a_lnb", pk(inputs["a_ln_b"], 128))
    wc = np.asarray(inputs["b_w_conv"], np.float32)
    put("b_wc", np.ascontiguousarray(wc.reshape(4, 10, 128).transpose(2, 1, 0)).reshape(128, 40))
    put("b_bc", pk(inputs["b_b_conv"], 128))
    put("b_brg", pk(inputs["b_b_rg"], 128))
    put("b_big", pk(inputs["b_b_ig"], 128))
    put("b_lam", pk(inputs["b_lam"], 128))
    mu = np.asarray(inputs["c_mu"], np.float32)
    put("c_mu", np.ascontiguousarray(mu.reshape(6, 8, 128).transpose(2, 0, 1)).reshape(128, 48))
    put("d_gng", pk(inputs["d_gn_g"], 128))
    for name, key in [("c_w0", "c_w0"), ("c_a0", "c_a0"), ("c_kk", "c_k_k"), ("c_ka", "c_k_a"), ("c_rk", "c_r_k"),
                      ("c_gng", "c_gn_g"), ("c_gnb", "c_gn_b")]:
        put(name, pk(np.asarray(inputs[key]).reshape(-1), 128))
    c64 = None

    kt = host_tables(plan, NT, Tp)
    pos = np.zeros(NT, np.float32)
    nin = np.zeros(NT, np.float64)
    for blk in plan:
        for c in blk["chunks"]:
            g0 = c["gtok"]
            L = c["L"]
            if c["seq"] == 0:
                p0 = g0
            else:
                p0 = NMETA + PAST
            pos[g0:g0 + L] = p0 + np.arange(L)
            nin[g0:g0 + L] = np.arange(L)
    half = 128
    inv = (10000.0 ** (-np.arange(half, dtype=np.float32) / half)).astype(np.float32)
    ang = pos[None, :].astype(np.float32) * inv[:, None]
    rope = np.zeros((128, 6, NT), np.float32)
    rope[:, 0, :] = np.cos(ang)
    rope[:, 1, :] = np.sin(ang)
    gam = (1.0 - 2.0 ** (-5.0 - np.arange(4))).astype(np.float64)
    for h in range(4):
        rope[:, 2 + h, :] = (gam[h] ** (nin + 1.0) / 16.0)[None, :]
    return plan, NT, Tp, c128, c64, kt, rope, meta, x_prompt, x_sample


_CACHE = {}
_SIM = None
CSTOP = 99
FILLER = 0


def kernel(**inputs):
    plan, NT, Tp, c128, c64, kt, rope, meta, x_prompt, x_sample = _prep(inputs)
    NL = int(inputs.pop("_NL", 4)) if "_NL" in inputs else 4
    key = (Tp, NL)
    if key not in _CACHE:
        _CACHE[key] = build(Tp, NL)
    nc, _ = _CACHE[key]
    B = x_prompt.shape[0]
    f32 = lambda a: np.ascontiguousarray(np.asarray(a, np.float32))
    ktf = kt.copy()
    in_maps = []
    for core in range(8):
        b = core % B
        sidx = [2 * (core % 4), 2 * (core % 4) + 1]
        xs = np.concatenate([meta, x_prompt[b], x_sample[sidx[0]], x_sample[sidx[1]]], axis=0)
        m = {"xin": f32(xs.T), "cst128": c128, "ktab": ktf, "rope": rope}

        def st3(samp, zshape, tr):
            z = np.zeros(zshape, np.float32)
            return f32(np.stack([z, tr(samp[sidx[0]]), tr(samp[sidx[1]])], axis=0))

        m["st_ca"] = st3(np.asarray(inputs["cache_conv_a"], np.float32), (D, 30), lambda a: a.T)
        m["st_cb"] = st3(np.asarray(inputs["cache_conv_b"], np.float32), (DRNN, 3), lambda a: a.T)
        m["st_lru"] = st3(np.asarray(inputs["state_lru_b"], np.float32), (DRNN,), lambda a: a)
        m["st_sh"] = st3(np.asarray(inputs["state_shift_c"], np.float32), (D,), lambda a: a)
        m["st_wkv"] = st3(np.asarray(inputs["state_wkv_c"], np.float32), (128, 8, 64),
                           lambda a: a.reshape(8, 2, 64, 64).transpose(1, 3, 0, 2).reshape(128, 8, 64))
        m["st_ret"] = st3(np.asarray(inputs["state_ret_d"], np.float32), (1024, 512), lambda a: a.reshape(1024, 512))
        for n in WSHAPES:
            m[n] = f32(inputs[n])
        in_maps.append(m)
    if "_SIM" in globals() and _SIM is not None:
        res = _SIM(nc, in_maps)
    else:
        res = run_bass_kernel_spmd(nc, in_maps, core_ids=list(range(8)))
    R = res.results
    S = x_prompt.shape[1]
    y_prompt = np.stack([R[b]["yT"][:, NMETA:Tp].T for b in range(B)], axis=0)
    y_sample = np.stack([R[i // 2]["yT"][:, Tp + 64 * (i % 2):Tp + 64 * (i % 2) + 64].T for i in range(8)], axis=0)

    def outs(name, tr):
        p = np.stack([tr(R[b][name][0]) for b in range(B)], axis=0)
        s = np.stack([tr(R[i // 2][name][1 + i % 2]) for i in range(8)], axis=0)
        return np.ascontiguousarray(p, np.float32), np.ascontiguousarray(s, np.float32)

    ca_p, ca_s = outs("o_ca", lambda a: a.T)
    cb_p, cb_s = outs("o_cb", lambda a: a.T)
    lr_p, lr_s = outs("o_lru", lambda a: a)
    sh_p, sh_s = outs("o_sh", lambda a: a)
    wk_p, wk_s = outs("o_wkv", lambda a: a.reshape(2, 64, 8, 64).transpose(2, 0, 3, 1).reshape(16, 64, 64))
    rt_p, rt_s = outs("o_ret", lambda a: a.reshape(4, 256, 512))
    return (np.ascontiguousarray(y_prompt, np.float32), np.ascontiguousarray(y_sample, np.float32),
            ca_p, ca_s, cb_p, cb_s, lr_p, lr_s, sh_p, sh_s, wk_p, wk_s, rt_p, rt_s)
```

```python
import numpy as np
from contextlib import ExitStack
import concourse.bass as bass
import concourse.mybir as mybir
from concourse.bass_utils import run_bass_kernel_spmd

F32 = mybir.dt.float32
BF16 = mybir.dt.bfloat16
AF = mybir.ActivationFunctionType
ALU = mybir.AluOpType
AX = mybir.AxisListType

D = 1024
NMETA = 16
PAST = 1024
DRNN = 1280
NORM_EPS = 1e-6
LN_EPS = 1e-5
GN_EPS = 64e-5
SLOT = 4096
NSLOT = 5
ARENA_BYTES = 120 * 1024
MAXB = 256
KTW = 1280


def _esz(dt):
    return 2 if dt == BF16 else 4


class Prog:
    ENGS = ["pe", "act", "dve", "pool", "sp"]

    def __init__(self, nc, es):
        self.nc = nc
        self.es = es
        self.semobj = {}
        self.cnt = {}
        self.ops = {e: [] for e in self.ENGS}
        self.waited = {e: {} for e in self.ENGS}
        self.recs = {}
        self.tracked_dram = set()
        for e in self.ENGS:
            self.semobj["e_" + e] = es.enter_context(nc.semaphore("s_" + e))
            self.cnt["e_" + e] = 0
        self.nops = 0
        self.pe_filler = None

    def rng(self, ap):
        t = ap.tensor
        name = t.name
        tn = type(t).__name__
        if "DRam" in tn:
            if name in self.tracked_dram:
                return (name, 0, 1 << 60)
            return None
        pairs = ap.ap
        row = pairs[0][0]
        esz = _esz(ap.dtype)
        off = ap.offset % row if row > 0 else ap.offset
        ext = sum((c - 1) * abs(s) for s, c in pairs[1:]) + 1
        lo, hi = off * esz, (off + ext) * esz
        if "PSum" in tn:
            lo = (lo // 2048) * 2048
            hi = ((hi + 2047) // 2048) * 2048
            return ("@" + name, lo, hi)
        return (name, lo, hi)

    def op(self, eng, fn, reads, writes, dma_key=None):
        need = {}
        accs = [(self.rng(a), "r") for a in reads] + [(self.rng(a), "w") for a in writes]
        accs = [(r, "w" if (r is not None and r[0].startswith("@")) else k) for r, k in accs]
        for r, kind in accs:
            if r is None:
                continue
            name, lo, hi = r
            for (lo2, hi2, k2, s2), v2 in self.recs.get(name, {}).items():
                if lo2 < hi and lo < hi2 and (kind == "w" or k2 == "w"):
                    if need.get(s2, 0) < v2:
                        need[s2] = v2
        if dma_key is None:
            sk = "e_" + eng
            self.cnt[sk] += 1
            inc = 1
        else:
            sk = "d_" + dma_key + "_" + eng
            if sk not in self.semobj:
                self.semobj[sk] = self.es.enter_context(self.nc.semaphore("s_" + dma_key + "_" + eng))
                self.cnt[sk] = 0
            if self.cnt[sk] > 0:
                need[sk] = max(need.get(sk, 0), self.cnt[sk])
            self.cnt[sk] += 16
            inc = 16
        tokv = self.cnt[sk]
        waits = []
        for s, v in need.items():
            if s == "e_pe" and eng == "pe":
                continue
            if s.startswith("d_"):
                v = self.cnt[s] if s != sk else v
            if self.waited[eng].get(s, 0) >= v:
                continue
            self.waited[eng][s] = v
            waits.append((self.semobj[s], v))
        self.ops[eng].append((waits, fn, (self.semobj[sk], inc), self.pe_filler if eng == "pe" else None))
        self.nops += 1
        for r, kind in accs:
            if r is None:
                continue
            name, lo, hi = r
            d = self.recs.setdefault(name, {})
            if kind == "w":
                for key in [k for k in d if lo <= k[0] and k[1] <= hi]:
                    del d[key]
            d[(lo, hi, kind, sk)] = tokv

    def call(self, eng, method, out, **kw):
        reads = [v for v in kw.values() if isinstance(v, bass.AP)]
        writes = [out]

        def fn(e, method=method, out=out, kw=kw):
            return getattr(e, method)(out=out, **kw)

        self.op(eng, fn, reads, writes)

    def mm(self, groups):
        reads = []
        writes = []
        for out, pairs in groups:
            writes.append(out)
            for l, r in pairs:
                reads.append(l)
                reads.append(r)

        def fn(e, groups=groups):
            ins = None
            for out, pairs in groups:
                n = len(pairs)
                for i, (l, r) in enumerate(pairs):
                    ins = e.matmul(out, l, r, start=(i == 0), stop=(i == n - 1))
            return ins

        self.op("pe", fn, reads, writes)

    def tr(self, items, ident):
        reads = [ident] + [i for _, i in items]
        writes = [o for o, _ in items]

        def fn(e, items=items, ident=ident):
            ins = None
            for o, i in items:
                k = i.shape[0]
                ins = e.transpose(o, i, ident[:k, :k])
            return ins

        self.op("pe", fn, reads, writes)

    def tr3(self, items):
        reads = [i for _, i, _ in items] + [items[0][2]]
        writes = [o for o, _, _ in items]

        def fn(e, items=items):
            ins = None
            for o, i, idn in items:
                ins = e.transpose(o, i, idn)
            return ins

        self.op("pe", fn, reads, writes)

    def dma(self, eng, out, in_, key, slow=False):
        def fn(e, out=out, in_=in_):
            if slow:
                return e.dma_start(out=out, in_=in_, allow_slow_non_contiguous=True)
            return e.dma_start(out=out, in_=in_)

        self.op(eng, fn, [in_], [out], dma_key=key)

    def memset(self, eng, ap, val):
        def fn(e, ap=ap, val=val):
            return e.memset(ap, val)

        self.op(eng, fn, [], [ap])

    def act(self, out, in_, func, **kw):
        self.call("act", "activation", out, in_=in_, func=func, **kw)

    def tt(self, out, in0, in1, op, eng="dve"):
        self.call(eng, "tensor_tensor", out, in0=in0, in1=in1, op=op)

    def ts(self, out, in0, s1, s2, op0, op1=None, eng="dve"):
        if op1 is None:
            self.call(eng, "tensor_scalar", out, in0=in0, scalar1=s1, scalar2=None, op0=op0)
        else:
            self.call(eng, "tensor_scalar", out, in0=in0, scalar1=s1, scalar2=s2, op0=op0, op1=op1)

    def stt(self, out, in0, scalar, in1, op0, op1):
        self.call("dve", "scalar_tensor_tensor", out, in0=in0, scalar=scalar, in1=in1, op0=op0, op1=op1)

    def copy(self, out, in_, eng="dve"):
        if eng == "act":
            self.call("act", "copy", out, in_=in_)
        else:
            self.call(eng, "tensor_copy", out, in_=in_)

    def emit(self):
        nc = self.nc
        final = [(self.semobj[s], self.cnt[s]) for s in self.semobj if s.startswith("d_") and self.cnt[s] > 0]
        ops = self.ops

        def run(name, e):
            for waits, fn, inc, post in ops[name]:
                for s, v in waits:
                    e.wait_ge(s, v)
                ins = fn(e)
                ins.then_inc(inc[0], inc[1])
                if post is not None:
                    post(e)

        with nc.Block() as block:

            @block.tensor
            def _(e):
                run("pe", e)

            @block.scalar
            def _(e):
                run("act", e)

            @block.vector
            def _(e):
                run("dve", e)

            @block.gpsimd
            def _(e):
                run("pool", e)

            @block.sync
            def _(e):
                run("sp", e)
                for s, v in final:
                    e.wait_ge(s, v)


def bc(ap, pos, n):
    pairs = [list(p) for p in ap.ap]
    pairs.insert(pos, [0, n])
    return bass.AP(ap.tensor, ap.offset, pairs)


def make_plan(Tp):
    seqs = [[NMETA] + [64] * ((Tp - NMETA) // 64), [64], [64]]
    blocks = []
    cur = []
    cur_n = 0
    tok = 0
    for s, chs in enumerate(seqs):
        for ci, L in enumerate(chs):
            if cur_n + L > MAXB:
                blocks.append(cur)
                cur = []
                cur_n = 0
            cur.append(dict(seq=s, L=L, off=cur_n, first=(ci == 0), last=(ci == len(chs) - 1), gtok=tok, ci=ci))
            cur_n += L
            tok += L
    blocks.append(cur)
    out = []
    for b in blocks:
        segs = []
        for c in b:
            if segs and segs[-1]["seq"] == c["seq"]:
                segs[-1]["chunks"].append(c)
                segs[-1]["len"] += c["L"]
                segs[-1]["last"] = c["last"]
            else:
                segs.append(dict(seq=c["seq"], off=c["off"], len=c["L"], chunks=[c], first=c["first"], last=c["last"]))
        out.append(dict(Tb=sum(c["L"] for c in b), tok0=b[0]["gtok"], segs=segs, chunks=b))
    return out, tok


C128 = {}
_o = 0
for _n, _w in [("norm_g", 32), ("fin_g", 8), ("a_wdw", 8 * 31), ("a_bdw", 8), ("a_lng", 8), ("a_lnb", 8),
               ("b_wc", 40), ("b_bc", 10), ("b_brg", 10), ("b_big", 10), ("b_lam", 10), ("c_mu", 48), ("d_gng", 16),
               ("c_w0", 8), ("c_a0", 8), ("c_kk", 8), ("c_ka", 8), ("c_rk", 8), ("c_gng", 8), ("c_gnb", 8)]:
    C128[_n] = (_o, _w)
    _o += _w
NC128 = _o
C64 = {}
_o = 0
for _n, _w in [("w0", 16), ("a0", 16), ("kk", 16), ("ka", 16), ("rk", 16), ("gng", 16), ("gnb", 16)]:
    C64[_n] = (_o, _w)
    _o += _w
NC64 = _o


def weight_units():
    U = []

    def colslab(src, c0, nc_, nk=8):
        return (src, 128, nk, nc_, lambda w, c0=c0, nc_=nc_: w.rearrange("(k p) n -> p k n", p=128)[:, :, c0:c0 + nc_])

    for j in range(4):
        U.append(("A_in%d" % j, [colslab("a_w_in", 1024 + 256 * j, 256), colslab("a_w_in", 256 * j, 256)]))
    for j in range(2):
        U.append(("A_g%d" % j, [colslab("a_w_in", 2048 + 512 * j, 512)]))
    for j in range(2):
        U.append(("A_o%d" % j, [colslab("a_w_out", 512 * j, 512)]))
    for j in range(5):
        U.append(("B_in%d" % j, [colslab("b_w_in", 512 * j, 512)]))
    gv = lambda w: w.rearrange("n k j -> k n j")
    U.append(("B_gt", [("b_w_rg", 128, 10, 128, gv), ("b_w_ig", 128, 10, 128, gv)]))
    for j, (c0, n_) in enumerate([(0, 384), (384, 384), (768, 256)]):
        U.append(("B_o%d" % j, [colslab("b_w_out", c0, n_, nk=10)]))
    U.append(("C_lora", [colslab("c_w1", 0, 64), colslab("c_a1", 0, 64),
                         ("c_w2", 64, 1, 1024, lambda w: w.rearrange("(o p) n -> p o n", o=1)),
                         ("c_a2", 64, 1, 1024, lambda w: w.rearrange("(o p) n -> p o n", o=1))]))
    for s, nm in enumerate(["r", "k", "v", "g"]):
        for j in range(2):
            U.append(("C_%s%d" % (nm, j), [colslab("c_w_in", 1024 * s + 512 * j, 512)]))
    for j in range(2):
        U.append(("C_o%d" % j, [colslab("c_w_out", 512 * j, 512)]))
    for j in range(12):
        U.append(("D_in%d" % j, [colslab("d_w_in", 512 * j, 512)]))
    for j in range(4):
        U.append(("D_o%d" % j, [colslab("d_w_out", 256 * j, 256, nk=16)]))
    return U


WSHAPES = {"a_w_in": (1024, 3072), "a_w_out": (1024, 1024), "b_w_in": (1024, 2560), "b_w_rg": (10, 128, 128),
           "b_w_ig": (10, 128, 128), "b_w_out": (1280, 1024), "c_w_in": (1024, 4096), "c_w1": (1024, 64),
           "c_w2": (64, 1024), "c_a1": (1024, 64), "c_a2": (64, 1024), "c_w_out": (1024, 1024),
           "d_w_in": (1024, 6144), "d_w_out": (2048, 1024)}


def build(Tp, NL=4):
    plan, NT = make_plan(Tp)
    nc = bass.Bass("TRN2", target_bir_lowering=False)
    es = ExitStack()
    es.enter_context(nc.allow_low_precision("bf16 matmul operands, fp32 accumulation"))
    P = Prog(nc, es)

    def din(name, shape, dt=F32):
        return nc.dram_tensor(name, list(shape), dt, kind="ExternalInput").ap()

    def dout(name, shape):
        return nc.dram_tensor(name, list(shape), F32, kind="ExternalOutput").ap()

    xin = din("xin", (D, NT))
    cst128_d = din("cst128", (128, NC128))
    ktab_d = din("ktab", (128, KTW))
    rope_d = din("rope", (128, 6, NT))
    st_ca = din("st_ca", (3, D, 30))
    st_cb = din("st_cb", (3, DRNN, 3))
    st_lru = din("st_lru", (3, DRNN))
    st_sh = din("st_sh", (3, D))
    st_wkv = din("st_wkv", (3, 128, 8, 64))
    st_ret = din("st_ret", (3, 1024, 512))
    wd = {n: din(n, s) for n, s in WSHAPES.items()}
    yT = dout("yT", (D, NT))
    o_ca = dout("o_ca", (3, D, 30))
    o_cb = dout("o_cb", (3, DRNN, 3))
    o_lru = dout("o_lru", (3, DRNN))
    o_sh = dout("o_sh", (3, D))
    o_wkv = dout("o_wkv", (3, 128, 8, 64))
    o_ret = dout("o_ret", (3, 1024, 512))
    units = weight_units()
    NU = len(units)
    wscr = nc.dram_tensor("wscr", [NU, 128, SLOT], BF16, kind="Internal").ap()
    P.tracked_dram.add("wscr")

    def sb(name, shape, dt=F32):
        return es.enter_context(nc.sbuf_tensor("sb_" + name, list(shape), dt))

    psall = es.enter_context(nc.psum_tensor("psall", [128, 4096], F32))
    psall_bf = psall.bitcast(BF16)
    arena = sb("arena", [128, ARENA_BYTES // 4])
    arena_bf = arena.bitcast(BF16)
    xT = sb("xT", [128, 8, MAXB])
    cst = sb("cst", [128, NC128])
    ktab = sb("ktab_sb", [128, KTW])
    ktb = sb("ktb", [128, 640], BF16)
    cder = sb("cder", [128, 64])
    haloA = sb("haloA", [128, 8, 30])
    haloB = sb("haloB", [128, 10, 3])
    hstB = sb("hstB", [128, 10])
    shC = sb("shC", [128, 8])
    Mst = sb("Mst", [128, 8, 64])
    Mbf = sb("Mbf", [128, 8, 64], BF16)
    Sst = sb("Sst", [128, 8, 512])
    Sbf = sb("Sbf", [128, 8, 512], BF16)
    ring = [sb("ring%d" % i, [128, SLOT], BF16) for i in range(NSLOT)]

    NBANK = 7 if FILLER > 0 else 8
    bankctr = [0]

    def bank():
        i = bankctr[0] % NBANK
        bankctr[0] += 1
        return psall[:, 512 * i:512 * i + 512], psall_bf[:, 1024 * i:1024 * i + 1024]

    pairctr = [0]

    def bank2():
        i = (pairctr[0] % (NBANK // 2)) * 2
        pairctr[0] += 1
        return psall[:, 512 * i:512 * i + 1024], psall_bf[:, 1024 * i:1024 * i + 2048]

    aoff = [0]

    def areset():
        aoff[0] = 0

    def aal(shape, dt=F32):
        n = 1
        for s in shape[1:]:
            n *= s
        nb = n * _esz(dt)
        nb = (nb + 31) // 32 * 32
        o = aoff[0]
        aoff[0] += nb
        assert aoff[0] <= ARENA_BYTES, ("arena overflow", aoff[0])
        base = arena if dt == F32 else arena_bf
        e0 = o // _esz(dt)
        ap = base[:shape[0], e0:e0 + n]
        if len(shape) == 3:
            ap = ap.rearrange("p (a b) -> p a b", a=shape[1])
        return ap

    def c128(name, i0=0, n=None):
        o, w = C128[name]
        n = w - i0 if n is None else n
        return cst[:, o + i0:o + i0 + n]

    ident_bf = ktb[:, 0:128]
    onesD = ktb[:, 128:256]
    ones512 = ktb[:, 256:384]
    bd64 = ktb[:, 384:512]
    bd1 = ktb[:, 512:640]
    triS = ktab[:, 640:704]
    triI = ktab[:, 704:768]
    triL = ktab[:, 768:832]
    identf = ktab[:, 832:896]
    dmaskT = ktab[:64, 896:1152]
    kdec64 = ktab[:64, 1152:1156]
    kdec16 = ktab[:64, 1156:1160]
    onesf = ktab[:, 1160:1224]

    if FILLER > 0:
        dummy_ps = psall[:, 512 * 7:512 * 8]

        def _filler(e):
            for _ in range(FILLER):
                e.matmul(dummy_ps, ident_bf, ktb[:, 0:512], start=True, stop=True)

        P.pe_filler = _filler

    P.dma("pool", cst[:, :], cst128_d, "cst")
    P.dma("pool", ktab[:, :], ktab_d, "cst")
    P.copy(ktb[:, :], ktab[:, 0:640])
    P.act(cder[:, 0:10], c128("b_lam"), AF.Exp, scale=-1.0)
    xz = cder[:, 0:10]
    zz = cder[:, 32:42]
    z2 = cder[:, 42:52]
    P.ts(zz, xz, 2.0, None, ALU.add)
    P.call("dve", "reciprocal", zz, in_=zz)
    P.tt(zz, zz, xz, ALU.mult)
    P.tt(z2, zz, zz, ALU.mult)
    P.ts(xz, z2, 1.0 / 9.0, 1.0 / 7.0, ALU.mult, ALU.add)
    P.tt(xz, xz, z2, ALU.mult)
    P.ts(xz, xz, 1.0 / 5.0, None, ALU.add)
    P.tt(xz, xz, z2, ALU.mult)
    P.ts(xz, xz, 1.0 / 3.0, None, ALU.add)
    P.tt(xz, xz, z2, ALU.mult)
    P.ts(xz, xz, 1.0, None, ALU.add)
    P.tt(xz, xz, zz, ALU.mult)
    P.ts(cder[:, 0:10], xz, -16.0, None, ALU.mult)
    P.ts(cder[:, 16:24], c128("c_ka"), -1.0, 1.0, ALU.mult, ALU.add)

    areset()
    NSTG = 4
    stg32 = [aal([128, SLOT]) for _ in range(NSTG)]
    stgbf = [aal([128, SLOT], BF16) for _ in range(NSTG)]
    cast_engs = ["dve", "act", "dve", "act", "pool"]
    for u, (uname, pieces) in enumerate(units):
        par = u % NSTG
        off = 0
        for pi, (src, Pp, a, b, view) in enumerate(pieces):
            n = a * b
            dst32 = stg32[par][:Pp, off:off + n].rearrange("p (a b) -> p a b", a=a)
            P.dma("sp" if pi % 2 == 0 else "pool", dst32, view(wd[src]), "stg%d" % par)
            P.copy(stgbf[par][:Pp, off:off + n], stg32[par][:Pp, off:off + n], eng=cast_engs[(u + pi) % 5])
            P.dma("sp" if pi % 2 == 0 else "pool", wscr[u, :Pp, off:off + n], stgbf[par][:Pp, off:off + n], "stgo%d" % par)
            off += n
        assert off <= SLOT

    uidx = {n: i for i, (n, _) in enumerate(units)}
    layer_units = {L: [n for n, _ in units if n.startswith(L + "_")] for L in "ABCD"}
    stream = []
    for bi in range(len(plan)):
        for L in "ABCD"[:NL]:
            for n in layer_units[L]:
                stream.append(n)
    issued = [0]
    sidx = [0]

    def issue_upto(k):
        while issued[0] < min(k, len(stream)):
            i = issued[0]
            u = uidx[stream[i]]
            slot = ring[i % NSLOT]
            off = 0
            for pi, (src, Pp, a, b, view) in enumerate(units[u][1]):
                n = a * b
                P.dma("sp", slot[:Pp, off:off + n], wscr[u, :Pp, off:off + n], "ring%d" % (i % NSLOT))
                off += n
            issued[0] += 1

    def wget(name):
        i = sidx[0]
        assert stream[i] == name, (stream[i], name)
        issue_upto(i + NSLOT)
        sidx[0] += 1
        slot = ring[i % NSLOT]
        views = []
        off = 0
        for (src, Pp, a, b, view) in units[uidx[name]][1]:
            n = a * b
            views.append(slot[:Pp, off:off + n].rearrange("p (a b) -> p a b", a=a))
            off += n
        return views

    def rsqrt(out, in_, eps):
        P.act(out, in_, AF.Ln, bias=float(eps))
        P.act(out, out, AF.Exp, scale=-0.5)

    def rmsnorm(Tb, gname, gi0, dst, dstf=None):
        sq = aal([128, 8, MAXB], BF16)
        rstd = aal([128, MAXB])
        P.act(sq[:, :, :Tb], xT[:, :, :Tb], AF.Square)
        ps, _ = bank()
        P.mm([(ps[:, :Tb], [(onesD, sq[:, k, :Tb]) for k in range(8)])])
        rsqrt(rstd[:, :Tb], ps[:, :Tb], NORM_EPS)
        for k in range(8):
            g = c128(gname, gi0 + k, 1)
            if dst is not None:
                P.stt(dst[:, k, :Tb], xT[:, k, :Tb], g, rstd[:, :Tb], ALU.mult, ALU.mult)
            if dstf is not None:
                P.stt(dstf[:, k, :Tb], xT[:, k, :Tb], g, rstd[:, :Tb], ALU.mult, ALU.mult)

    def outproj(Tb, unit_names, ncols_per_unit, yK, nk):
        m = 0
        for un, ncu in zip(unit_names, ncols_per_unit):
            (w,) = wget(un)
            for j in range(ncu // 128):
                ps, _ = bank()
                P.mm([(ps[:, :Tb], [(w[:, k, 128 * j:128 * j + 128], yK(k)) for k in range(nk)])])
                P.tt(xT[:, m, :Tb], xT[:, m, :Tb], ps[:, :Tb], ALU.add)
                m += 1
        assert m == 8

    def layerA(blk):
        Tb = blk["Tb"]
        segs = blk["segs"]
        areset()
        hT = aal([128, 8, MAXB], BF16)
        WA = 30 * len(segs) + Tb
        glu = aal([128, 8, WA])
        gate = aal([128, 8, MAXB], BF16)
        sig = [aal([128, MAXB]) for _ in range(2)]
        acc = aal([128, 8, MAXB])
        rmsnorm(Tb, "norm_g", 0, hT)
        scol = [30 * (i + 1) + s["off"] for i, s in enumerate(segs)]
        for i, s in enumerate(segs):
            h = glu[:, :, scol[i] - 30:scol[i]]
            if s["first"]:
                P.dma("pool", h, st_ca[s["seq"]].rearrange("(k p) t -> p k t", p=128), "stin")
            else:
                P.copy(h, haloA[:, :, :], eng="pool")
        for j in range(4):
            wb, wa = wget("A_in%d" % j)
            for jj in range(2):
                m = 2 * j + jj
                ps, _ = bank()
                P.mm([(ps[:, :Tb], [(wb[:, k, 128 * jj:128 * jj + 128], hT[:, k, :Tb]) for k in range(8)])])
                sg = sig[m % 2]
                P.act(sg[:, :Tb], ps[:, :Tb], AF.Sigmoid)
                ps2, _ = bank()
                P.mm([(ps2[:, :Tb], [(wa[:, k, 128 * jj:128 * jj + 128], hT[:, k, :Tb]) for k in range(8)])])
                for i, s in enumerate(segs):
                    P.tt(glu[:, m, scol[i]:scol[i] + s["len"]], ps2[:, s["off"]:s["off"] + s["len"]],
                         sg[:, s["off"]:s["off"] + s["len"]], ALU.mult)
        for j in range(2):
            (wg,) = wget("A_g%d" % j)
            for jj in range(4):
                m = 4 * j + jj
                ps, _ = bank()
                P.mm([(ps[:, :Tb], [(wg[:, k, 128 * jj:128 * jj + 128], hT[:, k, :Tb]) for k in range(8)])])
                P.act(gate[:, m, :Tb], ps[:, :Tb], AF.Silu)
        o_w, _ = C128["a_wdw"]
        single = (len(segs) == 1)
        NDV = 13 if single else 31
        for j in range(NDV):
            for m in range(8):
                for i, s in enumerate(segs):
                    c0 = scol[i] - 30
                    L = s["len"]
                    dst = acc[:, m, s["off"]:s["off"] + L]
                    src = glu[:, m, c0 + j:c0 + j + L]
                    wj = cst[:, o_w + m * 31 + j:o_w + m * 31 + j + 1]
                    if j == 0:
                        P.ts(dst, src, wj, c128("a_bdw", m, 1), ALU.mult, ALU.add)
                    else:
                        P.stt(dst, src, wj, dst, ALU.mult, ALU.add)
        if single:
            NTMP = 6
            tmpb = [aal([128, MAXB], BF16) for _ in range(NTMP)]
            ti = 0
            c0 = scol[0] - 30
            for m in range(8):
                pc, _ = bank()
                for j in range(NDV, 31):
                    t_ = tmpb[ti % NTMP]
                    ti += 1
                    P.act(t_[:, :Tb], glu[:, m, c0 + j:c0 + j + Tb], AF.Copy, scale=cst[:, o_w + m * 31 + j:o_w + m * 31 + j + 1])
                    P.op("pe", (lambda e, o_=pc[:, :Tb], r_=t_[:, :Tb], j=j: e.matmul(o_, ident_bf, r_, start=(j == NDV), stop=(j == 30))),
                         [ident_bf, t_[:, :Tb]], [pc[:, :Tb]])
                P.tt(acc[:, m, :Tb], acc[:, m, :Tb], pc[:, :Tb], ALU.add)
        for i, s in enumerate(segs):
            last30 = glu[:, :, scol[i] + s["len"] - 30:scol[i] + s["len"]]
            if s["last"]:
                P.dma("pool", o_ca[s["seq"]].rearrange("(k p) t -> p k t", p=128), last30, "o_ca")
            else:
                P.copy(haloA[:, :, :], last30, eng="pool")
        cb = aal([128, 8, MAXB], BF16)
        csq = aal([128, 8, MAXB], BF16)
        P.copy(cb[:, :, :Tb], acc[:, :, :Tb], eng="act")
        P.act(csq[:, :, :Tb], acc[:, :, :Tb], AF.Square)
        psm, _ = bank()
        pss, _ = bank()
        P.mm([(psm[:, :Tb], [(onesD, cb[:, k, :Tb]) for k in range(8)])])
        P.mm([(pss[:, :Tb], [(onesD, csq[:, k, :Tb]) for k in range(8)])])
        mean = aal([128, MAXB])
        var = aal([128, MAXB])
        P.copy(mean[:, :Tb], psm[:, :Tb], eng="act")
        P.tt(var[:, :Tb], mean[:, :Tb], mean[:, :Tb], ALU.mult)
        P.tt(var[:, :Tb], pss[:, :Tb], var[:, :Tb], ALU.subtract)
        rsqrt(var[:, :Tb], var[:, :Tb], LN_EPS)
        P.stt(mean[:, :Tb], mean[:, :Tb], -1.0, var[:, :Tb], ALU.mult, ALU.mult)
        P.tt(acc[:, :, :Tb], acc[:, :, :Tb], bc(var[:, :Tb], 1, 8), ALU.mult)
        P.tt(acc[:, :, :Tb], acc[:, :, :Tb], bc(mean[:, :Tb], 1, 8), ALU.add)
        for m in range(8):
            P.act(acc[:, m, :Tb], acc[:, m, :Tb], AF.Silu, bias=c128("a_lnb", m, 1), scale=c128("a_lng", m, 1))
        yb = hT
        P.tt(yb[:, :, :Tb], acc[:, :, :Tb], gate[:, :, :Tb], ALU.mult)
        outproj(Tb, ["A_o0", "A_o1"], [512, 512], lambda k: yb[:, k, :Tb], 8)

    def layerB(blk):
        Tb = blk["Tb"]
        segs = blk["segs"]
        areset()
        hT = aal([128, 8, MAXB], BF16)
        WB = 3 * len(segs) + Tb
        xb = aal([128, 10, WB])
        gs = aal([128, 10, MAXB], BF16)
        xc = aal([128, 10, MAXB])
        xcb = aal([128, 10, MAXB], BF16)
        rmsnorm(Tb, "norm_g", 8, hT)
        scol = [3 * (i + 1) + s["off"] for i, s in enumerate(segs)]
        for i, s in enumerate(segs):
            h = xb[:, :, scol[i] - 3:scol[i]]
            if s["first"]:
                P.dma("pool", h, st_cb[s["seq"]].rearrange("(k p) t -> p k t", p=128), "stin")
            else:
                P.copy(h, haloB[:, :, :], eng="pool")
        for j in range(5):
            (w,) = wget("B_in%d" % j)
            for jj in range(4):
                m = 4 * j + jj
                ps, _ = bank()
                P.mm([(ps[:, :Tb], [(w[:, k, 128 * jj:128 * jj + 128], hT[:, k, :Tb]) for k in range(8)])])
                if m < 10:
                    for i, s in enumerate(segs):
                        P.copy(xb[:, m, scol[i]:scol[i] + s["len"]], ps[:, s["off"]:s["off"] + s["len"]], eng="act")
                else:
                    P.act(gs[:, m - 10, :Tb], ps[:, :Tb], AF.Silu)
        o_w, _ = C128["b_wc"]
        for j in range(4):
            for m in range(10):
                for i, s in enumerate(segs):
                    c0 = scol[i] - 3
                    L = s["len"]
                    dst = xc[:, m, s["off"]:s["off"] + L]
                    wj = cst[:, o_w + m * 4 + j:o_w + m * 4 + j + 1]
                    if j == 0:
                        P.ts(dst, xb[:, m, c0:c0 + L], wj, c128("b_bc", m, 1), ALU.mult, ALU.add)
                    else:
                        P.stt(dst, xb[:, m, c0 + j:c0 + j + L], wj, dst, ALU.mult, ALU.add)
        P.copy(xcb[:, :, :Tb], xc[:, :, :Tb], eng="act")
        for i, s in enumerate(segs):
            last3 = xb[:, :, scol[i] + s["len"] - 3:scol[i] + s["len"]]
            if s["last"]:
                P.dma("pool", o_cb[s["seq"]].rearrange("(k p) t -> p k t", p=128), last3, "o_cb", slow=True)
            else:
                P.copy(haloB[:, :, :], last3, eng="pool")
        R = aal([128, 10, MAXB])
        I = aal([128, 10, MAXB])
        A1 = aal([128, 10, MAXB])
        wrg, wig = wget("B_gt")
        for n in range(10):
            ps, _ = bank()
            P.mm([(ps[:, :Tb], [(wrg[:, n, :], xcb[:, n, :Tb])])])
            P.act(R[:, n, :Tb], ps[:, :Tb], AF.Sigmoid, bias=c128("b_brg", n, 1))
            ps2, _ = bank()
            P.mm([(ps2[:, :Tb], [(wig[:, n, :], xcb[:, n, :Tb])])])
            P.act(I[:, n, :Tb], ps2[:, :Tb], AF.Sigmoid, bias=c128("b_big", n, 1))
        for n in range(10):
            P.ts(R[:, n, :Tb], R[:, n, :Tb], cder[:, n:n + 1], None, ALU.mult)
        P.act(A1[:, :, :Tb], R[:, :, :Tb], AF.Exp)
        P.act(R[:, :, :Tb], R[:, :, :Tb], AF.Exp, scale=2.0)
        P.act(R[:, :, :Tb], R[:, :, :Tb], AF.Sqrt, scale=-1.0, bias=1.0)
        P.tt(I[:, :, :Tb], I[:, :, :Tb], R[:, :, :Tb], ALU.mult)
        P.tt(I[:, :, :Tb], I[:, :, :Tb], xc[:, :, :Tb], ALU.mult)
        for i, s in enumerate(segs):
            if s["first"]:
                P.dma("pool", hstB[:, :], st_lru[s["seq"]].rearrange("(k p) -> p k", p=128), "stin", slow=True)
            o, L = s["off"], s["len"]
            for n in range(10):
                P.call("dve", "tensor_tensor_scan", R[:, n, o:o + L], data0=A1[:, n, o:o + L], data1=I[:, n, o:o + L],
                       initial=hstB[:, n:n + 1], op0=ALU.mult, op1=ALU.add)
            P.copy(hstB[:, :], R[:, :, o + L - 1])
            if s["last"]:
                P.dma("pool", o_lru[s["seq"]].rearrange("(k p) -> p k", p=128), hstB[:, :], "o_lru", slow=True)
        yb = xcb
        P.tt(yb[:, :, :Tb], R[:, :, :Tb], gs[:, :, :Tb], ALU.mult)
        outproj(Tb, ["B_o0", "B_o1", "B_o2"], [384, 384, 256], lambda k: yb[:, k, :Tb], 10)

    def layerD(blk):
        Tb = blk["Tb"]
        segs = blk["segs"]
        tok0 = blk["tok0"]
        areset()
        hT = aal([128, 8, MAXB], BF16)
        qk = aal([128, 16, MAXB])
        vT = aal([128, 16, MAXB], BF16)
        gs = aal([128, 16, MAXB], BF16)
        rope = aal([128, 6, MAXB])
        rmsnorm(Tb, "norm_g", 24, hT)
        P.dma("pool", rope[:, :, :Tb], rope_d[:, :, tok0:tok0 + Tb], "rope")
        for j in range(12):
            (w,) = wget("D_in%d" % j)
            for jj in range(4):
                m = 4 * j + jj
                ps, _ = bank()
                P.mm([(ps[:, :Tb], [(w[:, k, 128 * jj:128 * jj + 128], hT[:, k, :Tb]) for k in range(8)])])
                if m < 16:
                    P.copy(qk[:, m, :Tb], ps[:, :Tb], eng="act")
                elif m < 32:
                    P.copy(vT[:, m - 16, :Tb], ps[:, :Tb], eng="act")
                else:
                    P.act(gs[:, m - 32, :Tb], ps[:, :Tb], AF.Silu)
        qr = aal([128, 16, MAXB], BF16)
        qd = aal([128, 8, MAXB], BF16)
        t1 = aal([128, 8, MAXB])
        t2 = aal([128, 8, MAXB])
        x1 = bass.AP(qk.tensor, qk.offset, [list(qk.ap[0]), [2 * MAXB, 8], [1, Tb]])
        x2 = bass.AP(qk.tensor, qk.offset + MAXB, [list(qk.ap[0]), [2 * MAXB, 8], [1, Tb]])
        o1 = bass.AP(qr.tensor, qr.offset, [list(qr.ap[0]), [2 * MAXB, 8], [1, Tb]])
        o2 = bass.AP(qr.tensor, qr.offset + MAXB, [list(qr.ap[0]), [2 * MAXB, 8], [1, Tb]])
        cosb = bc(rope[:, 0, :Tb], 1, 8)
        sinb = bc(rope[:, 1, :Tb], 1, 8)
        P.tt(t1[:, :, :Tb], x1, cosb, ALU.mult)
        P.tt(t2[:, :, :Tb], x2, sinb, ALU.mult)
        P.tt(o1, t1[:, :, :Tb], t2[:, :, :Tb], ALU.subtract)
        P.tt(t1[:, :, :Tb], x1, sinb, ALU.mult)
        P.tt(t2[:, :, :Tb], x2, cosb, ALU.mult)
        P.tt(o2, t1[:, :, :Tb], t2[:, :, :Tb], ALU.add)
        for h in range(4):
            P.tt(qd[:, 2 * h:2 * h + 2, :Tb], qr[:, 2 * h:2 * h + 2, :Tb], bc(rope[:, 2 + h, :Tb], 1, 2), ALU.mult)
        oT = aal([128, 16, MAXB])
        vtok = [aal([64, 2048], BF16) for _ in range(2)]
        kdt = [aal([64, 1024], BF16) for _ in range(2)]
        scm = [aal([64, 256], BF16) for _ in range(2)]
        dchunks = blk["chunks"]

        def d_pre(ci):
            c = dchunks[ci]
            o, L = c["off"], c["L"]
            vt = vtok[ci % 2]
            kt = kdt[ci % 2]
            sm = scm[ci % 2]
            _, pb = bank2()
            P.tr([(pb[:L, 128 * e:128 * e + 128], vT[:, e, o:o + L]) for e in range(16)], ident_bf)
            P.copy(vt[:L, :], pb[:L, 0:2048], eng="act")
            _, pk = bank()
            P.tr([(pk[:L, 128 * e:128 * e + 128], qr[:, 8 + e, o:o + L]) for e in range(8)], ident_bf)
            kdc = kdec64 if L == 64 else kdec16
            for h in range(4):
                P.ts(kt[:L, 256 * h:256 * h + 256], pk[:L, 256 * h:256 * h + 256], kdc[:L, h:h + 1], None, ALU.mult)
            ps, _ = bank()
            P.mm([(ps[:L, 64 * h:64 * h + L], [(qr[:, 8 + 2 * h + dc, o:o + L], qr[:, 2 * h + dc, o:o + L]) for dc in range(2)])
                  for h in range(4)])
            P.tt(sm[:L, :].rearrange("p (h n) -> p h n", h=4)[:, :, :L], ps[:L, 0:256].rearrange("p (h n) -> p h n", h=4)[:, :, :L],
                 dmaskT[:L, :].rearrange("p (h n) -> p h n", h=4)[:, :, :L], ALU.mult)

        def d_main(ci):
            c = dchunks[ci]
            o, L, sq_ = c["off"], c["L"], c["seq"]
            vt = vtok[ci % 2]
            kt = kdt[ci % 2]
            sm = scm[ci % 2]
            if c["first"]:
                P.dma("pool", Sst[:, :, :], st_ret[sq_].rearrange("(c p) e -> p c e", p=128), "stin")
                P.copy(Sbf[:, :, :], Sst[:, :, :], eng="act")
            po, _ = bank2()
            grp = []
            for h in range(4):
                for ec in range(4):
                    col = (4 * h + ec) * 64
                    pairs = [(vt[:L, 512 * h + 128 * ec:512 * h + 128 * ec + 128], sm[:L, 64 * h:64 * h + L])]
                    for dc in range(2):
                        pairs.append((Sbf[:, 2 * h + dc, 128 * ec:128 * ec + 128], qd[:, 2 * h + dc, o:o + L]))
                    grp.append((po[:, col:col + L], pairs))
            P.mm(grp)
            P.copy(oT[:, :, o:o + L], po[:, 0:1024].rearrange("p (a n) -> p a n", a=16)[:, :, :L], eng="act")
            for h in range(4):
                gL = float((1.0 - 2.0 ** (-5 - h)) ** L)
                for dc in range(2):
                    pS, _ = bank()
                    P.mm([(pS[:, :], [(kt[:L, 256 * h + 128 * dc:256 * h + 128 * dc + 128], vt[:L, 512 * h:512 * h + 512])])])
                    P.stt(Sst[:, 2 * h + dc, :], Sst[:, 2 * h + dc, :], gL, pS[:, :], ALU.mult, ALU.add)
            P.copy(Sbf[:, :, :], Sst[:, :, :], eng="act")
            if c["last"]:
                P.dma("pool", o_ret[sq_].rearrange("(c p) e -> p c e", p=128), Sst[:, :, :], "o_ret")

        d_pre(0)
        for ci in range(len(dchunks)):
            if ci + 1 < len(dchunks):
                d_pre(ci + 1)
            d_main(ci)
        osq = qr
        P.act(osq[:, :, :Tb], oT[:, :, :Tb], AF.Square)
        rs = t1
        for h in range(4):
            ps, _ = bank()
            P.mm([(ps[:, :Tb], [(ones512, osq[:, 4 * h + ec, :Tb]) for ec in range(4)])])
            rsqrt(rs[:, h, :Tb], ps[:, :Tb], NORM_EPS)
        for c_ in range(16):
            P.stt(oT[:, c_, :Tb], oT[:, c_, :Tb], c128("d_gng", c_, 1), rs[:, c_ // 4, :Tb], ALU.mult, ALU.mult)
        yb = vT
        P.tt(yb[:, :, :Tb], oT[:, :, :Tb], gs[:, :, :Tb], ALU.mult)
        outproj(Tb, ["D_o0", "D_o1", "D_o2", "D_o3"], [256] * 4, lambda k: yb[:, k, :Tb], 16)

    def layerC(blk):
        Tb = blk["Tb"]
        segs = blk["segs"]
        areset()
        hf = aal([128, 8, MAXB + 1])
        xx = aal([128, 8, MAXB])
        xm = [aal([128, 8, MAXB], BF16) for _ in range(2)]
        lw = aal([64, MAXB], BF16)
        la = aal([64, MAXB], BF16)
        rmsnorm(Tb, "norm_g", 16, None, dstf=hf[:, :, 1:])
        prefix_end = aoff[0]
        s0 = segs[0]
        if s0["first"]:
            P.dma("pool", hf[:, :, 0], st_sh[s0["seq"]].rearrange("(k p) -> p k", p=128), "stin", slow=True)
        else:
            P.copy(hf[:, :, 0], shC[:, :], eng="pool")
        P.tt(xx[:, :, :Tb], hf[:, :, 0:Tb], hf[:, :, 1:Tb + 1], ALU.subtract)
        for i, s in enumerate(segs):
            if i > 0:
                P.dma("pool", shC[:, :], st_sh[s["seq"]].rearrange("(k p) -> p k", p=128), "stin", slow=True)
                P.tt(xx[:, :, s["off"]], shC[:, :], hf[:, :, 1 + s["off"]], ALU.subtract)
            if s["last"]:
                P.dma("pool", o_sh[s["seq"]].rearrange("(k p) -> p k", p=128), hf[:, :, s["off"] + s["len"]], "o_sh", slow=True)
        if not segs[-1]["last"]:
            P.copy(shC[:, :], hf[:, :, Tb], eng="pool")

        def drain(names):
            for n_ in names:
                wget(n_)

        C_ALL = ["C_lora"] + ["C_%s%d" % (nm, j) for nm in "rkvg" for j in range(2)] + ["C_o0", "C_o1"]
        if CSTOP == 0:
            drain(C_ALL)
            return
        o_mu, _ = C128["c_mu"]

        def mkxm(si):
            t = xm[si % 2]
            for k in range(8):
                P.stt(t[:, k, :Tb], xx[:, k, :Tb], cst[:, o_mu + si * 8 + k:o_mu + si * 8 + k + 1], hf[:, k, 1:Tb + 1], ALU.mult, ALU.add)
            return t

        rr = aal([128, 8, MAXB])
        vb = aal([128, 8, MAXB], BF16)
        gsl = aal([128, 8, MAXB], BF16)
        bon = aal([128, 8, MAXB], BF16)
        bh = aal([128, 8, MAXB], BF16)
        kh_ = aal([128, 8, MAXB], BF16)
        at = aal([128, 8, MAXB], BF16)
        rt = aal([128, 8, MAXB], BF16)
        pl = aal([128, 8, 8])
        dead_start = aoff[0]
        kk_ = aal([128, 8, MAXB])
        lwd = aal([128, 8, MAXB])
        av = aal([128, 8, MAXB])
        tA = aal([128, 8, MAXB])
        tB = aal([128, 8, MAXB])
        ex = aal([128, 8, MAXB])
        tbf = aal([128, 8, MAXB], BF16)
        dead_end = aoff[0]
        w1, a1, w2, a2 = wget("C_lora")
        x4 = mkxm(4)
        ps, _ = bank()
        P.mm([(ps[:64, :Tb], [(w1[:, k, :], x4[:, k, :Tb]) for k in range(8)])])
        P.act(lw[:, :Tb], ps[:64, :Tb], AF.Tanh)
        x5 = mkxm(5)
        ps, _ = bank()
        P.mm([(ps[:64, :Tb], [(a1[:, k, :], x5[:, k, :Tb]) for k in range(8)])])
        P.copy(la[:, :Tb], ps[:64, :Tb], eng="act")
        for p in range(8):
            ps, _ = bank()
            P.mm([(ps[:, :Tb], [(w2[:, 0, 128 * p:128 * p + 128], lw[:, :Tb])])])
            P.act(lwd[:, p, :Tb], ps[:, :Tb], AF.Sigmoid, bias=c128("c_w0", p, 1))
            ps2, _ = bank()
            P.mm([(ps2[:, :Tb], [(a2[:, 0, 128 * p:128 * p + 128], la[:, :Tb])])])
            P.act(av[:, p, :Tb], ps2[:, :Tb], AF.Sigmoid, bias=c128("c_a0", p, 1))
        P.ts(lwd[:, :, :Tb], lwd[:, :, :Tb], -float(np.exp(-0.5)), None, ALU.mult)
        if CSTOP == 1:
            drain(C_ALL[1:])
            return

        def proj(si, nm, evac):
            x = mkxm(si)
            for j in range(2):
                (w,) = wget("C_%s%d" % (nm, j))
                for jj in range(4):
                    p = 4 * j + jj
                    ps, _ = bank()
                    P.mm([(ps[:, :Tb], [(w[:, k, 128 * jj:128 * jj + 128], x[:, k, :Tb]) for k in range(8)])])
                    evac(p, ps)

        proj(0, "r", lambda p, ps: P.copy(rr[:, p, :Tb], ps[:, :Tb], eng="act"))
        proj(1, "k", lambda p, ps: P.copy(kk_[:, p, :Tb], ps[:, :Tb], eng="act"))
        proj(2, "v", lambda p, ps: P.copy(vb[:, p, :Tb], ps[:, :Tb], eng="act"))
        proj(3, "g", lambda p, ps: P.act(gsl[:, p, :Tb], ps[:, :Tb], AF.Silu))

        if CSTOP == 2:
            drain(C_ALL[9:])
            return
        for p in range(8):
            P.ts(tA[:, p, :Tb], kk_[:, p, :Tb], c128("c_kk", p, 1), None, ALU.mult)
        P.act(tbf[:, :, :Tb], tA[:, :, :Tb], AF.Square)
        for p in range(8):
            ps, _ = bank()
            P.mm([(ps[:, :Tb], [(bd1, tbf[:, p, :Tb])])])
            P.ts(tB[:, p, :Tb], ps[:, :Tb], 1e-24, None, ALU.max)
            P.act(tB[:, p, :Tb], tB[:, p, :Tb], AF.Ln)
            P.act(tB[:, p, :Tb], tB[:, p, :Tb], AF.Exp, scale=-0.5)
        P.tt(tA[:, :, :Tb], tA[:, :, :Tb], tB[:, :, :Tb], ALU.mult)
        for p in range(8):
            P.ts(tB[:, p, :Tb], av[:, p, :Tb], c128("c_ka", p, 1), cder[:, 16 + p:17 + p], ALU.mult, ALU.add)
        P.tt(kk_[:, :, :Tb], kk_[:, :, :Tb], tB[:, :, :Tb], ALU.mult)
        P.tt(tB[:, :, :Tb], rr[:, :, :Tb], kk_[:, :, :Tb], ALU.mult)
        for p in range(8):
            P.ts(tbf[:, p, :Tb], tB[:, p, :Tb], c128("c_rk", p, 1), None, ALU.mult)
        for p in range(8):
            ps, _ = bank()
            P.mm([(ps[:, :Tb], [(bd1, tbf[:, p, :Tb])])])
            P.tt(bon[:, p, :Tb], ps[:, :Tb], vb[:, p, :Tb], ALU.mult)
        for c in blk["chunks"]:
            o, L = c["off"], c["L"]
            for p in range(8):
                P.call("dve", "tensor_tensor_scan", tB[:, p, o:o + L], data0=onesf[:, :L], data1=lwd[:, p, o:o + L],
                       initial=0.0, op0=ALU.mult, op1=ALU.add)
        P.act(ex[:, :, :Tb], tB[:, :, :Tb], AF.Exp, scale=-1.0)
        P.tt(av[:, :, :Tb], av[:, :, :Tb], tA[:, :, :Tb], ALU.mult)
        P.tt(bh[:, :, :Tb], av[:, :, :Tb], ex[:, :, :Tb], ALU.mult)
        P.tt(kh_[:, :, :Tb], kk_[:, :, :Tb], ex[:, :, :Tb], ALU.mult)
        P.act(ex[:, :, :Tb], tB[:, :, :Tb], AF.Exp)
        P.tt(rt[:, :, :Tb], rr[:, :, :Tb], ex[:, :, :Tb], ALU.mult)
        for ci, c in enumerate(blk["chunks"]):
            P.copy(pl[:, :, ci], ex[:, :, c["off"] + c["L"] - 1], eng="pool")
        P.tt(tB[:, :, :Tb], tB[:, :, :Tb], lwd[:, :, :Tb], ALU.subtract)
        P.act(ex[:, :, :Tb], tB[:, :, :Tb], AF.Exp)
        P.stt(at[:, :, :Tb], tA[:, :, :Tb], -1.0, ex[:, :, :Tb], ALU.mult, ALU.mult)

        if CSTOP == 3:
            drain(C_ALL[9:])
            return
        end_all = aoff[0]
        oTc = rr
        chunks = blk["chunks"]
        aoff[0] = dead_start
        FtL = [[aal([128, 8, 64], BF16) for _ in range(6)] for _ in range(2)]
        assert aoff[0] <= dead_end, (aoff[0], dead_end)
        aoff[0] = 0
        tkB = [aal([128, 8, 64], BF16) for _ in range(2)]
        tkK = [aal([128, 8, 64], BF16) for _ in range(2)]
        tkV = [aal([128, 8, 64], BF16) for _ in range(2)]
        sArb = [aal([128, 8, 64], BF16) for _ in range(2)]
        sArk = [aal([128, 8, 64], BF16) for _ in range(2)]
        sAak = [aal([128, 8, 64], BF16) for _ in range(2)]
        sAab = aal([128, 8, 64], BF16)
        Nn = [aal([128, 8, 64], BF16) for _ in range(2)]
        Nt = [aal([128, 8, 64], BF16) for _ in range(2)]
        Xb = [aal([128, 8, 64], BF16) for _ in range(2)]
        assert aoff[0] <= prefix_end, (aoff[0], prefix_end)
        aoff[0] = end_all

        def HS(t3, p, h2, c0, c1):
            return t3[64 * h2:64 * h2 + 64, p, c0:c1]

        def headmm(rows, ncols, pairs_fn):
            banks = [bank()[0], bank()[0]]
            groups = []
            for h2 in range(2):
                for p in range(8):
                    groups.append((banks[h2][64 * h2:64 * h2 + rows, 64 * p:64 * p + ncols], pairs_fn(p, h2)))
            P.mm(groups)
            return banks

        def pview(banks, h2, rows, ncols):
            return banks[h2][64 * h2:64 * h2 + rows, 0:512].rearrange("p (a n) -> p a n", a=8)[:, :, :ncols]

        def RLf(t3, p, h2, L, c1=None):
            return t3[64 * h2:64 * h2 + L, p, :c1] if c1 is not None else t3[64 * h2:64 * h2 + L, p, :]

        def pre_steps(ci):
            c = chunks[ci]
            o, L = c["off"], c["L"]
            q = ci % 2
            nlev = 6 if L == 64 else 4
            steps = []

            def tstep(src, dstv):
                def f():
                    bk = [bank()[1], bank()[1]]
                    items = []
                    for h2 in range(2):
                        for p in range(8):
                            items.append((bk[h2][64 * h2:64 * h2 + L, 64 * p:64 * p + 64], src[64 * h2:64 * h2 + 64, p, o:o + L],
                                          ident_bf[64 * h2:64 * h2 + 64, 64 * h2:64 * h2 + 64]))
                    P.tr3(items)
                    for h2 in range(2):
                        P.copy(dstv[64 * h2:64 * h2 + L, :, :], bk[h2][64 * h2:64 * h2 + L, 0:512].rearrange("p (a n) -> p a n", a=8), eng="act")
                return f

            for src, dstv in ((bh, tkB[q]), (kh_, tkK[q]), (vb, tkV[q])):
                steps.append(tstep(src, dstv))

            def sstep(lhs, rhs, dst, msk):
                def f():
                    bks = headmm(L, L, lambda p, h2: [(HS(lhs, p, h2, o, o + L), HS(rhs, p, h2, o, o + L))])
                    for h2 in range(2):
                        P.tt(dst[64 * h2:64 * h2 + L, :, :L], pview(bks, h2, L, L), bc(msk[64 * h2:64 * h2 + L, :L], 1, 8), ALU.mult)
                return f

            for lhs, rhs, dst, msk in ((bh, at, sAab, triS), (at, bh, Nn[0], triL), (kh_, at, sAak[q], triS),
                                       (bh, rt, sArb[q], triI), (kh_, rt, sArk[q], triI)):
                steps.append(sstep(lhs, rhs, dst, msk))

            def f0():
                for h2 in range(2):
                    R_ = slice(64 * h2, 64 * h2 + L)
                    P.copy(Nt[0][R_, :, :L], sAab[R_, :, :L], eng="pool")
                    P.tt(FtL[q][0][R_, :, :L], sAab[R_, :, :L], bc(identf[R_, :L], 1, 8), ALU.add, eng="pool")
            steps.insert(5, f0)

            def qstep(lv):
                def f():
                    a_, b_ = lv % 2, (lv + 1) % 2
                    b1 = headmm(L, L, lambda p, h2: [(RLf(Nt[a_], p, h2, L, L), RLf(Nn[a_], p, h2, L, L))])
                    b2 = headmm(L, L, lambda p, h2: [(RLf(Nn[a_], p, h2, L, L), RLf(Nt[a_], p, h2, L, L))])
                    for h2 in range(2):
                        R_ = slice(64 * h2, 64 * h2 + L)
                        if lv < nlev - 2:
                            P.copy(Nn[b_][R_, :, :L], pview(b1, h2, L, L))
                            P.copy(Nt[b_][R_, :, :L], pview(b2, h2, L, L), eng="act")
                        P.tt(FtL[q][lv + 1][R_, :, :L], pview(b2, h2, L, L), bc(identf[R_, :L], 1, 8), ALU.add)
                return f

            for lv in range(nlev - 1):
                steps.append(qstep(lv))
            return steps

        def main_steps(ci):
            c = chunks[ci]
            o, L, sq_ = c["off"], c["L"], c["seq"]
            q = ci % 2
            nlev = 6 if L == 64 else 4
            steps = []

            def x0():
                if c["first"]:
                    P.dma("pool", Mst[:, :, :], st_wkv[sq_], "stin")
                    P.copy(Mbf[:, :, :], Mst[:, :, :], eng="act")
                bks = headmm(L, 64, lambda p, h2: [(HS(at, p, h2, o, o + L), Mbf[64 * h2:64 * h2 + 64, p, :]),
                                                   (RLf(sAak[q], p, h2, L, L), RLf(tkV[q], p, h2, L))])
                for h2 in range(2):
                    P.copy(Xb[0][64 * h2:64 * h2 + L, :, :], pview(bks, h2, L, 64), eng="act")
            steps.append(x0)

            def xstep(lv):
                def f():
                    a_, b_ = lv % 2, (lv + 1) % 2
                    bks = headmm(L, 64, lambda p, h2: [(RLf(FtL[q][lv], p, h2, L, L), RLf(Xb[a_], p, h2, L))])
                    for h2 in range(2):
                        P.copy(Xb[b_][64 * h2:64 * h2 + L, :, :], pview(bks, h2, L, 64), eng="act")
                return f

            for lv in range(nlev):
                steps.append(xstep(lv))
            Ub = Xb[nlev % 2]

            def ostep():
                bks = headmm(64, L, lambda p, h2: [(Mbf[64 * h2:64 * h2 + 64, p, :], HS(rt, p, h2, o, o + L)),
                                                   (RLf(Ub, p, h2, L), RLf(sArb[q], p, h2, L, L)),
                                                   (RLf(tkV[q], p, h2, L), RLf(sArk[q], p, h2, L, L))])
                for h2 in range(2):
                    P.copy(oTc[64 * h2:64 * h2 + 64, :, o:o + L], pview(bks, h2, 64, L), eng="act")
            steps.append(ostep)

            def ststep():
                bks = headmm(64, 64, lambda p, h2: [(RLf(tkB[q], p, h2, L), RLf(Ub, p, h2, L)), (RLf(tkK[q], p, h2, L), RLf(tkV[q], p, h2, L))])
                for h2 in range(2):
                    R_ = slice(64 * h2, 64 * h2 + 64)
                    P.tt(Mst[R_, :, :], Mst[R_, :, :], pview(bks, h2, 64, 64), ALU.add)
                for p in range(8):
                    P.ts(Mst[:, p, :], Mst[:, p, :], pl[:, p, ci:ci + 1], None, ALU.mult)
                P.copy(Mbf[:, :, :], Mst[:, :, :], eng="act")
                if c["last"]:
                    P.dma("pool", o_wkv[sq_], Mst[:, :, :], "o_wkv")
            steps.append(ststep)
            return steps

        for f in pre_steps(0):
            f()
        for ci in range(len(chunks)):
            ms = main_steps(ci)
            pr = pre_steps(ci + 1) if ci + 1 < len(chunks) else []
            for i in range(max(len(ms), len(pr))):
                if i < len(pr):
                    pr[i]()
                if i < len(ms):
                    ms[i]()

        if CSTOP == 4 or 30 < CSTOP < 40:
            drain(C_ALL[9:])
            return
        ob = tbf
        osq = bh
        P.copy(ob[:, :, :Tb], oTc[:, :, :Tb], eng="act")
        P.act(osq[:, :, :Tb], oTc[:, :, :Tb], AF.Square)
        yC = at
        for p in range(8):
            pm, _ = bank()
            pq, _ = bank()
            P.mm([(pm[:, :Tb], [(bd64, ob[:, p, :Tb])])])
            P.mm([(pq[:, :Tb], [(bd64, osq[:, p, :Tb])])])
            mu = tA[:, p, :Tb]
            vr = tB[:, p, :Tb]
            P.copy(mu, pm[:, :Tb], eng="act")
            P.tt(vr, mu, mu, ALU.mult)
            P.tt(vr, pq[:, :Tb], vr, ALU.subtract)
            rsqrt(vr, vr, GN_EPS)
            P.tt(oTc[:, p, :Tb], oTc[:, p, :Tb], mu, ALU.subtract)
            P.tt(oTc[:, p, :Tb], oTc[:, p, :Tb], vr, ALU.mult)
            P.ts(oTc[:, p, :Tb], oTc[:, p, :Tb], c128("c_gng", p, 1), c128("c_gnb", p, 1), ALU.mult, ALU.add)
            P.tt(oTc[:, p, :Tb], oTc[:, p, :Tb], bon[:, p, :Tb], ALU.add)
            P.tt(yC[:, p, :Tb], oTc[:, p, :Tb], gsl[:, p, :Tb], ALU.mult)
        outproj(Tb, ["C_o0", "C_o1"], [512, 512], lambda k: yC[:, k, :Tb], 8)

    layers = [layerA, layerB, layerC, layerD][:NL]
    for blk in plan:
        Tb = blk["Tb"]
        P.dma("pool", xT[:, :, :Tb], xin.rearrange("(k p) t -> p k t", p=128)[:, :, blk["tok0"]:blk["tok0"] + Tb], "xin")
        for lf in layers:
            lf(blk)
        aoff[0] = ARENA_BYTES - (8 * MAXB * 4 + 8 * MAXB * 2 + MAXB * 4 + 256)
        yo = aal([128, 8, MAXB])
        rmsnorm(Tb, "fin_g", 0, None, dstf=yo)
        P.dma("pool", yT.rearrange("(k p) t -> p k t", p=128)[:, :, blk["tok0"]:blk["tok0"] + Tb], yo[:, :, :Tb], "yout")
    P.emit()
    return nc, P


def host_tables(plan, NT, Tp):
    kt = np.zeros((128, KTW), np.float32)
    kt[:, 0:128] = np.eye(128, dtype=np.float32)
    kt[:, 128:256] = 1.0 / 1024
    kt[:, 256:384] = 1.0 / 512
    bd = np.zeros((128, 128), np.float32)
    bd[:64, :64] = 1.0
    bd[64:, 64:] = 1.0
    kt[:, 384:512] = bd / 64.0
    kt[:, 512:640] = bd
    s = np.arange(64)[:, None]
    t = np.arange(64)[None, :]
    kt[:64, 640:704] = (t > s)
    kt[:64, 704:768] = (t >= s)
    kt[:64, 768:832] = (s > t)
    kt[:64, 832:896] = np.eye(64)
    gam = (1.0 - 2.0 ** (-5.0 - np.arange(4))).astype(np.float64)
    for h in range(4):
        dm = np.where(t >= s, gam[h] ** np.maximum(t - s, 0), 0.0) / 16.0
        kt[:64, 896 + 64 * h:896 + 64 * h + 64] = dm
        m = np.arange(64)
        kt[:64, 1152 + h] = gam[h] ** (63.0 - m)
        kt[:64, 1156 + h] = gam[h] ** np.maximum(15.0 - m, 0.0)
    kt[64:128, 640:896] = kt[0:64, 640:896]
    kt[:, 1160:1224] = 1.0
    return kt


def _prep(inputs):
    x_prompt = np.asarray(inputs["x_prompt"], np.float32)
    x_sample = np.asarray(inputs["x_sample"], np.float32)
    B, S, _ = x_prompt.shape
    Tp = S + NMETA
    plan, NT = make_plan(Tp)
    meta = np.asarray(inputs["meta_tokens"], np.float32)

    def pk(v, p):
        v = np.asarray(v, np.float32).reshape(-1, p)
        return np.ascontiguousarray(v.T)

    c128 = np.zeros((128, NC128), np.float32)

    def put(name, arr):
        o, w = C128[name]
        assert arr.shape == (128, w), (name, arr.shape)
        c128[:, o:o + w] = arr

    put("norm_g", np.concatenate([pk(inputs["norm_g"][i], 128) for i in range(4)], axis=1))
    put("fin_g", pk(inputs["final_norm_g"], 128))
    wdw = np.asarray(inputs["a_w_dw"], np.float32)
    put("a_wdw", np.ascontiguousarray(wdw.reshape(31, 8, 128).transpose(2, 1, 0)).reshape(128, 248))
    put("a_bdw", pk(inputs["a_b_dw"], 128))
    put("a_lng", pk(inputs["a_ln_g"], 128))
    put("a_lnb", pk(inputs["a_ln_b"], 128))
    wc = np.asarray(inputs["b_w_conv"], np.float32)
    put("b_wc", np.ascontiguousarray(wc.reshape(4, 10, 128).transpose(2, 1, 0)).reshape(128, 40))
    put("b_bc", pk(inputs["b_b_conv"], 128))
    put("b_brg", pk(inputs["b_b_rg"], 128))
    put("b_big", pk(inputs["b_b_ig"], 128))
    put("b_lam", pk(inputs["b_lam"], 128))
    mu = np.asarray(inputs["c_mu"], np.float32)
    put("c_mu", np.ascontiguousarray(mu.reshape(6, 8, 128).transpose(2, 0, 1)).reshape(128, 48))
    put("d_gng", pk(inputs["d_gn_g"], 128))
    for name, key in [("c_w0", "c_w0"), ("c_a0", "c_a0"), ("c_kk", "c_k_k"), ("c_ka", "c_k_a"), ("c_rk", "c_r_k"),
                      ("c_gng", "c_gn_g"), ("c_gnb", "c_gn_b")]:
        put(name, pk(np.asarray(inputs[key]).reshape(-1), 128))
    c64 = None

    kt = host_tables(plan, NT, Tp)
    pos = np.zeros(NT, np.float32)
    nin = np.zeros(NT, np.float64)
    for blk in plan:
        for c in blk["chunks"]:
            g0 = c["gtok"]
            L = c["L"]
            if c["seq"] == 0:
                p0 = g0
            else:
                p0 = NMETA + PAST
            pos[g0:g0 + L] = p0 + np.arange(L)
            nin[g0:g0 + L] = np.arange(L)
    half = 128
    inv = (10000.0 ** (-np.arange(half, dtype=np.float32) / half)).astype(np.float32)
    ang = pos[None, :].astype(np.float32) * inv[:, None]
    rope = np.zeros((128, 6, NT), np.float32)
    rope[:, 0, :] = np.cos(ang)
    rope[:, 1, :] = np.sin(ang)
    gam = (1.0 - 2.0 ** (-5.0 - np.arange(4))).astype(np.float64)
    for h in range(4):
        rope[:, 2 + h, :] = (gam[h] ** (nin + 1.0) / 16.0)[None, :]
    return plan, NT, Tp, c128, c64, kt, rope, meta, x_prompt, x_sample


_CACHE = {}
_SIM = None
CSTOP = 99
FILLER = 0


def kernel(**inputs):
    plan, NT, Tp, c128, c64, kt, rope, meta, x_prompt, x_sample = _prep(inputs)
    NL = int(inputs.pop("_NL", 4)) if "_NL" in inputs else 4
    key = (Tp, NL)
    if key not in _CACHE:
        _CACHE[key] = build(Tp, NL)
    nc, _ = _CACHE[key]
    B = x_prompt.shape[0]
    f32 = lambda a: np.ascontiguousarray(np.asarray(a, np.float32))
    ktf = kt.copy()
    in_maps = []
    for core in range(8):
        b = core % B
        sidx = [2 * (core % 4), 2 * (core % 4) + 1]
        xs = np.concatenate([meta, x_prompt[b], x_sample[sidx[0]], x_sample[sidx[1]]], axis=0)
        m = {"xin": f32(xs.T), "cst128": c128, "ktab": ktf, "rope": rope}

        def st3(samp, zshape, tr):
            z = np.zeros(zshape, np.float32)
            return f32(np.stack([z, tr(samp[sidx[0]]), tr(samp[sidx[1]])], axis=0))

        m["st_ca"] = st3(np.asarray(inputs["cache_conv_a"], np.float32), (D, 30), lambda a: a.T)
        m["st_cb"] = st3(np.asarray(inputs["cache_conv_b"], np.float32), (DRNN, 3), lambda a: a.T)
        m["st_lru"] = st3(np.asarray(inputs["state_lru_b"], np.float32), (DRNN,), lambda a: a)
        m["st_sh"] = st3(np.asarray(inputs["state_shift_c"], np.float32), (D,), lambda a: a)
        m["st_wkv"] = st3(np.asarray(inputs["state_wkv_c"], np.float32), (128, 8, 64),
                           lambda a: a.reshape(8, 2, 64, 64).transpose(1, 3, 0, 2).reshape(128, 8, 64))
        m["st_ret"] = st3(np.asarray(inputs["state_ret_d"], np.float32), (1024, 512), lambda a: a.reshape(1024, 512))
        for n in WSHAPES:
            m[n] = f32(inputs[n])
        in_maps.append(m)
    if "_SIM" in globals() and _SIM is not None:
        res = _SIM(nc, in_maps)
    else:
        res = run_bass_kernel_spmd(nc, in_maps, core_ids=list(range(8)))
    R = res.results
    S = x_prompt.shape[1]
    y_prompt = np.stack([R[b]["yT"][:, NMETA:Tp].T for b in range(B)], axis=0)
    y_sample = np.stack([R[i // 2]["yT"][:, Tp + 64 * (i % 2):Tp + 64 * (i % 2) + 64].T for i in range(8)], axis=0)

    def outs(name, tr):
        p = np.stack([tr(R[b][name][0]) for b in range(B)], axis=0)
        s = np.stack([tr(R[i // 2][name][1 + i % 2]) for i in range(8)], axis=0)
        return np.ascontiguousarray(p, np.float32), np.ascontiguousarray(s, np.float32)

    ca_p, ca_s = outs("o_ca", lambda a: a.T)
    cb_p, cb_s = outs("o_cb", lambda a: a.T)
    lr_p, lr_s = outs("o_lru", lambda a: a)
    sh_p, sh_s = outs("o_sh", lambda a: a)
    wk_p, wk_s = outs("o_wkv", lambda a: a.reshape(2, 64, 8, 64).transpose(2, 0, 3, 1).reshape(16, 64, 64))
    rt_p, rt_s = outs("o_ret", lambda a: a.reshape(4, 256, 512))
    return (np.ascontiguousarray(y_prompt, np.float32), np.ascontiguousarray(y_sample, np.float32),
            ca_p, ca_s, cb_p, cb_s, lr_p, lr_s, sh_p, sh_s, wk_p, wk_s, rt_p, rt_s)
```

```python
import numpy as np
from contextlib import ExitStack
import concourse.bass as bass
import concourse.mybir as mybir
from concourse.bass_utils import run_bass_kernel_spmd

F32 = mybir.dt.float32
BF16 = mybir.dt.bfloat16
AF = mybir.ActivationFunctionType
ALU = mybir.AluOpType
AX = mybir.AxisListType

D = 1024
NMETA = 16
PAST = 1024
DRNN = 1280
NORM_EPS = 1e-6
LN_EPS = 1e-5
GN_EPS = 64e-5
SLOT = 4096
NSLOT = 5
ARENA_BYTES = 120 * 1024
MAXB = 256
KTW = 1280


def _esz(dt):
    return 2 if dt == BF16 else 4


class Prog:
    ENGS = ["pe", "act", "dve", "pool", "sp"]

    def __init__(self, nc, es):
        self.nc = nc
        self.es = es
        self.semobj = {}
        self.cnt = {}
        self.ops = {e: [] for e in self.ENGS}
        self.waited = {e: {} for e in self.ENGS}
        self.recs = {}
        self.tracked_dram = set()
        for e in self.ENGS:
            self.semobj["e_" + e] = es.enter_context(nc.semaphore("s_" + e))
            self.cnt["e_" + e] = 0
        self.nops = 0
        self.pe_filler = None

    def rng(self, ap):
        t = ap.tensor
        name = t.name
        tn = type(t).__name__
        if "DRam" in tn:
            if name in self.tracked_dram:
                return (name, 0, 1 << 60)
            return None
        pairs = ap.ap
        row = pairs[0][0]
        esz = _esz(ap.dtype)
        off = ap.offset % row if row > 0 else ap.offset
        ext = sum((c - 1) * abs(s) for s, c in pairs[1:]) + 1
        lo, hi = off * esz, (off + ext) * esz
        if "PSum" in tn:
            lo = (lo // 2048) * 2048
            hi = ((hi + 2047) // 2048) * 2048
            return ("@" + name, lo, hi)
        return (name, lo, hi)

    def op(self, eng, fn, reads, writes, dma_key=None):
        need = {}
        accs = [(self.rng(a), "r") for a in reads] + [(self.rng(a), "w") for a in writes]
        accs = [(r, "w" if (r is not None and r[0].startswith("@")) else k) for r, k in accs]
        for r, kind in accs:
            if r is None:
                continue
            name, lo, hi = r
            for (lo2, hi2, k2, s2), v2 in self.recs.get(name, {}).items():
                if lo2 < hi and lo < hi2 and (kind == "w" or k2 == "w"):
                    if need.get(s2, 0) < v2:
                        need[s2] = v2
        if dma_key is None:
            sk = "e_" + eng
            self.cnt[sk] += 1
            inc = 1
        else:
            sk = "d_" + dma_key + "_" + eng
            if sk not in self.semobj:
                self.semobj[sk] = self.es.enter_context(self.nc.semaphore("s_" + dma_key + "_" + eng))
                self.cnt[sk] = 0
            if self.cnt[sk] > 0:
                need[sk] = max(need.get(sk, 0), self.cnt[sk])
            self.cnt[sk] += 16
            inc = 16
        tokv = self.cnt[sk]
        waits = []
        for s, v in need.items():
            if s == "e_pe" and eng == "pe":
                continue
            if s.startswith("d_"):
                v = self.cnt[s] if s != sk else v
            if self.waited[eng].get(s, 0) >= v:
                continue
            self.waited[eng][s] = v
            waits.append((self.semobj[s], v))
        self.ops[eng].append((waits, fn, (self.semobj[sk], inc), self.pe_filler if eng == "pe" else None))
        self.nops += 1
        for r, kind in accs:
            if r is None:
                continue
            name, lo, hi = r
            d = self.recs.setdefault(name, {})
            if kind == "w":
                for key in [k for k in d if lo <= k[0] and k[1] <= hi]:
                    del d[key]
            d[(lo, hi, kind, sk)] = tokv

    def call(self, eng, method, out, **kw):
        reads = [v for v in kw.values() if isinstance(v, bass.AP)]
        writes = [out]

        def fn(e, method=method, out=out, kw=kw):
            return getattr(e, method)(out=out, **kw)

        self.op(eng, fn, reads, writes)

    def mm(self, groups):
        reads = []
        writes = []
        for out, pairs in groups:
            writes.append(out)
            for l, r in pairs:
                reads.append(l)
                reads.append(r)

        def fn(e, groups=groups):
            ins = None
            for out, pairs in groups:
                n = len(pairs)
                for i, (l, r) in enumerate(pairs):
                    ins = e.matmul(out, l, r, start=(i == 0), stop=(i == n - 1))
            return ins

        self.op("pe", fn, reads, writes)

    def tr(self, items, ident):
        reads = [ident] + [i for _, i in items]
        writes = [o for o, _ in items]

        def fn(e, items=items, ident=ident):
            ins = None
            for o, i in items:
                k = i.shape[0]
                ins = e.transpose(o, i, ident[:k, :k])
            return ins

        self.op("pe", fn, reads, writes)

    def tr3(self, items):
        reads = [i for _, i, _ in items] + [items[0][2]]
        writes = [o for o, _, _ in items]

        def fn(e, items=items):
            ins = None
            for o, i, idn in items:
                ins = e.transpose(o, i, idn)
            return ins

        self.op("pe", fn, reads, writes)

    def dma(self, eng, out, in_, key, slow=False):
        def fn(e, out=out, in_=in_):
            if slow:
                return e.dma_start(out=out, in_=in_, allow_slow_non_contiguous=True)
            return e.dma_start(out=out, in_=in_)

        self.op(eng, fn, [in_], [out], dma_key=key)

    def memset(self, eng, ap, val):
        def fn(e, ap=ap, val=val):
            return e.memset(ap, val)

        self.op(eng, fn, [], [ap])

    def act(self, out, in_, func, **kw):
        self.call("act", "activation", out, in_=in_, func=func, **kw)

    def tt(self, out, in0, in1, op, eng="dve"):
        self.call(eng, "tensor_tensor", out, in0=in0, in1=in1, op=op)

    def ts(self, out, in0, s1, s2, op0, op1=None, eng="dve"):
        if op1 is None:
            self.call(eng, "tensor_scalar", out, in0=in0, scalar1=s1, scalar2=None, op0=op0)
        else:
            self.call(eng, "tensor_scalar", out, in0=in0, scalar1=s1, scalar2=s2, op0=op0, op1=op1)

    def stt(self, out, in0, scalar, in1, op0, op1):
        self.call("dve", "scalar_tensor_tensor", out, in0=in0, scalar=scalar, in1=in1, op0=op0, op1=op1)

    def copy(self, out, in_, eng="dve"):
        if eng == "act":
            self.call("act", "copy", out, in_=in_)
        else:
            self.call(eng, "tensor_copy", out, in_=in_)

    def emit(self):
        nc = self.nc
        final = [(self.semobj[s], self.cnt[s]) for s in self.semobj if s.startswith("d_") and self.cnt[s] > 0]
        ops = self.ops

        def run(name, e):
            for waits, fn, inc, post in ops[name]:
                for s, v in waits:
                    e.wait_ge(s, v)
                ins = fn(e)
                ins.then_inc(inc[0], inc[1])
                if post is not None:
                    post(e)

        with nc.Block() as block:

            @block.tensor
            def _(e):
                run("pe", e)

            @block.scalar
            def _(e):
                run("act", e)

            @block.vector
            def _(e):
                run("dve", e)

            @block.gpsimd
            def _(e):
                run("pool", e)

            @block.sync
            def _(e):
                run("sp", e)
                for s, v in final:
                    e.wait_ge(s, v)


def bc(ap, pos, n):
    pairs = [list(p) for p in ap.ap]
    pairs.insert(pos, [0, n])
    return bass.AP(ap.tensor, ap.offset, pairs)


def make_plan(Tp):
    seqs = [[NMETA] + [64] * ((Tp - NMETA) // 64), [64], [64]]
    blocks = []
    cur = []
    cur_n = 0
    tok = 0
    for s, chs in enumerate(seqs):
        for ci, L in enumerate(chs):
            if cur_n + L > MAXB:
                blocks.append(cur)
                cur = []
                cur_n = 0
            cur.append(dict(seq=s, L=L, off=cur_n, first=(ci == 0), last=(ci == len(chs) - 1), gtok=tok, ci=ci))
            cur_n += L
            tok += L
    blocks.append(cur)
    out = []
    for b in blocks:
        segs = []
        for c in b:
            if segs and segs[-1]["seq"] == c["seq"]:
                segs[-1]["chunks"].append(c)
                segs[-1]["len"] += c["L"]
                segs[-1]["last"] = c["last"]
            else:
                segs.append(dict(seq=c["seq"], off=c["off"], len=c["L"], chunks=[c], first=c["first"], last=c["last"]))
        out.append(dict(Tb=sum(c["L"] for c in b), tok0=b[0]["gtok"], segs=segs, chunks=b))
    return out, tok


C128 = {}
_o = 0
for _n, _w in [("norm_g", 32), ("fin_g", 8), ("a_wdw", 8 * 31), ("a_bdw", 8), ("a_lng", 8), ("a_lnb", 8),
               ("b_wc", 40), ("b_bc", 10), ("b_brg", 10), ("b_big", 10), ("b_lam", 10), ("c_mu", 48), ("d_gng", 16),
               ("c_w0", 8), ("c_a0", 8), ("c_kk", 8), ("c_ka", 8), ("c_rk", 8), ("c_gng", 8), ("c_gnb", 8)]:
    C128[_n] = (_o, _w)
    _o += _w
NC128 = _o
C64 = {}
_o = 0
for _n, _w in [("w0", 16), ("a0", 16), ("kk", 16), ("ka", 16), ("rk", 16), ("gng", 16), ("gnb", 16)]:
    C64[_n] = (_o, _w)
    _o += _w
NC64 = _o


def weight_units():
    U = []

    def colslab(src, c0, nc_, nk=8):
        return (src, 128, nk, nc_, lambda w, c0=c0, nc_=nc_: w.rearrange("(k p) n -> p k n", p=128)[:, :, c0:c0 + nc_])

    for j in range(4):
        U.append(("A_in%d" % j, [colslab("a_w_in", 1024 + 256 * j, 256), colslab("a_w_in", 256 * j, 256)]))
    for j in range(2):
        U.append(("A_g%d" % j, [colslab("a_w_in", 2048 + 512 * j, 512)]))
    for j in range(2):
        U.append(("A_o%d" % j, [colslab("a_w_out", 512 * j, 512)]))
    for j in range(5):
        U.append(("B_in%d" % j, [colslab("b_w_in", 512 * j, 512)]))
    gv = lambda w: w.rearrange("n k j -> k n j")
    U.append(("B_gt", [("b_w_rg", 128, 10, 128, gv), ("b_w_ig", 128, 10, 128, gv)]))
    for j, (c0, n_) in enumerate([(0, 384), (384, 384), (768, 256)]):
        U.append(("B_o%d" % j, [colslab("b_w_out", c0, n_, nk=10)]))
    U.append(("C_lora", [colslab("c_w1", 0, 64), colslab("c_a1", 0, 64),
                         ("c_w2", 64, 1, 1024, lambda w: w.rearrange("(o p) n -> p o n", o=1)),
                         ("c_a2", 64, 1, 1024, lambda w: w.rearrange("(o p) n -> p o n", o=1))]))
    for s, nm in enumerate(["r", "k", "v", "g"]):
        for j in range(2):
            U.append(("C_%s%d" % (nm, j), [colslab("c_w_in", 1024 * s + 512 * j, 512)]))
    for j in range(2):
        U.append(("C_o%d" % j, [colslab("c_w_out", 512 * j, 512)]))
    for j in range(12):
        U.append(("D_in%d" % j, [colslab("d_w_in", 512 * j, 512)]))
    for j in range(4):
        U.append(("D_o%d" % j, [colslab("d_w_out", 256 * j, 256, nk=16)]))
    return U


WSHAPES = {"a_w_in": (1024, 3072), "a_w_out": (1024, 1024), "b_w_in": (1024, 2560), "b_w_rg": (10, 128, 128),
           "b_w_ig": (10, 128, 128), "b_w_out": (1280, 1024), "c_w_in": (1024, 4096), "c_w1": (1024, 64),
           "c_w2": (64, 1024), "c_a1": (1024, 64), "c_a2": (64, 1024), "c_w_out": (1024, 1024),
           "d_w_in": (1024, 6144), "d_w_out": (2048, 1024)}


def build(Tp, NL=4):
    plan, NT = make_plan(Tp)
    nc = bass.Bass("TRN2", target_bir_lowering=False)
    es = ExitStack()
    es.enter_context(nc.allow_low_precision("bf16 matmul operands, fp32 accumulation"))
    P = Prog(nc, es)

    def din(name, shape, dt=F32):
        return nc.dram_tensor(name, list(shape), dt, kind="ExternalInput").ap()

    def dout(name, shape):
        return nc.dram_tensor(name, list(shape), F32, kind="ExternalOutput").ap()

    xin = din("xin", (D, NT))
    cst128_d = din("cst128", (128, NC128))
    ktab_d = din("ktab", (128, KTW))
    rope_d = din("rope", (128, 6, NT))
    st_ca = din("st_ca", (3, D, 30))
    st_cb = din("st_cb", (3, DRNN, 3))
    st_lru = din("st_lru", (3, DRNN))
    st_sh = din("st_sh", (3, D))
    st_wkv = din("st_wkv", (3, 128, 8, 64))
    st_ret = din("st_ret", (3, 1024, 512))
    wd = {n: din(n, s) for n, s in WSHAPES.items()}
    yT = dout("yT", (D, NT))
    o_ca = dout("o_ca", (3, D, 30))
    o_cb = dout("o_cb", (3, DRNN, 3))
    o_lru = dout("o_lru", (3, DRNN))
    o_sh = dout("o_sh", (3, D))
    o_wkv = dout("o_wkv", (3, 128, 8, 64))
    o_ret = dout("o_ret", (3, 1024, 512))
    units = weight_units()
    NU = len(units)
    wscr = nc.dram_tensor("wscr", [NU, 128, SLOT], BF16, kind="Internal").ap()
    P.tracked_dram.add("wscr")

    def sb(name, shape, dt=F32):
        return es.enter_context(nc.sbuf_tensor("sb_" + name, list(shape), dt))

    psall = es.enter_context(nc.psum_tensor("psall", [128, 4096], F32))
    psall_bf = psall.bitcast(BF16)
    arena = sb("arena", [128, ARENA_BYTES // 4])
    arena_bf = arena.bitcast(BF16)
    xT = sb("xT", [128, 8, MAXB])
    cst = sb("cst", [128, NC128])
    ktab = sb("ktab_sb", [128, KTW])
    ktb = sb("ktb", [128, 640], BF16)
    cder = sb("cder", [128, 64])
    haloA = sb("haloA", [128, 8, 30])
    haloB = sb("haloB", [128, 10, 3])
    hstB = sb("hstB", [128, 10])
    shC = sb("shC", [128, 8])
    Mst = sb("Mst", [128, 8, 64])
    Mbf = sb("Mbf", [128, 8, 64], BF16)
    Sst = sb("Sst", [128, 8, 512])
    Sbf = sb("Sbf", [128, 8, 512], BF16)
    ring = [sb("ring%d" % i, [128, SLOT], BF16) for i in range(NSLOT)]

    NBANK = 7 if FILLER > 0 else 8
    bankctr = [0]

    def bank():
        i = bankctr[0] % NBANK
        bankctr[0] += 1
        return psall[:, 512 * i:512 * i + 512], psall_bf[:, 1024 * i:1024 * i + 1024]

    pairctr = [0]

    def bank2():
        i = (pairctr[0] % (NBANK // 2)) * 2
        pairctr[0] += 1
        return psall[:, 512 * i:512 * i + 1024], psall_bf[:, 1024 * i:1024 * i + 2048]

    aoff = [0]

    def areset():
        aoff[0] = 0

    def aal(shape, dt=F32):
        n = 1
        for s in shape[1:]:
            n *= s
        nb = n * _esz(dt)
        nb = (nb + 31) // 32 * 32
        o = aoff[0]
        aoff[0] += nb
        assert aoff[0] <= ARENA_BYTES, ("arena overflow", aoff[0])
        base = arena if dt == F32 else arena_bf
        e0 = o // _esz(dt)
        ap = base[:shape[0], e0:e0 + n]
        if len(shape) == 3:
            ap = ap.rearrange("p (a b) -> p a b", a=shape[1])
        return ap

    def c128(name, i0=0, n=None):
        o, w = C128[name]
        n = w - i0 if n is None else n
        return cst[:, o + i0:o + i0 + n]

    ident_bf = ktb[:, 0:128]
    onesD = ktb[:, 128:256]
    ones512 = ktb[:, 256:384]
    bd64 = ktb[:, 384:512]
    bd1 = ktb[:, 512:640]
    triS = ktab[:, 640:704]
    triI = ktab[:, 704:768]
    triL = ktab[:, 768:832]
    identf = ktab[:, 832:896]
    dmaskT = ktab[:64, 896:1152]
    kdec64 = ktab[:64, 1152:1156]
    kdec16 = ktab[:64, 1156:1160]
    onesf = ktab[:, 1160:1224]

    if FILLER > 0:
        dummy_ps = psall[:, 512 * 7:512 * 8]

        def _filler(e):
            for _ in range(FILLER):
                e.matmul(dummy_ps, ident_bf, ktb[:, 0:512], start=True, stop=True)

        P.pe_filler = _filler

    P.dma("pool", cst[:, :], cst128_d, "cst")
    P.dma("pool", ktab[:, :], ktab_d, "cst")
    P.copy(ktb[:, :], ktab[:, 0:640])
    P.act(cder[:, 0:10], c128("b_lam"), AF.Exp, scale=-1.0)
    xz = cder[:, 0:10]
    zz = cder[:, 32:42]
    z2 = cder[:, 42:52]
    P.ts(zz, xz, 2.0, None, ALU.add)
    P.call("dve", "reciprocal", zz, in_=zz)
    P.tt(zz, zz, xz, ALU.mult)
    P.tt(z2, zz, zz, ALU.mult)
    P.ts(xz, z2, 1.0 / 9.0, 1.0 / 7.0, ALU.mult, ALU.add)
    P.tt(xz, xz, z2, ALU.mult)
    P.ts(xz, xz, 1.0 / 5.0, None, ALU.add)
    P.tt(xz, xz, z2, ALU.mult)
    P.ts(xz, xz, 1.0 / 3.0, None, ALU.add)
    P.tt(xz, xz, z2, ALU.mult)
    P.ts(xz, xz, 1.0, None, ALU.add)
    P.tt(xz, xz, zz, ALU.mult)
    P.ts(cder[:, 0:10], xz, -16.0, None, ALU.mult)
    P.ts(cder[:, 16:24], c128("c_ka"), -1.0, 1.0, ALU.mult, ALU.add)

    areset()
    NSTG = 4
    stg32 = [aal([128, SLOT]) for _ in range(NSTG)]
    stgbf = [aal([128, SLOT], BF16) for _ in range(NSTG)]
    cast_engs = ["dve", "act", "dve", "act", "pool"]
    for u, (uname, pieces) in enumerate(units):
        par = u % NSTG
        off = 0
        for pi, (src, Pp, a, b, view) in enumerate(pieces):
            n = a * b
            dst32 = stg32[par][:Pp, off:off + n].rearrange("p (a b) -> p a b", a=a)
            P.dma("sp" if pi % 2 == 0 else "pool", dst32, view(wd[src]), "stg%d" % par)
            P.copy(stgbf[par][:Pp, off:off + n], stg32[par][:Pp, off:off + n], eng=cast_engs[(u + pi) % 5])
            P.dma("sp" if pi % 2 == 0 else "pool", wscr[u, :Pp, off:off + n], stgbf[par][:Pp, off:off + n], "stgo%d" % par)
            off += n
        assert off <= SLOT

    uidx = {n: i for i, (n, _) in enumerate(units)}
    layer_units = {L: [n for n, _ in units if n.startswith(L + "_")] for L in "ABCD"}
    stream = []
    for bi in range(len(plan)):
        for L in "ABCD"[:NL]:
            for n in layer_units[L]:
                stream.append(n)
    issued = [0]
    sidx = [0]

    def issue_upto(k):
        while issued[0] < min(k, len(stream)):
            i = issued[0]
            u = uidx[stream[i]]
            slot = ring[i % NSLOT]
            off = 0
            for pi, (src, Pp, a, b, view) in enumerate(units[u][1]):
                n = a * b
                P.dma("sp", slot[:Pp, off:off + n], wscr[u, :Pp, off:off + n], "ring%d" % (i % NSLOT))
                off += n
            issued[0] += 1

    def wget(name):
        i = sidx[0]
        assert stream[i] == name, (stream[i], name)
        issue_upto(i + NSLOT)
        sidx[0] += 1
        slot = ring[i % NSLOT]
        views = []
        off = 0
        for (src, Pp, a, b, view) in units[uidx[name]][1]:
            n = a * b
            views.append(slot[:Pp, off:off + n].rearrange("p (a b) -> p a b", a=a))
            off += n
        return views

    def rsqrt(out, in_, eps):
        P.act(out, in_, AF.Ln, bias=float(eps))
        P.act(out, out, AF.Exp, scale=-0.5)

    def rmsnorm(Tb, gname, gi0, dst, dstf=None):
        sq = aal([128, 8, MAXB], BF16)
        rstd = aal([128, MAXB])
        P.act(sq[:, :, :Tb], xT[:, :, :Tb], AF.Square)
        ps, _ = bank()
        P.mm([(ps[:, :Tb], [(onesD, sq[:, k, :Tb]) for k in range(8)])])
        rsqrt(rstd[:, :Tb], ps[:, :Tb], NORM_EPS)
        for k in range(8):
            g = c128(gname, gi0 + k, 1)
            if dst is not None:
                P.stt(dst[:, k, :Tb], xT[:, k, :Tb], g, rstd[:, :Tb], ALU.mult, ALU.mult)
            if dstf is not None:
                P.stt(dstf[:, k, :Tb], xT[:, k, :Tb], g, rstd[:, :Tb], ALU.mult, ALU.mult)

    def outproj(Tb, unit_names, ncols_per_unit, yK, nk):
        m = 0
        for un, ncu in zip(unit_names, ncols_per_unit):
            (w,) = wget(un)
            for j in range(ncu // 128):
                ps, _ = bank()
                P.mm([(ps[:, :Tb], [(w[:, k, 128 * j:128 * j + 128], yK(k)) for k in range(nk)])])
                P.tt(xT[:, m, :Tb], xT[:, m, :Tb], ps[:, :Tb], ALU.add)
                m += 1
        assert m == 8

    def layerA(blk):
        Tb = blk["Tb"]
        segs = blk["segs"]
        areset()
        hT = aal([128, 8, MAXB], BF16)
        WA = 30 * len(segs) + Tb
        glu = aal([128, 8, WA])
        gate = aal([128, 8, MAXB], BF16)
        sig = [aal([128, MAXB]) for _ in range(2)]
        acc = aal([128, 8, MAXB])
        rmsnorm(Tb, "norm_g", 0, hT)
        scol = [30 * (i + 1) + s["off"] for i, s in enumerate(segs)]
        for i, s in enumerate(segs):
            h = glu[:, :, scol[i] - 30:scol[i]]
            if s["first"]:
                P.dma("pool", h, st_ca[s["seq"]].rearrange("(k p) t -> p k t", p=128), "stin")
            else:
                P.copy(h, haloA[:, :, :], eng="pool")
        for j in range(4):
            wb, wa = wget("A_in%d" % j)
            for jj in range(2):
                m = 2 * j + jj
                ps, _ = bank()
                P.mm([(ps[:, :Tb], [(wb[:, k, 128 * jj:128 * jj + 128], hT[:, k, :Tb]) for k in range(8)])])
                sg = sig[m % 2]
                P.act(sg[:, :Tb], ps[:, :Tb], AF.Sigmoid)
                ps2, _ = bank()
                P.mm([(ps2[:, :Tb], [(wa[:, k, 128 * jj:128 * jj + 128], hT[:, k, :Tb]) for k in range(8)])])
                for i, s in enumerate(segs):
                    P.tt(glu[:, m, scol[i]:scol[i] + s["len"]], ps2[:, s["off"]:s["off"] + s["len"]],
                         sg[:, s["off"]:s["off"] + s["len"]], ALU.mult)
        for j in range(2):
            (wg,) = wget("A_g%d" % j)
            for jj in range(4):
                m = 4 * j + jj
                ps, _ = bank()
                P.mm([(ps[:, :Tb], [(wg[:, k, 128 * jj:128 * jj + 128], hT[:, k, :Tb]) for k in range(8)])])
                P.act(gate[:, m, :Tb], ps[:, :Tb], AF.Silu)
        o_w, _ = C128["a_wdw"]
        single = (len(segs) == 1)
        NDV = 19 if single else 31
        for j in range(NDV):
            for m in range(8):
                for i, s in enumerate(segs):
                    c0 = scol[i] - 30
                    L = s["len"]
                    dst = acc[:, m, s["off"]:s["off"] + L]
                    src = glu[:, m, c0 + j:c0 + j + L]
                    wj = cst[:, o_w + m * 31 + j:o_w + m * 31 + j + 1]
                    if j == 0:
                        P.ts(dst, src, wj, c128("a_bdw", m, 1), ALU.mult, ALU.add)
                    else:
                        P.stt(dst, src, wj, dst, ALU.mult, ALU.add)
        if single:
            NTMP = 6
            tmpb = [aal([128, MAXB], BF16) for _ in range(NTMP)]
            ti = 0
            c0 = scol[0] - 30
            for m in range(8):
                pc, _ = bank()
                for j in range(NDV, 31):
                    t_ = tmpb[ti % NTMP]
                    ti += 1
                    P.act(t_[:, :Tb], glu[:, m, c0 + j:c0 + j + Tb], AF.Copy, scale=cst[:, o_w + m * 31 + j:o_w + m * 31 + j + 1])
                    P.op("pe", (lambda e, o_=pc[:, :Tb], r_=t_[:, :Tb], j=j: e.matmul(o_, ident_bf, r_, start=(j == NDV), stop=(j == 30))),
                         [ident_bf, t_[:, :Tb]], [pc[:, :Tb]])
                P.tt(acc[:, m, :Tb], acc[:, m, :Tb], pc[:, :Tb], ALU.add)
        for i, s in enumerate(segs):
            last30 = glu[:, :, scol[i] + s["len"] - 30:scol[i] + s["len"]]
            if s["last"]:
                P.dma("pool", o_ca[s["seq"]].rearrange("(k p) t -> p k t", p=128), last30, "o_ca")
            else:
                P.copy(haloA[:, :, :], last30, eng="pool")
        cb = aal([128, 8, MAXB], BF16)
        csq = aal([128, 8, MAXB], BF16)
        P.copy(cb[:, :, :Tb], acc[:, :, :Tb], eng="act")
        P.act(csq[:, :, :Tb], acc[:, :, :Tb], AF.Square)
        psm, _ = bank()
        pss, _ = bank()
        P.mm([(psm[:, :Tb], [(onesD, cb[:, k, :Tb]) for k in range(8)])])
        P.mm([(pss[:, :Tb], [(onesD, csq[:, k, :Tb]) for k in range(8)])])
        mean = aal([128, MAXB])
        var = aal([128, MAXB])
        P.copy(mean[:, :Tb], psm[:, :Tb], eng="act")
        P.tt(var[:, :Tb], mean[:, :Tb], mean[:, :Tb], ALU.mult)
        P.tt(var[:, :Tb], pss[:, :Tb], var[:, :Tb], ALU.subtract)
        rsqrt(var[:, :Tb], var[:, :Tb], LN_EPS)
        P.stt(mean[:, :Tb], mean[:, :Tb], -1.0, var[:, :Tb], ALU.mult, ALU.mult)
        P.tt(acc[:, :, :Tb], acc[:, :, :Tb], bc(var[:, :Tb], 1, 8), ALU.mult)
        P.tt(acc[:, :, :Tb], acc[:, :, :Tb], bc(mean[:, :Tb], 1, 8), ALU.add)
        for m in range(8):
            P.act(acc[:, m, :Tb], acc[:, m, :Tb], AF.Silu, bias=c128("a_lnb", m, 1), scale=c128("a_lng", m, 1))
        yb = hT
        P.tt(yb[:, :, :Tb], acc[:, :, :Tb], gate[:, :, :Tb], ALU.mult)
        outproj(Tb, ["A_o0", "A_o1"], [512, 512], lambda k: yb[:, k, :Tb], 8)

    def layerB(blk):
        Tb = blk["Tb"]
        segs = blk["segs"]
        areset()
        hT = aal([128, 8, MAXB], BF16)
        WB = 3 * len(segs) + Tb
        xb = aal([128, 10, WB])
        gs = aal([128, 10, MAXB], BF16)
        xc = aal([128, 10, MAXB])
        xcb = aal([128, 10, MAXB], BF16)
        rmsnorm(Tb, "norm_g", 8, hT)
        scol = [3 * (i + 1) + s["off"] for i, s in enumerate(segs)]
        for i, s in enumerate(segs):
            h = xb[:, :, scol[i] - 3:scol[i]]
            if s["first"]:
                P.dma("pool", h, st_cb[s["seq"]].rearrange("(k p) t -> p k t", p=128), "stin")
            else:
                P.copy(h, haloB[:, :, :], eng="pool")
        for j in range(5):
            (w,) = wget("B_in%d" % j)
            for jj in range(4):
                m = 4 * j + jj
                ps, _ = bank()
                P.mm([(ps[:, :Tb], [(w[:, k, 128 * jj:128 * jj + 128], hT[:, k, :Tb]) for k in range(8)])])
                if m < 10:
                    for i, s in enumerate(segs):
                        P.copy(xb[:, m, scol[i]:scol[i] + s["len"]], ps[:, s["off"]:s["off"] + s["len"]], eng="act")
                else:
                    P.act(gs[:, m - 10, :Tb], ps[:, :Tb], AF.Silu)
        o_w, _ = C128["b_wc"]
        for j in range(4):
            for m in range(10):
                for i, s in enumerate(segs):
                    c0 = scol[i] - 3
                    L = s["len"]
                    dst = xc[:, m, s["off"]:s["off"] + L]
                    wj = cst[:, o_w + m * 4 + j:o_w + m * 4 + j + 1]
                    if j == 0:
                        P.ts(dst, xb[:, m, c0:c0 + L], wj, c128("b_bc", m, 1), ALU.mult, ALU.add)
                    else:
                        P.stt(dst, xb[:, m, c0 + j:c0 + j + L], wj, dst, ALU.mult, ALU.add)
        P.copy(xcb[:, :, :Tb], xc[:, :, :Tb], eng="act")
        for i, s in enumerate(segs):
            last3 = xb[:, :, scol[i] + s["len"] - 3:scol[i] + s["len"]]
            if s["last"]:
                P.dma("pool", o_cb[s["seq"]].rearrange("(k p) t -> p k t", p=128), last3, "o_cb", slow=True)
            else:
                P.copy(haloB[:, :, :], last3, eng="pool")
        R = aal([128, 10, MAXB])
        I = aal([128, 10, MAXB])
        A1 = aal([128, 10, MAXB])
        wrg, wig = wget("B_gt")
        for n in range(10):
            ps, _ = bank()
            P.mm([(ps[:, :Tb], [(wrg[:, n, :], xcb[:, n, :Tb])])])
            P.act(R[:, n, :Tb], ps[:, :Tb], AF.Sigmoid, bias=c128("b_brg", n, 1))
            ps2, _ = bank()
            P.mm([(ps2[:, :Tb], [(wig[:, n, :], xcb[:, n, :Tb])])])
            P.act(I[:, n, :Tb], ps2[:, :Tb], AF.Sigmoid, bias=c128("b_big", n, 1))
        for n in range(10):
            P.ts(R[:, n, :Tb], R[:, n, :Tb], cder[:, n:n + 1], None, ALU.mult)
        P.act(A1[:, :, :Tb], R[:, :, :Tb], AF.Exp)
        P.act(R[:, :, :Tb], R[:, :, :Tb], AF.Exp, scale=2.0)
        P.act(R[:, :, :Tb], R[:, :, :Tb], AF.Sqrt, scale=-1.0, bias=1.0)
        P.tt(I[:, :, :Tb], I[:, :, :Tb], R[:, :, :Tb], ALU.mult)
        P.tt(I[:, :, :Tb], I[:, :, :Tb], xc[:, :, :Tb], ALU.mult)
        for i, s in enumerate(segs):
            if s["first"]:
                P.dma("pool", hstB[:, :], st_lru[s["seq"]].rearrange("(k p) -> p k", p=128), "stin", slow=True)
            o, L = s["off"], s["len"]
            for n in range(10):
                P.call("dve", "tensor_tensor_scan", R[:, n, o:o + L], data0=A1[:, n, o:o + L], data1=I[:, n, o:o + L],
                       initial=hstB[:, n:n + 1], op0=ALU.mult, op1=ALU.add)
            P.copy(hstB[:, :], R[:, :, o + L - 1])
            if s["last"]:
                P.dma("pool", o_lru[s["seq"]].rearrange("(k p) -> p k", p=128), hstB[:, :], "o_lru", slow=True)
        yb = xcb
        P.tt(yb[:, :, :Tb], R[:, :, :Tb], gs[:, :, :Tb], ALU.mult)
        outproj(Tb, ["B_o0", "B_o1", "B_o2"], [384, 384, 256], lambda k: yb[:, k, :Tb], 10)

    def layerD(blk):
        Tb = blk["Tb"]
        segs = blk["segs"]
        tok0 = blk["tok0"]
        areset()
        hT = aal([128, 8, MAXB], BF16)
        qk = aal([128, 16, MAXB])
        vT = aal([128, 16, MAXB], BF16)
        gs = aal([128, 16, MAXB], BF16)
        rope = aal([128, 6, MAXB])
        rmsnorm(Tb, "norm_g", 24, hT)
        P.dma("pool", rope[:, :, :Tb], rope_d[:, :, tok0:tok0 + Tb], "rope")
        for j in range(12):
            (w,) = wget("D_in%d" % j)
            for jj in range(4):
                m = 4 * j + jj
                ps, _ = bank()
                P.mm([(ps[:, :Tb], [(w[:, k, 128 * jj:128 * jj + 128], hT[:, k, :Tb]) for k in range(8)])])
                if m < 16:
                    P.copy(qk[:, m, :Tb], ps[:, :Tb], eng="act")
                elif m < 32:
                    P.copy(vT[:, m - 16, :Tb], ps[:, :Tb], eng="act")
                else:
                    P.act(gs[:, m - 32, :Tb], ps[:, :Tb], AF.Silu)
        qr = aal([128, 16, MAXB], BF16)
        qd = aal([128, 8, MAXB], BF16)
        t1 = aal([128, 8, MAXB])
        t2 = aal([128, 8, MAXB])
        x1 = bass.AP(qk.tensor, qk.offset, [list(qk.ap[0]), [2 * MAXB, 8], [1, Tb]])
        x2 = bass.AP(qk.tensor, qk.offset + MAXB, [list(qk.ap[0]), [2 * MAXB, 8], [1, Tb]])
        o1 = bass.AP(qr.tensor, qr.offset, [list(qr.ap[0]), [2 * MAXB, 8], [1, Tb]])
        o2 = bass.AP(qr.tensor, qr.offset + MAXB, [list(qr.ap[0]), [2 * MAXB, 8], [1, Tb]])
        cosb = bc(rope[:, 0, :Tb], 1, 8)
        sinb = bc(rope[:, 1, :Tb], 1, 8)
        P.tt(t1[:, :, :Tb], x1, cosb, ALU.mult)
        P.tt(t2[:, :, :Tb], x2, sinb, ALU.mult)
        P.tt(o1, t1[:, :, :Tb], t2[:, :, :Tb], ALU.subtract)
        P.tt(t1[:, :, :Tb], x1, sinb, ALU.mult)
        P.tt(t2[:, :, :Tb], x2, cosb, ALU.mult)
        P.tt(o2, t1[:, :, :Tb], t2[:, :, :Tb], ALU.add)
        for h in range(4):
            P.tt(qd[:, 2 * h:2 * h + 2, :Tb], qr[:, 2 * h:2 * h + 2, :Tb], bc(rope[:, 2 + h, :Tb], 1, 2), ALU.mult)
        oT = aal([128, 16, MAXB])
        vtok = [aal([64, 2048], BF16) for _ in range(2)]
        kdt = [aal([64, 1024], BF16) for _ in range(2)]
        scm = [aal([64, 256], BF16) for _ in range(2)]
        dchunks = blk["chunks"]

        def d_pre(ci):
            c = dchunks[ci]
            o, L = c["off"], c["L"]
            vt = vtok[ci % 2]
            kt = kdt[ci % 2]
            sm = scm[ci % 2]
            _, pb = bank2()
            P.tr([(pb[:L, 128 * e:128 * e + 128], vT[:, e, o:o + L]) for e in range(16)], ident_bf)
            P.copy(vt[:L, :], pb[:L, 0:2048], eng="act")
            _, pk = bank()
            P.tr([(pk[:L, 128 * e:128 * e + 128], qr[:, 8 + e, o:o + L]) for e in range(8)], ident_bf)
            kdc = kdec64 if L == 64 else kdec16
            for h in range(4):
                P.ts(kt[:L, 256 * h:256 * h + 256], pk[:L, 256 * h:256 * h + 256], kdc[:L, h:h + 1], None, ALU.mult)
            ps, _ = bank()
            P.mm([(ps[:L, 64 * h:64 * h + L], [(qr[:, 8 + 2 * h + dc, o:o + L], qr[:, 2 * h + dc, o:o + L]) for dc in range(2)])
                  for h in range(4)])
            P.tt(sm[:L, :].rearrange("p (h n) -> p h n", h=4)[:, :, :L], ps[:L, 0:256].rearrange("p (h n) -> p h n", h=4)[:, :, :L],
                 dmaskT[:L, :].rearrange("p (h n) -> p h n", h=4)[:, :, :L], ALU.mult)

        def d_main(ci):
            c = dchunks[ci]
            o, L, sq_ = c["off"], c["L"], c["seq"]
            vt = vtok[ci % 2]
            kt = kdt[ci % 2]
            sm = scm[ci % 2]
            if c["first"]:
                P.dma("pool", Sst[:, :, :], st_ret[sq_].rearrange("(c p) e -> p c e", p=128), "stin")
                P.copy(Sbf[:, :, :], Sst[:, :, :], eng="act")
            po, _ = bank2()
            grp = []
            for h in range(4):
                for ec in range(4):
                    col = (4 * h + ec) * 64
                    pairs = [(vt[:L, 512 * h + 128 * ec:512 * h + 128 * ec + 128], sm[:L, 64 * h:64 * h + L])]
                    for dc in range(2):
                        pairs.append((Sbf[:, 2 * h + dc, 128 * ec:128 * ec + 128], qd[:, 2 * h + dc, o:o + L]))
                    grp.append((po[:, col:col + L], pairs))
            P.mm(grp)
            P.copy(oT[:, :, o:o + L], po[:, 0:1024].rearrange("p (a n) -> p a n", a=16)[:, :, :L], eng="act")
            for h in range(4):
                gL = float((1.0 - 2.0 ** (-5 - h)) ** L)
                for dc in range(2):
                    pS, _ = bank()
                    P.mm([(pS[:, :], [(kt[:L, 256 * h + 128 * dc:256 * h + 128 * dc + 128], vt[:L, 512 * h:512 * h + 512])])])
                    P.stt(Sst[:, 2 * h + dc, :], Sst[:, 2 * h + dc, :], gL, pS[:, :], ALU.mult, ALU.add)
            P.copy(Sbf[:, :, :], Sst[:, :, :], eng="act")
            if c["last"]:
                P.dma("pool", o_ret[sq_].rearrange("(c p) e -> p c e", p=128), Sst[:, :, :], "o_ret")

        d_pre(0)
        for ci in range(len(dchunks)):
            if ci + 1 < len(dchunks):
                d_pre(ci + 1)
            d_main(ci)
        osq = qr
        P.act(osq[:, :, :Tb], oT[:, :, :Tb], AF.Square)
        rs = t1
        for h in range(4):
            ps, _ = bank()
            P.mm([(ps[:, :Tb], [(ones512, osq[:, 4 * h + ec, :Tb]) for ec in range(4)])])
            rsqrt(rs[:, h, :Tb], ps[:, :Tb], NORM_EPS)
        for c_ in range(16):
            P.stt(oT[:, c_, :Tb], oT[:, c_, :Tb], c128("d_gng", c_, 1), rs[:, c_ // 4, :Tb], ALU.mult, ALU.mult)
        yb = vT
        P.tt(yb[:, :, :Tb], oT[:, :, :Tb], gs[:, :, :Tb], ALU.mult)
        outproj(Tb, ["D_o0", "D_o1", "D_o2", "D_o3"], [256] * 4, lambda k: yb[:, k, :Tb], 16)

    def layerC(blk):
        Tb = blk["Tb"]
        segs = blk["segs"]
        areset()
        hf = aal([128, 8, MAXB + 1])
        xx = aal([128, 8, MAXB])
        xm = [aal([128, 8, MAXB], BF16) for _ in range(2)]
        lw = aal([64, MAXB], BF16)
        la = aal([64, MAXB], BF16)
        rmsnorm(Tb, "norm_g", 16, None, dstf=hf[:, :, 1:])
        prefix_end = aoff[0]
        s0 = segs[0]
        if s0["first"]:
            P.dma("pool", hf[:, :, 0], st_sh[s0["seq"]].rearrange("(k p) -> p k", p=128), "stin", slow=True)
        else:
            P.copy(hf[:, :, 0], shC[:, :], eng="pool")
        P.tt(xx[:, :, :Tb], hf[:, :, 0:Tb], hf[:, :, 1:Tb + 1], ALU.subtract)
        for i, s in enumerate(segs):
            if i > 0:
                P.dma("pool", shC[:, :], st_sh[s["seq"]].rearrange("(k p) -> p k", p=128), "stin", slow=True)
                P.tt(xx[:, :, s["off"]], shC[:, :], hf[:, :, 1 + s["off"]], ALU.subtract)
            if s["last"]:
                P.dma("pool", o_sh[s["seq"]].rearrange("(k p) -> p k", p=128), hf[:, :, s["off"] + s["len"]], "o_sh", slow=True)
        if not segs[-1]["last"]:
            P.copy(shC[:, :], hf[:, :, Tb], eng="pool")

        def drain(names):
            for n_ in names:
                wget(n_)

        C_ALL = ["C_lora"] + ["C_%s%d" % (nm, j) for nm in "rkvg" for j in range(2)] + ["C_o0", "C_o1"]
        if CSTOP == 0:
            drain(C_ALL)
            return
        o_mu, _ = C128["c_mu"]

        def mkxm(si):
            t = xm[si % 2]
            for k in range(8):
                P.stt(t[:, k, :Tb], xx[:, k, :Tb], cst[:, o_mu + si * 8 + k:o_mu + si * 8 + k + 1], hf[:, k, 1:Tb + 1], ALU.mult, ALU.add)
            return t

        rr = aal([128, 8, MAXB])
        vb = aal([128, 8, MAXB], BF16)
        gsl = aal([128, 8, MAXB], BF16)
        bon = aal([128, 8, MAXB], BF16)
        bh = aal([128, 8, MAXB], BF16)
        kh_ = aal([128, 8, MAXB], BF16)
        at = aal([128, 8, MAXB], BF16)
        rt = aal([128, 8, MAXB], BF16)
        pl = aal([128, 8, 8])
        dead_start = aoff[0]
        kk_ = aal([128, 8, MAXB])
        lwd = aal([128, 8, MAXB])
        av = aal([128, 8, MAXB])
        tA = aal([128, 8, MAXB])
        tB = aal([128, 8, MAXB])
        ex = aal([128, 8, MAXB])
        tbf = aal([128, 8, MAXB], BF16)
        dead_end = aoff[0]
        w1, a1, w2, a2 = wget("C_lora")
        x4 = mkxm(4)
        ps, _ = bank()
        P.mm([(ps[:64, :Tb], [(w1[:, k, :], x4[:, k, :Tb]) for k in range(8)])])
        P.act(lw[:, :Tb], ps[:64, :Tb], AF.Tanh)
        x5 = mkxm(5)
        ps, _ = bank()
        P.mm([(ps[:64, :Tb], [(a1[:, k, :], x5[:, k, :Tb]) for k in range(8)])])
        P.copy(la[:, :Tb], ps[:64, :Tb], eng="act")
        for p in range(8):
            ps, _ = bank()
            P.mm([(ps[:, :Tb], [(w2[:, 0, 128 * p:128 * p + 128], lw[:, :Tb])])])
            P.act(lwd[:, p, :Tb], ps[:, :Tb], AF.Sigmoid, bias=c128("c_w0", p, 1))
            ps2, _ = bank()
            P.mm([(ps2[:, :Tb], [(a2[:, 0, 128 * p:128 * p + 128], la[:, :Tb])])])
            P.act(av[:, p, :Tb], ps2[:, :Tb], AF.Sigmoid, bias=c128("c_a0", p, 1))
        P.ts(lwd[:, :, :Tb], lwd[:, :, :Tb], -float(np.exp(-0.5)), None, ALU.mult)
        if CSTOP == 1:
            drain(C_ALL[1:])
            return

        def proj(si, nm, evac):
            x = mkxm(si)
            for j in range(2):
                (w,) = wget("C_%s%d" % (nm, j))
                for jj in range(4):
                    p = 4 * j + jj
                    ps, _ = bank()
                    P.mm([(ps[:, :Tb], [(w[:, k, 128 * jj:128 * jj + 128], x[:, k, :Tb]) for k in range(8)])])
                    evac(p, ps)

        proj(0, "r", lambda p, ps: P.copy(rr[:, p, :Tb], ps[:, :Tb], eng="act"))
        proj(1, "k", lambda p, ps: P.copy(kk_[:, p, :Tb], ps[:, :Tb], eng="act"))
        proj(2, "v", lambda p, ps: P.copy(vb[:, p, :Tb], ps[:, :Tb], eng="act"))
        proj(3, "g", lambda p, ps: P.act(gsl[:, p, :Tb], ps[:, :Tb], AF.Silu))

        if CSTOP == 2:
            drain(C_ALL[9:])
            return
        for p in range(8):
            P.ts(tA[:, p, :Tb], kk_[:, p, :Tb], c128("c_kk", p, 1), None, ALU.mult)
        P.act(tbf[:, :, :Tb], tA[:, :, :Tb], AF.Square)
        for p in range(8):
            ps, _ = bank()
            P.mm([(ps[:, :Tb], [(bd1, tbf[:, p, :Tb])])])
            P.ts(tB[:, p, :Tb], ps[:, :Tb], 1e-24, None, ALU.max)
            P.act(tB[:, p, :Tb], tB[:, p, :Tb], AF.Ln)
            P.act(tB[:, p, :Tb], tB[:, p, :Tb], AF.Exp, scale=-0.5)
        P.tt(tA[:, :, :Tb], tA[:, :, :Tb], tB[:, :, :Tb], ALU.mult)
        for p in range(8):
            P.ts(tB[:, p, :Tb], av[:, p, :Tb], c128("c_ka", p, 1), cder[:, 16 + p:17 + p], ALU.mult, ALU.add)
        P.tt(kk_[:, :, :Tb], kk_[:, :, :Tb], tB[:, :, :Tb], ALU.mult)
        P.tt(tB[:, :, :Tb], rr[:, :, :Tb], kk_[:, :, :Tb], ALU.mult)
        for p in range(8):
            P.ts(tbf[:, p, :Tb], tB[:, p, :Tb], c128("c_rk", p, 1), None, ALU.mult)
        for p in range(8):
            ps, _ = bank()
            P.mm([(ps[:, :Tb], [(bd1, tbf[:, p, :Tb])])])
            P.tt(bon[:, p, :Tb], ps[:, :Tb], vb[:, p, :Tb], ALU.mult)
        for c in blk["chunks"]:
            o, L = c["off"], c["L"]
            for p in range(8):
                P.call("dve", "tensor_tensor_scan", tB[:, p, o:o + L], data0=onesf[:, :L], data1=lwd[:, p, o:o + L],
                       initial=0.0, op0=ALU.mult, op1=ALU.add)
        P.act(ex[:, :, :Tb], tB[:, :, :Tb], AF.Exp, scale=-1.0)
        P.tt(av[:, :, :Tb], av[:, :, :Tb], tA[:, :, :Tb], ALU.mult)
        P.tt(bh[:, :, :Tb], av[:, :, :Tb], ex[:, :, :Tb], ALU.mult)
        P.tt(kh_[:, :, :Tb], kk_[:, :, :Tb], ex[:, :, :Tb], ALU.mult)
        P.act(ex[:, :, :Tb], tB[:, :, :Tb], AF.Exp)
        P.tt(rt[:, :, :Tb], rr[:, :, :Tb], ex[:, :, :Tb], ALU.mult)
        for ci, c in enumerate(blk["chunks"]):
            P.copy(pl[:, :, ci], ex[:, :, c["off"] + c["L"] - 1], eng="pool")
        P.tt(tB[:, :, :Tb], tB[:, :, :Tb], lwd[:, :, :Tb], ALU.subtract)
        P.act(ex[:, :, :Tb], tB[:, :, :Tb], AF.Exp)
        P.stt(at[:, :, :Tb], tA[:, :, :Tb], -1.0, ex[:, :, :Tb], ALU.mult, ALU.mult)

        if CSTOP == 3:
            drain(C_ALL[9:])
            return
        end_all = aoff[0]
        oTc = rr
        chunks = blk["chunks"]
        aoff[0] = dead_start
        FtL = [[aal([128, 8, 64], BF16) for _ in range(6)] for _ in range(2)]
        assert aoff[0] <= dead_end, (aoff[0], dead_end)
        aoff[0] = 0
        tkB = [aal([128, 8, 64], BF16) for _ in range(2)]
        tkK = [aal([128, 8, 64], BF16) for _ in range(2)]
        tkV = [aal([128, 8, 64], BF16) for _ in range(2)]
        sArb = [aal([128, 8, 64], BF16) for _ in range(2)]
        sArk = [aal([128, 8, 64], BF16) for _ in range(2)]
        sAak = [aal([128, 8, 64], BF16) for _ in range(2)]
        sAab = aal([128, 8, 64], BF16)
        Nn = [aal([128, 8, 64], BF16) for _ in range(2)]
        Nt = [aal([128, 8, 64], BF16) for _ in range(2)]
        Xb = [aal([128, 8, 64], BF16) for _ in range(2)]
        assert aoff[0] <= prefix_end, (aoff[0], prefix_end)
        aoff[0] = end_all

        def HS(t3, p, h2, c0, c1):
            return t3[64 * h2:64 * h2 + 64, p, c0:c1]

        def headmm(rows, ncols, pairs_fn):
            banks = [bank()[0], bank()[0]]
            groups = []
            for h2 in range(2):
                for p in range(8):
                    groups.append((banks[h2][64 * h2:64 * h2 + rows, 64 * p:64 * p + ncols], pairs_fn(p, h2)))
            P.mm(groups)
            return banks

        def pview(banks, h2, rows, ncols):
            return banks[h2][64 * h2:64 * h2 + rows, 0:512].rearrange("p (a n) -> p a n", a=8)[:, :, :ncols]

        def RLf(t3, p, h2, L, c1=None):
            return t3[64 * h2:64 * h2 + L, p, :c1] if c1 is not None else t3[64 * h2:64 * h2 + L, p, :]

        def pre_steps(ci):
            c = chunks[ci]
            o, L = c["off"], c["L"]
            q = ci % 2
            nlev = 6 if L == 64 else 4
            steps = []

            def tstep(src, dstv):
                def f():
                    bk = [bank()[1], bank()[1]]
                    items = []
                    for h2 in range(2):
                        for p in range(8):
                            items.append((bk[h2][64 * h2:64 * h2 + L, 64 * p:64 * p + 64], src[64 * h2:64 * h2 + 64, p, o:o + L],
                                          ident_bf[64 * h2:64 * h2 + 64, 64 * h2:64 * h2 + 64]))
                    P.tr3(items)
                    for h2 in range(2):
                        P.copy(dstv[64 * h2:64 * h2 + L, :, :], bk[h2][64 * h2:64 * h2 + L, 0:512].rearrange("p (a n) -> p a n", a=8), eng="act")
                return f

            for src, dstv in ((bh, tkB[q]), (kh_, tkK[q]), (vb, tkV[q])):
                steps.append(tstep(src, dstv))

            def sstep(lhs, rhs, dst, msk):
                def f():
                    bks = headmm(L, L, lambda p, h2: [(HS(lhs, p, h2, o, o + L), HS(rhs, p, h2, o, o + L))])
                    for h2 in range(2):
                        P.tt(dst[64 * h2:64 * h2 + L, :, :L], pview(bks, h2, L, L), bc(msk[64 * h2:64 * h2 + L, :L], 1, 8), ALU.mult)
                return f

            for lhs, rhs, dst, msk in ((bh, at, sAab, triS), (at, bh, Nn[0], triL), (kh_, at, sAak[q], triS),
                                       (bh, rt, sArb[q], triI), (kh_, rt, sArk[q], triI)):
                steps.append(sstep(lhs, rhs, dst, msk))

            def f0():
                for h2 in range(2):
                    R_ = slice(64 * h2, 64 * h2 + L)
                    P.copy(Nt[0][R_, :, :L], sAab[R_, :, :L], eng="pool")
                    P.tt(FtL[q][0][R_, :, :L], sAab[R_, :, :L], bc(identf[R_, :L], 1, 8), ALU.add, eng="pool")
            steps.insert(5, f0)

            def qstep(lv):
                def f():
                    a_, b_ = lv % 2, (lv + 1) % 2
                    b1 = headmm(L, L, lambda p, h2: [(RLf(Nt[a_], p, h2, L, L), RLf(Nn[a_], p, h2, L, L))])
                    b2 = headmm(L, L, lambda p, h2: [(RLf(Nn[a_], p, h2, L, L), RLf(Nt[a_], p, h2, L, L))])
                    for h2 in range(2):
                        R_ = slice(64 * h2, 64 * h2 + L)
                        if lv < nlev - 2:
                            P.copy(Nn[b_][R_, :, :L], pview(b1, h2, L, L))
                            P.copy(Nt[b_][R_, :, :L], pview(b2, h2, L, L), eng="act")
                        P.tt(FtL[q][lv + 1][R_, :, :L], pview(b2, h2, L, L), bc(identf[R_, :L], 1, 8), ALU.add)
                return f

            for lv in range(nlev - 1):
                steps.append(qstep(lv))
            return steps

        def main_steps(ci):
            c = chunks[ci]
            o, L, sq_ = c["off"], c["L"], c["seq"]
            q = ci % 2
            nlev = 6 if L == 64 else 4
            steps = []

            def x0():
                if c["first"]:
                    P.dma("pool", Mst[:, :, :], st_wkv[sq_], "stin")
                    P.copy(Mbf[:, :, :], Mst[:, :, :], eng="act")
                bks = headmm(L, 64, lambda p, h2: [(HS(at, p, h2, o, o + L), Mbf[64 * h2:64 * h2 + 64, p, :]),
                                                   (RLf(sAak[q], p, h2, L, L), RLf(tkV[q], p, h2, L))])
                for h2 in range(2):
                    P.copy(Xb[0][64 * h2:64 * h2 + L, :, :], pview(bks, h2, L, 64), eng="act")
            steps.append(x0)

            def xstep(lv):
                def f():
                    a_, b_ = lv % 2, (lv + 1) % 2
                    bks = headmm(L, 64, lambda p, h2: [(RLf(FtL[q][lv], p, h2, L, L), RLf(Xb[a_], p, h2, L))])
                    for h2 in range(2):
                        P.copy(Xb[b_][64 * h2:64 * h2 + L, :, :], pview(bks, h2, L, 64), eng="act")
                return f

            for lv in range(nlev):
                steps.append(xstep(lv))
            Ub = Xb[nlev % 2]

            def ostep():
                bks = headmm(64, L, lambda p, h2: [(Mbf[64 * h2:64 * h2 + 64, p, :], HS(rt, p, h2, o, o + L)),
                                                   (RLf(Ub, p, h2, L), RLf(sArb[q], p, h2, L, L)),
                                                   (RLf(tkV[q], p, h2, L), RLf(sArk[q], p, h2, L, L))])
                for h2 in range(2):
                    P.copy(oTc[64 * h2:64 * h2 + 64, :, o:o + L], pview(bks, h2, 64, L), eng="act")
            steps.append(ostep)

            def ststep():
                bks = headmm(64, 64, lambda p, h2: [(RLf(tkB[q], p, h2, L), RLf(Ub, p, h2, L)), (RLf(tkK[q], p, h2, L), RLf(tkV[q], p, h2, L))])
                for h2 in range(2):
                    R_ = slice(64 * h2, 64 * h2 + 64)
                    P.tt(Mst[R_, :, :], Mst[R_, :, :], pview(bks, h2, 64, 64), ALU.add)
                for p in range(8):
                    P.ts(Mst[:, p, :], Mst[:, p, :], pl[:, p, ci:ci + 1], None, ALU.mult)
                P.copy(Mbf[:, :, :], Mst[:, :, :], eng="act")
                if c["last"]:
                    P.dma("pool", o_wkv[sq_], Mst[:, :, :], "o_wkv")
            steps.append(ststep)
            return steps

        for f in pre_steps(0):
            f()
        for ci in range(len(chunks)):
            ms = main_steps(ci)
            pr = pre_steps(ci + 1) if ci + 1 < len(chunks) else []
            for i in range(max(len(ms), len(pr))):
                if i < len(pr):
                    pr[i]()
                if i < len(ms):
                    ms[i]()

        if CSTOP == 4 or 30 < CSTOP < 40:
            drain(C_ALL[9:])
            return
        ob = tbf
        osq = bh
        P.copy(ob[:, :, :Tb], oTc[:, :, :Tb], eng="act")
        P.act(osq[:, :, :Tb], oTc[:, :, :Tb], AF.Square)
        yC = at
        for p in range(8):
            pm, _ = bank()
            pq, _ = bank()
            P.mm([(pm[:, :Tb], [(bd64, ob[:, p, :Tb])])])
            P.mm([(pq[:, :Tb], [(bd64, osq[:, p, :Tb])])])
            mu = tA[:, p, :Tb]
            vr = tB[:, p, :Tb]
            P.copy(mu, pm[:, :Tb], eng="act")
            P.tt(vr, mu, mu, ALU.mult)
            P.tt(vr, pq[:, :Tb], vr, ALU.subtract)
            rsqrt(vr, vr, GN_EPS)
            P.tt(oTc[:, p, :Tb], oTc[:, p, :Tb], mu, ALU.subtract)
            P.tt(oTc[:, p, :Tb], oTc[:, p, :Tb], vr, ALU.mult)
            P.ts(oTc[:, p, :Tb], oTc[:, p, :Tb], c128("c_gng", p, 1), c128("c_gnb", p, 1), ALU.mult, ALU.add)
            P.tt(oTc[:, p, :Tb], oTc[:, p, :Tb], bon[:, p, :Tb], ALU.add)
            P.tt(yC[:, p, :Tb], oTc[:, p, :Tb], gsl[:, p, :Tb], ALU.mult)
        outproj(Tb, ["C_o0", "C_o1"], [512, 512], lambda k: yC[:, k, :Tb], 8)

    layers = [layerA, layerB, layerC, layerD][:NL]
    for blk in plan:
        Tb = blk["Tb"]
        P.dma("pool", xT[:, :, :Tb], xin.rearrange("(k p) t -> p k t", p=128)[:, :, blk["tok0"]:blk["tok0"] + Tb], "xin")
        for lf in layers:
            lf(blk)
        aoff[0] = ARENA_BYTES - (8 * MAXB * 4 + 8 * MAXB * 2 + MAXB * 4 + 256)
        yo = aal([128, 8, MAXB])
        rmsnorm(Tb, "fin_g", 0, None, dstf=yo)
        P.dma("pool", yT.rearrange("(k p) t -> p k t", p=128)[:, :, blk["tok0"]:blk["tok0"] + Tb], yo[:, :, :Tb], "yout")
    P.emit()
    return nc, P


def host_tables(plan, NT, Tp):
    kt = np.zeros((128, KTW), np.float32)
    kt[:, 0:128] = np.eye(128, dtype=np.float32)
    kt[:, 128:256] = 1.0 / 1024
    kt[:, 256:384] = 1.0 / 512
    bd = np.zeros((128, 128), np.float32)
    bd[:64, :64] = 1.0
    bd[64:, 64:] = 1.0
    kt[:, 384:512] = bd / 64.0
    kt[:, 512:640] = bd
    s = np.arange(64)[:, None]
    t = np.arange(64)[None, :]
    kt[:64, 640:704] = (t > s)
    kt[:64, 704:768] = (t >= s)
    kt[:64, 768:832] = (s > t)
    kt[:64, 832:896] = np.eye(64)
    gam = (1.0 - 2.0 ** (-5.0 - np.arange(4))).astype(np.float64)
    for h in range(4):
        dm = np.where(t >= s, gam[h] ** np.maximum(t - s, 0), 0.0) / 16.0
        kt[:64, 896 + 64 * h:896 + 64 * h + 64] = dm
        m = np.arange(64)
        kt[:64, 1152 + h] = gam[h] ** (63.0 - m)
        kt[:64, 1156 + h] = gam[h] ** np.maximum(15.0 - m, 0.0)
    kt[64:128, 640:896] = kt[0:64, 640:896]
    kt[:, 1160:1224] = 1.0
    return kt


def _prep(inputs):
    x_prompt = np.asarray(inputs["x_prompt"], np.float32)
    x_sample = np.asarray(inputs["x_sample"], np.float32)
    B, S, _ = x_prompt.shape
    Tp = S + NMETA
    plan, NT = make_plan(Tp)
    meta = np.asarray(inputs["meta_tokens"], np.float32)

    def pk(v, p):
        v = np.asarray(v, np.float32).reshape(-1, p)
        return np.ascontiguousarray(v.T)

    c128 = np.zeros((128, NC128), np.float32)

    def put(name, arr):
        o, w = C128[name]
        assert arr.shape == (128, w), (name, arr.shape)
        c128[:, o:o + w] = arr

    put("norm_g", np.concatenate([pk(inputs["norm_g"][i], 128) for i in range(4)], axis=1))
    put("fin_g", pk(inputs["final_norm_g"], 128))
    wdw = np.asarray(inputs["a_w_dw"], np.float32)
    put("a_wdw", np.ascontiguousarray(wdw.reshape(31, 8, 128).transpose(2, 1, 0)).reshape(128, 248))
    put("a_bdw", pk(inputs["a_b_dw"], 128))
    put("a_lng", pk(inputs["a_ln_g"], 128))
    put("a_lnb", pk(inputs["a_ln_b"], 128))
    wc = np.asarray(inputs["b_w_conv"], np.float32)
    put("b_wc", np.ascontiguousarray(wc.reshape(4, 10, 128).transpose(2, 1, 0)).reshape(128, 40))
    put("b_bc", pk(inputs["b_b_conv"], 128))
    put("b_brg", pk(inputs["b_b_rg"], 128))
    put("b_big", pk(inputs["b_b_ig"], 128))
    put("b_lam", pk(inputs["b_lam"], 128))
    mu = np.asarray(inputs["c_mu"], np.float32)
    put("c_mu", np.ascontiguousarray(mu.reshape(6, 8, 128).transpose(2, 0, 1)).reshape(128, 48))
    put("d_gng", pk(inputs["d_gn_g"], 128))
    for name, key in [("c_w0", "c_w0"), ("c_a0", "c_a0"), ("c_kk", "c_k_k"), ("c_ka", "c_k_a"), ("c_rk", "c_r_k"),
                      ("c_gng", "c_gn_g"), ("c_gnb", "c_gn_b")]:
        put(name, pk(np.asarray(inputs[key]).reshape(-1), 128))
    c64 = None

    kt = host_tables(plan, NT, Tp)
    pos = np.zeros(NT, np.float32)
    nin = np.zeros(NT, np.float64)
    for blk in plan:
        for c in blk["chunks"]:
            g0 = c["gtok"]
            L = c["L"]
            if c["seq"] == 0:
                p0 = g0
            else:
                p0 = NMETA + PAST
            pos[g0:g0 + L] = p0 + np.arange(L)
            nin[g0:g0 + L] = np.arange(L)
    half = 128
    inv = (10000.0 ** (-np.arange(half, dtype=np.float32) / half)).astype(np.float32)
    ang = pos[None, :].astype(np.float32) * inv[:, None]
    rope = np.zeros((128, 6, NT), np.float32)
    rope[:, 0, :] = np.cos(ang)
    rope[:, 1, :] = np.sin(ang)
    gam = (1.0 - 2.0 ** (-5.0 - np.arange(4))).astype(np.float64)
    for h in range(4):
        rope[:, 2 + h, :] = (gam[h] ** (nin + 1.0) / 16.0)[None, :]
    return plan, NT, Tp, c128, c64, kt, rope, meta, x_prompt, x_sample


_CACHE = {}
_SIM = None
CSTOP = 99
FILLER = 0


def kernel(**inputs):
    plan, NT, Tp, c128, c64, kt, rope, meta, x_prompt, x_sample = _prep(inputs)
    NL = int(inputs.pop("_NL", 4)) if "_NL" in inputs else 4
    key = (Tp, NL)
    if key not in _CACHE:
        _CACHE[key] = build(Tp, NL)
    nc, _ = _CACHE[key]
    B = x_prompt.shape[0]
    f32 = lambda a: np.ascontiguousarray(np.asarray(a, np.float32))
    ktf = kt.copy()
    in_maps = []
    for core in range(8):
        b = core % B
        sidx = [2 * (core % 4), 2 * (core % 4) + 1]
        xs = np.concatenate([meta, x_prompt[b], x_sample[sidx[0]], x_sample[sidx[1]]], axis=0)
        m = {"xin": f32(xs.T), "cst128": c128, "ktab": ktf, "rope": rope}

        def st3(samp, zshape, tr):
            z = np.zeros(zshape, np.float32)
            return f32(np.stack([z, tr(samp[sidx[0]]), tr(samp[sidx[1]])], axis=0))

        m["st_ca"] = st3(np.asarray(inputs["cache_conv_a"], np.float32), (D, 30), lambda a: a.T)
        m["st_cb"] = st3(np.asarray(inputs["cache_conv_b"], np.float32), (DRNN, 3), lambda a: a.T)
        m["st_lru"] = st3(np.asarray(inputs["state_lru_b"], np.float32), (DRNN,), lambda a: a)
        m["st_sh"] = st3(np.asarray(inputs["state_shift_c"], np.float32), (D,), lambda a: a)
        m["st_wkv"] = st3(np.asarray(inputs["state_wkv_c"], np.float32), (128, 8, 64),
                           lambda a: a.reshape(8, 2, 64, 64).transpose(1, 3, 0, 2).reshape(128, 8, 64))
        m["st_ret"] = st3(np.asarray(inputs["state_ret_d"], np.float32), (1024, 512), lambda a: a.reshape(1024, 512))
        for n in WSHAPES:
            m[n] = f32(inputs[n])
        in_maps.append(m)
    if "_SIM" in globals() and _SIM is not None:
        res = _SIM(nc, in_maps)
    else:
        res = run_bass_kernel_spmd(nc, in_maps, core_ids=list(range(8)))
    R = res.results
    S = x_prompt.shape[1]
    y_prompt = np.stack([R[b]["yT"][:, NMETA:Tp].T for b in range(B)], axis=0)
    y_sample = np.stack([R[i // 2]["yT"][:, Tp + 64 * (i % 2):Tp + 64 * (i % 2) + 64].T for i in range(8)], axis=0)

    def outs(name, tr):
        p = np.stack([tr(R[b][name][0]) for b in range(B)], axis=0)
        s = np.stack([tr(R[i // 2][name][1 + i % 2]) for i in range(8)], axis=0)
        return np.ascontiguousarray(p, np.float32), np.ascontiguousarray(s, np.float32)

    ca_p, ca_s = outs("o_ca", lambda a: a.T)
    cb_p, cb_s = outs("o_cb", lambda a: a.T)
    lr_p, lr_s = outs("o_lru", lambda a: a)
    sh_p, sh_s = outs("o_sh", lambda a: a)
    wk_p, wk_s = outs("o_wkv", lambda a: a.reshape(2, 64, 8, 64).transpose(2, 0, 3, 1).reshape(16, 64, 64))
    rt_p, rt_s = outs("o_ret", lambda a: a.reshape(4, 256, 512))
    return (np.ascontiguousarray(y_prompt, np.float32), np.ascontiguousarray(y_sample, np.float32),
            ca_p, ca_s, cb_p, cb_s, lr_p, lr_s, sh_p, sh_s, wk_p, wk_s, rt_p, rt_s)
```
